# Optimizing a Trainium2 kernel written in Bass

```python
import jax, jax.numpy as jnp
from jax import lax
import numpy as np

D_MODEL = 1024
BATCH = 16
SEQ = 4096
DEPTH = 4

N_META = 16
NORM_EPS = 1e-6
N_BRANCHES = 4
MASK_VALUE = -1e30

A_HEADS = 4
A_HEAD_DIM = 64
A_WIDTH = A_HEADS * A_HEAD_DIM
A_DECAY_LORA = 64
A_ICL_LORA = 64
A_VRES_LORA = 32
A_GATE_LORA = 128
A_GN_EPS = 64e-5
A_COLS = 3 * A_WIDTH + A_DECAY_LORA + A_ICL_LORA + A_GATE_LORA

B_GROUPS = 4
B_GROUP_DIM = 64
B_WIDTH = B_GROUPS * B_GROUP_DIM
B_WINDOWS = (2, 4, 8, 16)

C_HEADS = 8
C_KV_HEADS = 2
C_GROUP = C_HEADS // C_KV_HEADS
C_HEAD_DIM = 64
C_WIDTH = C_HEADS * C_HEAD_DIM
C_KV_WIDTH = C_KV_HEADS * C_HEAD_DIM
C_COLS = C_WIDTH + 2 * C_KV_WIDTH
WINDOW = 128
C_BLOCK = 128

D_HEADS = 4
D_KEY_DIM = 64
D_VAL_DIM = 64
D_WIDTH = D_HEADS * D_KEY_DIM
D_OUT = D_HEADS * D_VAL_DIM
D_COLS = 2 * D_WIDTH + 2 * D_OUT
D_CHUNK = 64

D_FF = ((8 * D_MODEL + 3 * 256 - 1) // (3 * 256)) * 256

GATE_COLS = N_BRANCHES * D_MODEL
OFF_A = GATE_COLS
OFF_B = OFF_A + A_COLS
OFF_C = OFF_B + B_WIDTH
OFF_D = OFF_C + C_COLS
IN_COLS = OFF_D + D_COLS

ROW_B = A_WIDTH
ROW_C = ROW_B + B_WIDTH
ROW_D = ROW_C + C_WIDTH
MIX_WIDTH = ROW_D + D_OUT

kernel_name = "hybrid_rwkv7_pool_swa_hgrn2_gated"

F32 = jnp.float32


def rms_norm(x, g):
    xf = x.astype(F32)
    y = xf * lax.rsqrt(jnp.mean(xf * xf, axis=-1, keepdims=True) + NORM_EPS)
    return (y * g.astype(F32)).astype(x.dtype)


def token_shift(u):
    return jnp.pad(u, ((0, 0), (1, 0), (0, 0)))[:, :-1]


def split_heads(t, n):
    return t.reshape(t.shape[:-1] + (-1, n))


def alibi_slopes(n):
    return jnp.asarray([2.0 ** (-8.0 * (i + 1) / n) for i in range(n)], dtype=F32)


def rwkv7_branch(u, mu, w_up, w0, a_up, a0, g_up, k_k, k_a, r_k, ln_w, ln_b,
                 v_first, vres_down, vres_up, vres0):
    dt = u.dtype
    bsz, L, _ = u.shape
    u = u + (token_shift(u) - u) * mu
    r = u[..., 0:A_WIDTH]
    k = u[..., A_WIDTH:2 * A_WIDTH]
    v = u[..., 2 * A_WIDTH:3 * A_WIDTH]
    o1 = 3 * A_WIDTH
    o2 = o1 + A_DECAY_LORA
    o3 = o2 + A_ICL_LORA
    wd = u[..., o1:o2]
    ad = u[..., o2:o3]
    gd = u[..., o3:o3 + A_GATE_LORA]
    if vres_down is not None:
        v = v + (v_first - v) * jax.nn.sigmoid(vres0 + (v @ vres_down) @ vres_up)
    w_log = -jax.nn.softplus(-(w0 + jnp.tanh(wd) @ w_up)) - 0.5
    decay = jnp.exp(-jnp.exp(w_log.astype(F32)))
    a = jax.nn.sigmoid(a0 + ad @ a_up)
    g = jax.nn.sigmoid(gd) @ g_up
    kk = split_heads((k * k_k).astype(F32), A_HEAD_DIM)
    kk = kk / jnp.maximum(jnp.sqrt(jnp.sum(kk * kk, axis=-1, keepdims=True)), 1e-12)
    k = k * (1 + (a - 1) * k_a)

    def tm(t):
        return jnp.moveaxis(split_heads(t.astype(F32), A_HEAD_DIM), 1, 0)

    xs = (tm(r), tm(decay), tm(k), tm(v), jnp.moveaxis(kk, 1, 0), tm(a))

    def step(S, inp):
        r_t, w_t, k_t, v_t, kk_t, a_t = inp
        sa = jnp.einsum('bhvk,bhk->bhv', S, -kk_t)
        S = (S * w_t[:, :, None, :] + sa[..., None] * (kk_t * a_t)[:, :, None, :]
             + v_t[..., None] * k_t[:, :, None, :])
        return S, jnp.einsum('bhvk,bhk->bhv', S, r_t)

    S0 = jnp.zeros((bsz, A_HEADS, A_HEAD_DIM, A_HEAD_DIM), F32)
    _, o = lax.scan(step, S0, xs)
    o = jnp.moveaxis(o, 0, 1)
    mean = jnp.mean(o, axis=-1, keepdims=True)
    var = jnp.mean(jnp.square(o - mean), axis=-1, keepdims=True)
    o = ((o - mean) * lax.rsqrt(var + A_GN_EPS)).reshape(bsz, L, A_WIDTH)
    o = o * ln_w.astype(F32) + ln_b.astype(F32)
    bonus = (jnp.sum(split_heads((r * k * r_k).astype(F32), A_HEAD_DIM), axis=-1, keepdims=True)
             * split_heads(v.astype(F32), A_HEAD_DIM)).reshape(bsz, L, A_WIDTH)
    return ((o + bonus) * g.astype(F32)).astype(dt), v


def pool_branch(u, mix, scale):
    dt = u.dtype
    bsz, L, _ = u.shape
    ug = u.reshape(bsz, L, B_GROUPS, B_GROUP_DIM).astype(F32)
    cs = jnp.cumsum(ug, axis=1)
    wmax = max(B_WINDOWS)
    cs_pad = jnp.pad(cs, ((0, 0), (wmax, 0), (0, 0), (0, 0)))
    t = jnp.arange(L)
    outs = []
    for gi, w in enumerate(B_WINDOWS):
        prev = cs_pad[:, wmax - w:wmax - w + L, gi]
        cnt = jnp.minimum(t + 1, w).astype(F32)[None, :, None]
        outs.append((cs[:, :, gi] - prev) / cnt)
    pooled = jnp.stack(outs, axis=2) - ug
    y = jnp.einsum('blgc,gcd->blgd', pooled, mix.astype(F32)).reshape(bsz, L, B_WIDTH)
    return (y * scale.astype(F32)).astype(dt)


def swa_branch(u, sinks, slopes):
    dt = u.dtype
    bsz, L, _ = u.shape
    pad = (-L) % C_BLOCK
    Lp = L + pad
    nb = Lp // C_BLOCK

    def blocks(t, nh):
        t = jnp.pad(t.astype(F32), ((0, 0), (pad, 0), (0, 0)))
        return t.reshape(bsz, nb, C_BLOCK, nh, C_HEAD_DIM)

    q = blocks(u[..., :C_WIDTH], C_HEADS).reshape(bsz, nb, C_BLOCK, C_KV_HEADS, C_GROUP, C_HEAD_DIM)
    k = blocks(u[..., C_WIDTH:C_WIDTH + C_KV_WIDTH], C_KV_HEADS)
    v = blocks(u[..., C_WIDTH + C_KV_WIDTH:], C_KV_HEADS)

    def with_prev(t):
        prev = jnp.pad(t, ((0, 0), (1, 0), (0, 0), (0, 0), (0, 0)))[:, :-1]
        return jnp.concatenate([prev, t], axis=2)

    kw, vw = with_prev(k), with_prev(v)
    scores = jnp.einsum('bnqhgd,bnshd->bhgnqs', q, kw) * (C_HEAD_DIM ** -0.5)
    qi = np.arange(C_BLOCK)[:, None]
    si = np.arange(2 * C_BLOCK)[None, :]
    dist = C_BLOCK + qi - si
    band = (dist >= 0) & (dist < WINDOW)
    key_pos = np.arange(nb)[:, None] * C_BLOCK + np.arange(2 * C_BLOCK)[None, :] - C_BLOCK
    mask = band[None] & (key_pos >= pad)[:, None, :]
    sl = slopes.reshape(C_KV_HEADS, C_GROUP)
    logits = scores - sl[:, :, None, None, None] * jnp.asarray(dist, F32)
    logits = jnp.where(mask, logits, MASK_VALUE)
    sink = sinks.astype(F32).reshape(C_KV_HEADS, C_GROUP)[None, :, :, None, None, None]
    m = jnp.maximum(jnp.max(logits, axis=-1, keepdims=True), sink)
    p = jnp.exp(logits - m)
    denom = jnp.sum(p, axis=-1, keepdims=True) + jnp.exp(sink - m)
    out = jnp.einsum('bhgnqs,bnshd->bnqhgd', p / denom, vw)
    return out.reshape(bsz, Lp, C_WIDTH)[:, pad:].astype(dt)


def hgrn2_branch(u, lb, norm_g):
    dt = u.dtype
    bsz, L, _ = u.shape
    q = jax.nn.silu(u[..., :D_WIDTH].astype(F32))
    fpre = u[..., D_WIDTH:2 * D_WIDTH].astype(F32)
    i_in = u[..., 2 * D_WIDTH:2 * D_WIDTH + D_OUT].astype(F32)
    g = u[..., 2 * D_WIDTH + D_OUT:].astype(F32)
    f = lb + (1 - lb) * jax.nn.sigmoid(fpre)
    logf = jnp.log(jnp.maximum(f, 1e-30))
    k = (1 - lb) * jax.nn.sigmoid(-fpre)
    pad = (-L) % D_CHUNK
    n = (L + pad) // D_CHUNK

    def chunks(t, d):
        t = jnp.pad(t, ((0, 0), (pad, 0), (0, 0)))
        return t.reshape(bsz, n, D_CHUNK, D_HEADS, d).transpose(1, 0, 3, 2, 4)

    qc, kc, vc = chunks(q, D_KEY_DIM), chunks(k, D_KEY_DIM), chunks(i_in, D_VAL_DIM)
    bc = jnp.cumsum(chunks(logf, D_KEY_DIM), axis=3)
    causal = jnp.asarray(np.tril(np.ones((D_CHUNK, D_CHUNK), dtype=bool)))

    def chunk_step(S, inp):
        q_, k_, v_, b_ = inp
        o_inter = jnp.einsum('bhtc,bhcv->bhtv', q_ * jnp.exp(b_), S)
        diff = b_[:, :, :, None, :] - b_[:, :, None, :, :]
        dec = jnp.where(causal[:, :, None], jnp.exp(jnp.minimum(diff, 0.0)), 0.0)
        A = jnp.einsum('bhtc,bhsc,bhtsc->bhts', q_, k_, dec)
        o = o_inter + jnp.einsum('bhts,bhsv->bhtv', A, v_)
        b_last = b_[:, :, -1:, :]
        S = S * jnp.exp(b_last)[:, :, 0, :, None] + jnp.einsum(
            'bhsc,bhsv->bhcv', k_ * jnp.exp(b_last - b_), v_)
        return S, o

    S0 = jnp.zeros((bsz, D_HEADS, D_KEY_DIM, D_VAL_DIM), F32)
    _, o = lax.scan(chunk_step, S0, (qc, kc, vc, bc))
    o = o.transpose(1, 0, 3, 2, 4).reshape(bsz, n * D_CHUNK, D_HEADS, D_VAL_DIM)[:, pad:]
    o = o * lax.rsqrt(jnp.mean(o * o, axis=-1, keepdims=True) + NORM_EPS)
    o = o.reshape(bsz, L, D_OUT) * norm_g.astype(F32) * jax.nn.silu(g)
    return o.astype(dt)


def setup_inputs(seed: int = 0) -> dict:
    key = jax.random.key(seed)
    ks = jax.random.split(key, 32)

    def nrm(k, shape, scale):
        return jax.random.normal(k, shape, F32) * scale

    row_scale = jnp.concatenate([
        jnp.full((A_WIDTH,), A_WIDTH ** -0.5, F32), jnp.full((B_WIDTH,), B_WIDTH ** -0.5, F32),
        jnp.full((C_WIDTH,), C_WIDTH ** -0.5, F32), jnp.full((D_OUT,), D_OUT ** -0.5, F32)])
    return {
        "x": nrm(ks[0], (BATCH, SEQ, D_MODEL), 1.0),
        "meta": nrm(ks[1], (N_META, D_MODEL), 1.0),
        "norm_mix": 1.0 + nrm(ks[2], (DEPTH, D_MODEL), 0.1),
        "norm_ffn": 1.0 + nrm(ks[3], (DEPTH, D_MODEL), 0.1),
        "norm_final": 1.0 + nrm(ks[4], (D_MODEL,), 0.1),
        "w_in": nrm(ks[5], (DEPTH, D_MODEL, IN_COLS), D_MODEL ** -0.5),
        "w_branch": nrm(ks[6], (DEPTH, MIX_WIDTH, D_MODEL), 1.0) * row_scale[None, :, None],
        "w_out": nrm(ks[7], (DEPTH, D_MODEL, D_MODEL), D_MODEL ** -0.5),
        "a_mu": jax.random.uniform(ks[8], (DEPTH, A_COLS), F32),
        "a_w_up": nrm(ks[9], (DEPTH, A_DECAY_LORA, A_WIDTH), A_DECAY_LORA ** -0.5),
        "a_w0": -1.0 + nrm(ks[10], (DEPTH, A_WIDTH), 0.5),
        "a_a_up": nrm(ks[11], (DEPTH, A_ICL_LORA, A_WIDTH), A_ICL_LORA ** -0.5),
        "a_a0": nrm(ks[12], (DEPTH, A_WIDTH), 0.1),
        "a_g_up": nrm(ks[13], (DEPTH, A_GATE_LORA, A_WIDTH), A_GATE_LORA ** -0.5),
        "a_kk": 0.85 + nrm(ks[14], (DEPTH, A_WIDTH), 0.1),
        "a_ka": 1.0 + nrm(ks[15], (DEPTH, A_WIDTH), 0.1),
        "a_rk": nrm(ks[16], (DEPTH, A_WIDTH), 0.1),
        "a_ln_w": 1.0 + nrm(ks[17], (DEPTH, A_WIDTH), 0.1),
        "a_ln_b": nrm(ks[18], (DEPTH, A_WIDTH), 0.01),
        "a_vres_down": nrm(ks[19], (DEPTH - 1, A_WIDTH, A_VRES_LORA), A_WIDTH ** -0.5),
        "a_vres_up": nrm(ks[20], (DEPTH - 1, A_VRES_LORA, A_WIDTH), A_VRES_LORA ** -0.5),
        "a_vres0": nrm(ks[21], (DEPTH - 1, A_WIDTH), 0.1),
        "b_mix": nrm(ks[22], (DEPTH, B_GROUPS, B_GROUP_DIM, B_GROUP_DIM), B_GROUP_DIM ** -0.5),
        "b_scale": 1.0 + nrm(ks[23], (DEPTH, B_WIDTH), 0.1),
        "c_sinks": nrm(ks[24], (DEPTH, C_HEADS), 0.5),
        "d_lower_bounds": nrm(ks[25], (DEPTH, D_WIDTH), 0.1),
        "d_norm": 1.0 + nrm(ks[26], (DEPTH, D_OUT), 0.1),
        "w_ffn_up": nrm(ks[27], (DEPTH, D_MODEL, 2 * D_FF), D_MODEL ** -0.5),
        "w_ffn_down": nrm(ks[28], (DEPTH, D_FF, D_MODEL), D_FF ** -0.5),
    }


def reference(x, meta, norm_mix, norm_ffn, norm_final, w_in, w_branch, w_out,
              a_mu, a_w_up, a_w0, a_a_up, a_a0, a_g_up, a_kk, a_ka, a_rk, a_ln_w, a_ln_b,
              a_vres_down, a_vres_up, a_vres0, b_mix, b_scale, c_sinks,
              d_lower_bounds, d_norm, w_ffn_up, w_ffn_down):
    bsz = x.shape[0]
    h = jnp.concatenate(
        [jnp.broadcast_to(meta.astype(x.dtype)[None], (bsz, N_META, D_MODEL)), x], axis=1)
    L = h.shape[1]
    lb_w = jax.nn.softmax(d_lower_bounds.astype(F32), axis=0)
    lb_table = jnp.cumsum(lb_w, axis=0) - lb_w[0]
    slopes = alibi_slopes(C_HEADS)
    v_first = None
    for l in range(DEPTH):
        z = rms_norm(h, norm_mix[l])
        proj = z @ w_in[l]
        gates = jax.nn.sigmoid(proj[..., :GATE_COLS]).reshape(bsz, L, N_BRANCHES, D_MODEL)
        u_a = proj[..., OFF_A:OFF_B]
        if l == 0:
            y_a, v_first = rwkv7_branch(u_a, a_mu[l], a_w_up[l], a_w0[l], a_a_up[l], a_a0[l],
                                        a_g_up[l], a_kk[l], a_ka[l], a_rk[l], a_ln_w[l], a_ln_b[l],
                                        None, None, None, None)
        else:
            y_a, _ = rwkv7_branch(u_a, a_mu[l], a_w_up[l], a_w0[l], a_a_up[l], a_a0[l],
                                  a_g_up[l], a_kk[l], a_ka[l], a_rk[l], a_ln_w[l], a_ln_b[l],
                                  v_first, a_vres_down[l - 1], a_vres_up[l - 1], a_vres0[l - 1])
        y_b = pool_branch(proj[..., OFF_B:OFF_C], b_mix[l], b_scale[l])
        y_c = swa_branch(proj[..., OFF_C:OFF_D], c_sinks[l], slopes)
        y_d = hgrn2_branch(proj[..., OFF_D:IN_COLS], lb_table[l], d_norm[l])
        wb = w_branch[l]
        merged = (gates[:, :, 0] * (y_a @ wb[:ROW_B])
                  + gates[:, :, 1] * (y_b @ wb[ROW_B:ROW_C])
                  + gates[:, :, 2] * (y_c @ wb[ROW_C:ROW_D])
                  + gates[:, :, 3] * (y_d @ wb[ROW_D:]))
        h = h + merged @ w_out[l]
        z = rms_norm(h, norm_ffn[l])
        gu = z @ w_ffn_up[l]
        h = h + (jax.nn.silu(gu[..., :D_FF]) * gu[..., D_FF:]) @ w_ffn_down[l]
    return rms_norm(h, norm_final)[:, N_META:]
```

```python
import contextlib
import numpy as np
import concourse.bass as bass
import concourse.mybir as mybir
from concourse.bass_utils import run_bass_kernel_spmd

F32 = mybir.dt.float32
BF16 = mybir.dt.bfloat16
AF = mybir.ActivationFunctionType
ALU = mybir.AluOpType

D_MODEL = 1024
N_META = 16
DFF = 2816
NCORES = 8
C0 = 0.6065306597126334
NEG = -30000.0
B_WINDOWS = (2, 4, 8, 16)
OFF_A, OFF_B, OFF_C, OFF_D = 4096, 5120, 5376, 6144


class Buf:
    __slots__ = ("name", "w", "r", "dsem", "dval", "excl")

    def __init__(self, name):
        self.name = name
        self.w = None
        self.r = []
        self.dsem = None
        self.dval = 0
        self.excl = False


class Reg:
    __slots__ = ("ap", "buf")

    def __init__(self, ap, buf):
        self.ap = ap
        self.buf = buf

    def __getitem__(self, idx):
        return Reg(self.ap[idx], self.buf)


def R(x):
    return x.ap if isinstance(x, Reg) else x


class Prog:
    ENG = ("pe", "act", "dve", "pool", "sp")

    def __init__(self, nc):
        self.nc = nc
        self.stack = contextlib.ExitStack()
        self.ops = {e: [] for e in self.ENG}
        self.cnt = {e: 0 for e in self.ENG}
        self.seen = {e: {} for e in self.ENG}
        self.sems = {}
        self.nsem = 0
        for e in ("pe", "act", "dve", "pool"):
            self.sems[e] = self.stack.enter_context(nc.semaphore("s_" + e))
        self.nops = 0

    def tile(self, name, shape, dtype):
        t = self.stack.enter_context(self.nc.sbuf_tensor(name, list(shape), dtype))
        return Reg(t[:], Buf(name))

    def psum(self, name, shape, dtype):
        t = self.stack.enter_context(self.nc.psum_tensor(name, list(shape), dtype))
        b = Buf(name)
        b.excl = True
        return Reg(t[:], b)

    def dsem(self, buf):
        if buf.dsem is None:
            buf.dsem = self.stack.enter_context(self.nc.semaphore("d%d" % self.nsem))
            self.nsem += 1
        return buf.dsem

    def _need(self, eng, reads, writes):
        need = {}

        def add(tok, same_ok):
            if tok is None:
                return
            k, v = tok
            if k == eng and (eng == "pe"):
                return
            if need.get(k, 0) < v:
                need[k] = v

        for b in reads:
            add(b.w, False)
        for b in writes:
            add(b.w, True)
            for t in b.r:
                add(t, True)
        seen = self.seen[eng]
        waits = []
        for k, v in need.items():
            if seen.get(k, 0) >= v:
                continue
            seen[k] = v
            waits.append((k, v))
        return waits

    def _commit(self, tok, reads, writes):
        for b in writes:
            b.w = tok
            b.r = []
        for b in reads:
            b.r.append(tok)
            if len(b.r) > 16:
                d = {}
                for k, v in b.r:
                    if d.get(k, 0) < v:
                        d[k] = v
                b.r = list(d.items())

    def op(self, eng, emit, reads, writes):
        reads = [r.buf for r in reads if isinstance(r, Reg) and r.buf is not None]
        writes = [r.buf for r in writes if isinstance(r, Reg) and r.buf is not None]
        ex = [b for b in reads if b.excl]
        if ex:
            reads = [b for b in reads if not b.excl]
            writes = writes + ex
        waits = self._need(eng, reads, writes)
        self.cnt[eng] += 1
        tok = (eng, self.cnt[eng])
        self.ops[eng].append((waits, emit, (eng, 1)))
        self._commit(tok, reads, writes)
        self.nops += 1

    def dma(self, q, out, in_, owner=None, **kw):
        if owner is None:
            owner = out.buf if out.buf is not None else in_.buf
        reads = [in_.buf] if in_.buf is not None else []
        writes = [out.buf] if out.buf is not None else []
        waits = self._need(q, reads, writes)
        self.dsem(owner)
        owner.dval += 16
        tok = (owner, owner.dval)
        oap, iap = out.ap, in_.ap

        def emit(e):
            return e.dma_start(out=oap, in_=iap, **kw)
        self.ops[q].append((waits, emit, (owner, 16)))
        self._commit(tok, reads, writes)
        self.nops += 1

    def wait_all(self, eng, bufs):
        need = {}
        for b in bufs:
            for tok in ([b.w] if b.w else []) + list(b.r):
                k, v = tok
                if need.get(k, 0) < v:
                    need[k] = v
        self.ops[eng].append((list(need.items()), None, None))

    def _semh(self, k):
        return self.sems[k] if isinstance(k, str) else k.dsem

    def emit(self):
        with self.nc.Block() as block:
            def run(name):
                def f(e):
                    for waits, emit, inc in self.ops[name]:
                        for k, v in waits:
                            e.wait_ge(self._semh(k), v)
                        if emit is None:
                            continue
                        emit(e).then_inc(self._semh(inc[0]), inc[1])
                return f
            block.tensor(run("pe"))
            block.scalar(run("act"))
            block.vector(run("dve"))
            block.gpsimd(run("pool"))
            block.sync(run("sp"))

    def mm(self, out, lhsT, rhs, start=True, stop=True):
        o, l, r = out.ap, lhsT.ap, rhs.ap
        sp = l.start_partition
        sp = sp() if callable(sp) else sp
        rg = (int(sp), int(l.shape[0]))
        last = getattr(self, "last_rg", None)
        if last is not None and last != rg and self.cnt["pe"] > 0:
            v = self.cnt["pe"]
            if self.seen["pe"].get("pe", 0) < v:
                self.seen["pe"]["pe"] = v
                self.ops["pe"].append(([("pe", v)], None, None))
        self.last_rg = rg
        self.op("pe", lambda e: e.matmul(o, l, r, start=start, stop=stop), [lhsT, rhs], [out])

    def tt(self, eng, out, in0, in1, op):
        o, a, b = out.ap, in0.ap, in1.ap
        self.op(eng, lambda e: e.tensor_tensor(out=o, in0=a, in1=b, op=op), [in0, in1], [out])

    def ts(self, eng, out, in0, s1, s2, op0, op1=None):
        o, a = out.ap, in0.ap
        rs = [in0] + [s for s in (s1, s2) if isinstance(s, Reg)]
        if op1 is None:
            self.op(eng, lambda e: e.tensor_scalar(out=o, in0=a, scalar1=R(s1), scalar2=None, op0=op0), rs, [out])
        else:
            self.op(eng, lambda e: e.tensor_scalar(out=o, in0=a, scalar1=R(s1), scalar2=R(s2), op0=op0, op1=op1), rs, [out])

    def stt(self, eng, out, in0, sc, in1, op0, op1):
        o, a, b = out.ap, in0.ap, in1.ap
        rs = [in0, in1] + ([sc] if isinstance(sc, Reg) else [])
        self.op(eng, lambda e: e.scalar_tensor_tensor(out=o, in0=a, scalar=R(sc), in1=b, op0=op0, op1=op1), rs, [out])

    def act(self, out, in_, func, bias=None, scale=1.0):
        o, a = out.ap, in_.ap
        rs = [in_] + [s for s in (bias, scale) if isinstance(s, Reg)]
        if bias is None:
            self.op("act", lambda e: e.activation(out=o, in_=a, func=func, scale=R(scale)), rs, [out])
        else:
            self.op("act", lambda e: e.activation(out=o, in_=a, func=func, bias=R(bias), scale=R(scale)), rs, [out])

    def copy(self, eng, out, in_):
        o, a = out.ap, in_.ap
        if eng == "act":
            self.op("act", lambda e: e.copy(out=o, in_=a), [in_], [out])
        else:
            self.op(eng, lambda e: e.tensor_copy(out=o, in_=a), [in_], [out])

    def memset(self, eng, out, val):
        o = out.ap
        self.op(eng, lambda e: e.memset(o, val), [], [out])

    def recip(self, out, in_):
        o, a = out.ap, in_.ap
        self.op("dve", lambda e: e.reciprocal(out=o, in_=a), [in_], [out])

    def scan(self, out, d0, d1):
        o, a, b = out.ap, d0.ap, d1.ap
        self.op("dve", lambda e: e.tensor_tensor_scan(out=o, data0=a, data1=b, initial=0.0,
                                                      op0=ALU.mult, op1=ALU.add), [d0, d1], [out])


def pcol(v):
    v = np.asarray(v, np.float32)
    return np.ascontiguousarray(v.reshape(-1, 128).T)


def make_consts():
    c = {}
    c["ident"] = np.eye(128, dtype=np.float32)
    bd = np.zeros((128, 128), np.float32)
    bd[:64, :64] = 1.0
    bd[64:, 64:] = 1.0
    c["bd1"] = bd
    c["bd64"] = bd / 64.0
    c["onesm"] = np.full((128, 128), 1.0 / D_MODEL, np.float32)
    i = np.arange(128)[:, None]
    j = np.arange(128)[None, :]
    same64 = (i // 64) == (j // 64)
    same32 = (i // 32) == (j // 32)
    c["mu64"] = (same64 & (i < j)).astype(np.float32)
    c["mui64"] = (same64 & (i <= j)).astype(np.float32)
    c["ml64"] = (same64 & (j < i)).astype(np.float32)
    c["mui32"] = (same32 & (i <= j)).astype(np.float32)
    m64 = np.ones((128, 256), np.float32)
    m64[:, ::64] = 0.0
    m32 = np.ones((128, 256), np.float32)
    m32[:, ::32] = 0.0
    hm = np.zeros((128, 2), np.float32)
    hm[:, 0] = ((np.arange(128) // 32) % 2 == 0)
    hm[:, 1] = ((np.arange(128) // 32) % 2 == 1)
    c["hm32"] = hm
    c["rs64"] = m64
    c["rs32"] = m32
    si = np.arange(128)[:, None]
    qi = np.arange(128)[None, :]
    bias = np.zeros((128, 8, 256), np.float32)
    for h in range(8):
        sl = 2.0 ** (-8.0 * (h + 1) / 8)
        dprev = 128 + qi - si
        dcur = qi - si
        bp = np.where((dprev >= 0) & (dprev < 128), -sl * dprev, NEG)
        bc = np.where((dcur >= 0) & (dcur < 128), -sl * dcur, NEG)
        bias[:, h, 0:128] = bp
        bias[:, h, 128:256] = bc
    c["swab"] = bias
    pt = np.zeros((128, 2, 16), np.float32)
    for ch in range(2):
        for half in range(2):
            w = B_WINDOWS[ch * 2 + half]
            pt[half * 64:(half + 1) * 64, ch, :] = 1.0 / np.minimum(np.arange(16) + 1, w)
    c["poolt"] = pt
    return c


CONST_ORDER = ["ident", "bd1", "bd64", "onesm", "mu64", "mui64", "ml64", "mui32", "rs64", "rs32",
               "swab", "poolt", "hm32"]

PV = {"nmix": 0, "nffn": 8, "mu": 16, "w0": 24, "a0": 26, "kk": 28, "ka": 30, "rk": 32, "lnw": 34,
      "lnb": 36, "vres0": 38, "bscale": 40, "dnorm": 42}
PVL = 44


def pack_small(inp, depth):
    cols = []
    for l in range(depth):
        blk = np.zeros((128, PVL), np.float32)
        blk[:, 0:8] = pcol(inp["norm_mix"][l])
        blk[:, 8:16] = pcol(inp["norm_ffn"][l])
        blk[:, 16:24] = pcol(inp["a_mu"][l])
        for nm, key in (("w0", "a_w0"), ("a0", "a_a0"), ("kk", "a_kk"), ("ka", "a_ka"), ("rk", "a_rk"),
                        ("lnw", "a_ln_w"), ("lnb", "a_ln_b"), ("bscale", "b_scale"), ("dnorm", "d_norm")):
            blk[:, PV[nm]:PV[nm] + 2] = pcol(inp[key][l])
        if l >= 1:
            blk[:, PV["vres0"]:PV["vres0"] + 2] = pcol(inp["a_vres0"][l - 1])
        cols.append(blk)
    g = np.zeros((128, 8 + 2 * depth + 8 * depth), np.float32)
    g[:, 0:8] = pcol(inp["norm_final"])
    for l in range(depth):
        g[:, 8 + 2 * l:10 + 2 * l] = pcol(inp["d_lower_bounds"][l])
        g[:, 8 + 2 * depth + 8 * l:8 + 2 * depth + 8 * l + 8] = np.broadcast_to(
            np.asarray(inp["c_sinks"][l], np.float32)[None, :], (128, 8))
    pvec = np.concatenate(cols + [g], axis=1)
    lora = np.zeros((depth, 128, 256), np.float32)
    gup = np.zeros((depth, 128, 256), np.float32)
    bmix = np.zeros((depth, 128, 2, 128), np.float32)
    vdn = np.zeros((depth, 128, 2, 32), np.float32)
    vup = np.zeros((depth, 32, 256), np.float32)
    for l in range(depth):
        lora[l, :64] = inp["a_w_up"][l]
        lora[l, 64:] = inp["a_a_up"][l]
        gup[l] = inp["a_g_up"][l]
        for gi in range(4):
            ch, half = gi // 2, gi % 2
            bmix[l, half * 64:(half + 1) * 64, ch, half * 64:(half + 1) * 64] = inp["b_mix"][l, gi]
        if l >= 1:
            vdn[l] = np.asarray(inp["a_vres_down"][l - 1], np.float32).reshape(2, 128, 32).transpose(1, 0, 2)
            vup[l] = inp["a_vres_up"][l - 1]
    return {"pvec": np.ascontiguousarray(pvec), "lora": lora, "gup": gup, "bmix": bmix,
            "vdn": vdn, "vup": vup}


def build_nc(nseq, seq, depth, NT=256, dbg=None, stage=9, castmode=0):
    L = N_META + seq
    PADL = (-L) % 128
    LP = L + PADL
    assert PADL == 112 or True
    tiles = []
    p = 0
    while p < LP:
        n = min(NT, LP - p)
        tiles.append((p, n))
        p += n
    NB = LP // 128

    nc = bass.Bass("TRN2", target_bir_lowering=False)
    P = Prog(nc)
    DR = lambda ap: Reg(ap, None)

    def din(name, shape, dt=F32):
        return nc.dram_tensor(name, list(shape), dt, kind="ExternalInput").ap()

    x_d = din("x", [nseq, seq, D_MODEL])
    meta_d = din("meta", [N_META, D_MODEL])
    win_d = din("w_in", [depth, D_MODEL, 7168])
    wbr_d = din("w_branch", [depth, 1280, D_MODEL])
    wout_d = din("w_out", [depth, D_MODEL, D_MODEL])
    wup_d = din("w_ffn_up", [depth, D_MODEL, 2 * DFF])
    wdn_d = din("w_ffn_down", [depth, DFF, D_MODEL])
    NPV = PVL * depth + 8 + 2 * depth + 8 * depth
    pvec_d = din("pvec", [128, NPV])
    lora_d = din("lora", [depth, 128, 256])
    gup_d = din("gup", [depth, 128, 256])
    bmix_d = din("bmix", [depth, 128, 2, 128])
    vdn_d = din("vdn", [depth, 128, 2, 32])
    vup_d = din("vup", [depth, 32, 256])
    cshape = {"ident": [128, 128], "bd1": [128, 128], "bd64": [128, 128], "onesm": [128, 128],
              "mu64": [128, 128], "mui64": [128, 128], "ml64": [128, 128], "mui32": [128, 128],
              "rs64": [128, 256], "rs32": [128, 256], "swab": [128, 8, 256], "hm32": [128, 2],
              "poolt": [128, 2, 16]}
    c_d = {k: din("c_" + k, cshape[k]) for k in CONST_ORDER}
    out_d = nc.dram_tensor("out", [nseq, seq, D_MODEL], F32, kind="ExternalOutput").ap()

    def dscr(name, shape):
        return nc.dram_tensor(name, list(shape), BF16, kind="Internal").ap()
    win_s = [Reg(dscr("s_win%d" % l, [D_MODEL, 7168]), Buf("s_win%d" % l)) for l in range(depth)]
    wbr_s = [Reg(dscr("s_wbr%d" % l, [1280, D_MODEL]), win_s[l].buf) for l in range(depth)]
    wout_s = [Reg(dscr("s_wout%d" % l, [D_MODEL, D_MODEL]), win_s[l].buf) for l in range(depth)]
    wup_s = [Reg(dscr("s_wup%d" % l, [D_MODEL, 2 * DFF]), win_s[l].buf) for l in range(depth)]
    wdn_s = [Reg(dscr("s_wdn%d" % l, [DFF, D_MODEL]), win_s[l].buf) for l in range(depth)]

    cst = {k: P.tile("k_" + k, cshape[k], F32) for k in CONST_ORDER}
    cbuf = Buf("consts")
    for k in cst:
        cst[k].buf = cbuf
    pvec = P.tile("pvec_sb", [128, NPV], F32)
    pvec.buf = cbuf
    st32 = P.tile("st32", [128, 2, 256], F32)
    st32b = P.tile("st32b", [128, 2, 256], F32)
    lora = [P.tile("lora%d" % l, [128, 256], BF16) for l in range(depth)]
    gup = [P.tile("gup%d" % l, [128, 256], BF16) for l in range(depth)]
    bmix = [P.tile("bmix%d" % l, [128, 2, 128], BF16) for l in range(depth)]
    vdn = [P.tile("vdn%d" % l, [128, 2, 32], BF16) for l in range(depth)]
    vup = [P.tile("vup%d" % l, [32, 256], BF16) for l in range(depth)]
    identb = P.tile("identb", [128, 128], BF16)
    mu64b = P.tile("mu64b", [128, 128], BF16)
    drv = P.tile("drv", [128, 64], F32)
    DV_L = 12
    esink = P.tile("esink", [128, depth, 8], F32)
    kcol = P.tile("kcol", [128, 8], F32)
    P.memset("dve", kcol[:, 0:1], 0.0)
    P.memset("dve", kcol[:, 1:2], 1.0)
    P.memset("dve", kcol[:, 2:3], 1e-6)
    P.memset("dve", kcol[:, 3:4], 64e-5)
    P.memset("dve", kcol[:, 4:5], 1e-24)
    P.memset("dve", kcol[:, 5:6], 1e-30)

    hT = P.tile("hT", [128, 8, NT], F32)
    zb = P.tile("zb", [128, 8, NT], BF16)
    mg = P.tile("mg", [128, 8, NT], F32)
    mgb = zb
    vfirst = P.tile("vfirst", [128, 2, NT], F32)
    NWB = 3
    wb = [P.tile("wb%d" % i, [128, 8, 1024], BF16) for i in range(NWB)]
    wbi = [0]
    u8 = P.tile("u8", [128, 8, NT], F32)
    assert NT == 256
    xs = [Reg(u8.ap[:, 4 * i:4 * i + 4, :].rearrange("p a c -> p (a c)"), u8.buf) for i in range(2)]
    xsi = [0]
    os_ = xs
    osi = xsi
    fa = [P.tile("fa%d" % i, [128, 2, NT], F32) for i in range(8)]
    ha = [P.tile("ha%d" % i, [128, 2, NT], BF16) for i in range(8)]
    hq = P.tile("hq", [128, 4, NT], BF16)
    actb = P.tile("actb", [128, 22, NT], BF16)
    tokm = [P.tile("tokm%d" % i, [128, NT // 128, 2, 128], BF16) for i in range(3)]
    NBK = NT // 128
    sc = {nm: P.tile("sc_" + nm, [128, NBK, 4, 128], BF16) for nm in ("z", "ak", "rb", "rk")}
    nx = [P.tile("nx%d" % i, [128, 4, 128], BF16) for i in range(2)]
    ny = [P.tile("ny%d" % i, [128, 4, 128], BF16) for i in range(2)]
    nix = [P.tile("nix%d" % i, [128, 4, 128], BF16) for i in range(2)]
    nz = [P.tile("nz%d" % i, [128, 4, 128], BF16) for i in range(2)]
    rhsb = P.tile("rhsb", [128, 4, 64], BF16)
    ub = P.tile("ub", [128, 4, 64], BF16)
    wc = P.tile("wc", [128, 2, NT // 32], F32)
    kd = P.tile("kd", [128, 2, (NBK + 1) * 128], BF16)
    vtk = P.tile("vtk", [128, NBK + 1, 128], BF16)
    pT = P.tile("pT", [128, 8, 256], BF16)
    sTs = [P.tile("sT%d" % i, [128, 2, 256], F32) for i in range(2)]
    ycT = P.tile("ycT", [128, 4, NT], BF16)
    ubp = P.tile("ubp", [128, 2, 16 + NT], F32)
    pl = [P.tile("pl%d" % i, [128, 2, 16 + NT], F32) for i in range(3)]
    car_sh = [P.tile("car_sh%d" % l, [128, 8], F32) for l in range(depth)]
    stA = [P.tile("stA%d" % l, [128, 2, 64], F32) for l in range(depth)]
    stAb = [P.tile("stAb%d" % l, [128, 2, 64], BF16) for l in range(depth)]
    stD = [P.tile("stD%d" % l, [128, 2, 64], F32) for l in range(depth)]
    stDb = [P.tile("stDb%d" % l, [128, 2, 64], BF16) for l in range(depth)]
    car_p = [P.tile("car_p%d" % l, [128, 2, 16], F32) for l in range(depth)]
    car_k = [P.tile("car_k%d" % l, [128, 2, 128], BF16) for l in range(depth)]
    car_v = [P.tile("car_v%d" % l, [128, 128], BF16) for l in range(depth)]

    psb = [P.psum("ps%d" % i, [128, 512], F32) for i in range(6)]
    psacc = [P.psum("pa%d" % i, [128, 512], F32) for i in range(2)]
    psi = [0]

    def PS():
        r = psb[psi[0] % 6]
        psi[0] += 1
        return r

    def view(ps, ncols, a):
        return Reg(ps.ap[:, 0:ncols].rearrange("p (a c) -> p a c", a=a), ps.buf)

    class BC:
        def __init__(self, m):
            self.m = m

    def bc(m):
        return BC(m)

    _tt = P.tt

    def tt_b(eng, out, in0, in1, op):
        if isinstance(in1, BC):
            for a in range(4):
                _tt(eng, out[:, a, :], in0[:, a, :], in1.m, op)
        else:
            _tt(eng, out, in0, in1, op)
    P.tt = tt_b

    for k in CONST_ORDER:
        P.dma("sp", cst[k], DR(c_d[k]), owner=cbuf)
    P.dma("sp", pvec, DR(pvec_d), owner=cbuf)
    P.copy("dve", identb, cst["ident"])
    P.copy("dve", mu64b, cst["mu64"])
    for l in range(depth):
        for (dst, src, shp) in ((lora[l], lora_d[l], None), (gup[l], gup_d[l], None)):
            P.dma("sp", st32[:, 0, :], DR(src))
            P.copy("dve", dst, st32[:, 0, :])
        P.dma("sp", st32b[:, :, 0:128], DR(bmix_d[l]))
        P.copy("dve", bmix[l], st32b[:, :, 0:128])
        P.dma("sp", st32[:, :, 0:32], DR(vdn_d[l]))
        P.copy("dve", vdn[l], st32[:, :, 0:32])
        P.dma("sp", st32b[0:32, 0, :], DR(vup_d[l]))
        P.copy("dve", vup[l], st32b[0:32, 0, :])
    for l in range(depth):
        for (dst, src, rows) in ((win_s[l], win_d[l], D_MODEL), (wbr_s[l], wbr_d[l], 1280),
                                 (wout_s[l], wout_d[l], D_MODEL), (wup_s[l], wup_d[l], D_MODEL),
                                 (wdn_s[l], wdn_d[l], DFF)):
            step = 256
            for r0 in range(0, rows, step):
                r1 = min(rows, r0 + step)
                P.dma("pool", Reg(dst.ap[r0:r1, :], None), DR(src[r0:r1, :]), owner=win_s[l].buf)
        win_s[l].buf.w = (win_s[l].buf, win_s[l].buf.dval)

    def pv(l, nm, c=None, n=None):
        o = PVL * l + PV[nm]
        if c is None:
            return pvec[:, o:o + (n or 2)]
        return pvec[:, o + c:o + c + 1]
    GO = PVL * depth
    for l in range(depth):
        P.ts("dve", drv[:, DV_L * l:DV_L * l + 8], pvec[:, PVL * l + 16:PVL * l + 24], -1.0, 1.0, ALU.mult, ALU.add)
        so = GO + 8 + 2 * depth + 8 * l
        P.act(esink[:, l, :], pvec[:, so:so + 8], AF.Exp)
    lbx = fa[0]
    for c in range(2):
        mx = fa[1][:, 0, 0:1]
        P.copy("dve", mx, pvec[:, GO + 8 + c:GO + 9 + c])
        for l in range(1, depth):
            P.tt("dve", mx, mx, pvec[:, GO + 8 + 2 * l + c:GO + 9 + 2 * l + c], ALU.max)
        P.ts("dve", fa[1][:, 0, 1:2], mx, -1.0, None, ALU.mult)
        sm = fa[1][:, 0, 2:3]
        for l in range(depth):
            P.act(lbx[:, 0, l:l + 1], pvec[:, GO + 8 + 2 * l + c:GO + 9 + 2 * l + c], AF.Exp, bias=fa[1][:, 0, 1:2])
            if l == 0:
                P.copy("dve", sm, lbx[:, 0, 0:1])
            else:
                P.tt("dve", sm, sm, lbx[:, 0, l:l + 1], ALU.add)
        P.recip(fa[1][:, 0, 3:4], sm)
        for l in range(depth):
            P.ts("dve", lbx[:, 0, l:l + 1], lbx[:, 0, l:l + 1], fa[1][:, 0, 3:4], None, ALU.mult)
        for l in range(depth):
            dst = drv[:, DV_L * l + 8 + c:DV_L * l + 9 + c]
            if l == 0:
                P.memset("dve", dst, 0.0)
            elif l == 1:
                P.copy("dve", dst, lbx[:, 0, 1:2])
            else:
                P.tt("dve", dst, drv[:, DV_L * (l - 1) + 8 + c:DV_L * (l - 1) + 9 + c], lbx[:, 0, l:l + 1], ALU.add)
            P.ts("dve", drv[:, DV_L * l + 10 + c:DV_L * l + 11 + c], dst, -1.0, 1.0, ALU.mult, ALU.add)

    dbg_outs = {}

    def dump(name, reg, cond=True):
        if not dbg or not cond or name in dbg_outs:
            return
        shp = list(reg.ap.shape)
        d = nc.dram_tensor("dbg_" + name, shp, F32, kind="ExternalOutput").ap()
        dbg_outs[name] = d
        P.dma("pool", DR(d), reg, owner=reg.buf)

    def load_w(src_reg_ap, nkc, ncols, buf_owner, parts=None):
        w = wb[wbi[0] % NWB]
        wbi[0] += 1
        P.dma("sp", w[:, 0:nkc, 0:ncols], Reg(src_reg_ap.rearrange("(k p) c -> p k c", p=128), buf_owner))
        return w

    def rmsnorm(l_gcol, n, zout):
        sq = u8
        for c in range(8):
            if c % 2 == 0:
                P.act(sq[:, c, 0:n], hT[:, c, 0:n], AF.Square)
            else:
                P.tt("pool", sq[:, c, 0:n], hT[:, c, 0:n], hT[:, c, 0:n], ALU.mult)
        ps = PS()
        for c in range(8):
            P.mm(ps[:, 0:n], cst["onesm"], sq[:, c, 0:n], start=(c == 0), stop=(c == 7))
        rs = fa[7][:, 0, 0:n]
        P.act(rs, ps[:, 0:n], AF.Sqrt, bias=kcol[:, 2:3])
        P.recip(rs, rs)
        for c in range(8):
            P.stt("dve", zout[:, c, 0:n], hT[:, c, 0:n], pvec[:, l_gcol + c:l_gcol + c + 1],
                  rs, ALU.mult, ALU.mult)

    def proj(w, kcs, col0, rhs_fn, n):
        ps = PS()
        for i, k in enumerate(kcs):
            P.mm(ps[:, 0:n], w[:, i, col0:col0 + 128], rhs_fn(k), start=(i == 0), stop=(i == len(kcs) - 1))
        return ps

    def gate_merge(l, bi, n, gw, ysrc, wrows, first):
        nk = len(wrows)
        wbw = load_w(wbr_s[l].ap[wrows[0] * 128:(wrows[-1] + 1) * 128, :], nk, 1024, win_s[l].buf)
        for c in range(8):
            pg = proj(gw, range(8), c * 128, lambda k: zb[:, k, 0:n], n)
            g = fa[6][:, c % 2, 0:n]
            P.act(g, pg[:, 0:n], AF.Sigmoid)
            pp = proj(wbw, range(nk), c * 128, ysrc, n)
            if first:
                P.tt("dve", mg[:, c, 0:n], pp[:, 0:n], g, ALU.mult)
            else:
                P.tt("dve", g, pp[:, 0:n], g, ALU.mult)
                P.tt("pool", mg[:, c, 0:n], mg[:, c, 0:n], g, ALU.add)

    def branch_A(l, n, first_tile):
        nb = n // 128
        wA = load_w(win_s[l].ap[:, OFF_A:OFF_A + 1024], 8, 1024, win_s[l].buf)
        u = u8
        om = lambda c: drv[:, DV_L * l + c:DV_L * l + c + 1]
        mu = lambda c: pvec[:, PVL * l + 16 + c:PVL * l + 17 + c]
        for c in range(8):
            ps = proj(wA, range(8), c * 128, lambda k: zb[:, k, 0:n], n)
            P.act(u[:, c, 0:n], ps[:, 0:n], AF.Identity, scale=om(c))
            P.stt("dve", u[:, c, 1:n], ps[:, 0:n - 1], mu(c), u[:, c, 1:n], ALU.mult, ALU.add)
            P.stt("dve", u[:, c, 0:1], car_sh[l][:, c:c + 1], mu(c), u[:, c, 0:1], ALU.mult, ALU.add)
            P.copy("dve", car_sh[l][:, c:c + 1], ps[:, n - 1:n])
        if stage == 3.1:
            return ha[7]
        r = lambda hc: u[:, hc, 0:n]
        k = lambda hc: u[:, 2 + hc, 0:n]
        v = lambda hc: u[:, 4 + hc, 0:n]
        twd, adb, sg = ha[0][:, 0, 0:n], ha[0][:, 1, 0:n], ha[1][:, 0, 0:n]
        P.act(twd[0:64], u[0:64, 6, 0:n], AF.Tanh)
        P.copy("dve", adb[64:128], u[64:128, 6, 0:n])
        P.act(sg, u[:, 7, 0:n], AF.Sigmoid)
        vb = ha[1]
        if l == 0:
            for hc in range(2):
                P.copy("pool", vfirst[:, hc, 0:n], v(hc))
                P.copy("act", ha[2][:, hc, 0:n], v(hc))
        else:
            for hc in range(2):
                P.copy("dve", ha[2][:, hc, 0:n], v(hc))
            ps = PS()
            for hc in range(2):
                P.mm(ps[0:32, 0:n], vdn[l][:, hc, :], ha[2][:, hc, 0:n], start=(hc == 0), stop=(hc == 1))
            lo = ha[3][:, 0, 0:n]
            P.copy("dve", lo[0:32], ps[0:32, 0:n])
            for hc in range(2):
                ps = PS()
                P.mm(ps[:, 0:n], vup[l][:, hc * 128:(hc + 1) * 128], lo[0:32])
                sgm = fa[0][:, hc, 0:n]
                P.act(sgm, ps[:, 0:n], AF.Sigmoid, bias=pv(l, "vres0", hc))
                d = fa[1][:, hc, 0:n]
                P.tt("dve", d, vfirst[:, hc, 0:n], v(hc), ALU.subtract)
                P.tt("dve", d, d, sgm, ALU.mult)
                P.tt("dve", v(hc), v(hc), d, ALU.add)
                P.copy("act", ha[2][:, hc, 0:n], v(hc))
        vbf = ha[2]
        S_, CUM, A_, KKN, KF, E_ = fa[0], fa[1], fa[2], fa[3], fa[4], fa[5]
        rt, kt, bt, at = ha[3], ha[4], ha[5], ha[6]
        for hc in range(2):
            cs = slice(hc * 128, (hc + 1) * 128)
            ps = PS()
            P.mm(ps[:, 0:n], lora[l][0:64, cs], twd[0:64])
            P.act(S_[:, hc, 0:n], ps[:, 0:n], AF.Sigmoid, bias=pv(l, "w0", hc))
            ps = PS()
            P.mm(ps[:, 0:n], lora[l][64:128, cs], adb[64:128])
            P.act(A_[:, hc, 0:n], ps[:, 0:n], AF.Sigmoid, bias=pv(l, "a0", hc))
            P.scan(CUM[:, hc, 0:n], cst["rs64"][:, 0:n], S_[:, hc, 0:n])
            P.ts("dve", KKN[:, hc, 0:n], k(hc), pv(l, "kk", hc), None, ALU.mult)
            P.act(E_[:, hc, 0:n], KKN[:, hc, 0:n], AF.Square)
            ps = PS()
            P.mm(ps[:, 0:n], cst["bd1"], E_[:, hc, 0:n])
            P.ts("dve", E_[:, hc, 0:n], ps[:, 0:n], 1e-24, None, ALU.max)
            P.act(E_[:, hc, 0:n], E_[:, hc, 0:n], AF.Sqrt)
            P.recip(E_[:, hc, 0:n], E_[:, hc, 0:n])
            P.tt("dve", KKN[:, hc, 0:n], KKN[:, hc, 0:n], E_[:, hc, 0:n], ALU.mult)
            P.ts("dve", KF[:, hc, 0:n], A_[:, hc, 0:n], -1.0, pv(l, "ka", hc), ALU.add, ALU.mult)
            P.stt("dve", KF[:, hc, 0:n], KF[:, hc, 0:n], 1.0, k(hc), ALU.add, ALU.mult)
            P.act(E_[:, hc, 0:n], CUM[:, hc, 0:n], AF.Exp, scale=-C0)
            P.tt("dve", rt[:, hc, 0:n], r(hc), E_[:, hc, 0:n], ALU.mult)
            for j in range(n // 64):
                P.copy("pool", wc[:, hc, j:j + 1], E_[:, hc, j * 64 + 63:j * 64 + 64])
            P.act(E_[:, hc, 0:n], CUM[:, hc, 0:n], AF.Exp, scale=C0)
            P.tt("dve", kt[:, hc, 0:n], KF[:, hc, 0:n], E_[:, hc, 0:n], ALU.mult)
            P.tt("pool", A_[:, hc, 0:n], A_[:, hc, 0:n], KKN[:, hc, 0:n], ALU.mult)
            P.tt("dve", bt[:, hc, 0:n], A_[:, hc, 0:n], E_[:, hc, 0:n], ALU.mult)
            P.tt("dve", S_[:, hc, 0:n], CUM[:, hc, 0:n], S_[:, hc, 0:n], ALU.subtract)
            P.act(E_[:, hc, 0:n], S_[:, hc, 0:n], AF.Exp, scale=-C0)
            P.stt("dve", at[:, hc, 0:n], KKN[:, hc, 0:n], -1.0, E_[:, hc, 0:n], ALU.mult, ALU.mult)
            P.stt("dve", S_[:, hc, 0:n], r(hc), pv(l, "rk", hc), KF[:, hc, 0:n], ALU.mult, ALU.mult)
            ps = PS()
            P.mm(ps[:, 0:n], cst["bd1"], S_[:, hc, 0:n])
            P.tt("dve", CUM[:, hc, 0:n], ps[:, 0:n], v(hc), ALU.mult)
            ps = PS()
            P.mm(ps[:, 0:n], gup[l][:, cs], sg)
            P.copy("act", A_[:, hc, 0:n], ps[:, 0:n])
        BON, G_ = CUM, A_
        if stage == 3.2:
            return ha[7]
        for (src, dst) in ((vbf, tokm[0]), (bt, tokm[1]), (kt, tokm[2])):
            for b in range(nb):
                ps = PS()
                for hc in range(2):
                    P.mm(ps[:, hc * 128:(hc + 1) * 128], src[:, hc, b * 128:(b + 1) * 128], identb)
                P.copy("act", dst[:, b, :, :], view(ps, 256, 2))
        VT, BT, KT = tokm
        if stage == 3.3:
            return ha[7]
        for b in range(nb):
            bs = slice(b * 128, (b + 1) * 128)
            hr = lambda hh: slice((hh % 2) * 64, (hh % 2) * 64 + 64)

            def scores(lhs, rhs):
                ps = PS()
                for hh in range(4):
                    P.mm(ps[:, hh * 128:(hh + 1) * 128], lhs[hr(hh), hh // 2, bs], rhs[hr(hh), hh // 2, bs])
                return view(ps, 512, 4)
            X, Y, IX, Z = nx[0], ny[0], nix[0], nz[0]
            p1 = scores(bt, at)
            P.tt("dve", Y, p1, bc(cst["mu64"]), ALU.mult)
            if stage == 3.41:
                continue
            p2 = scores(at, bt)
            P.tt("dve", X, p2, bc(cst["ml64"]), ALU.mult)
            P.tt("pool", Z, Y, bc(identb), ALU.add)
            p3 = scores(kt, at)
            P.tt("dve", sc["ak"][:, b], p3, bc(cst["mu64"]), ALU.mult)
            p4 = scores(bt, rt)
            P.tt("dve", sc["rb"][:, b], p4, bc(cst["mui64"]), ALU.mult)
            p5 = scores(kt, rt)
            P.tt("dve", sc["rk"][:, b], p5, bc(cst["mui64"]), ALU.mult)
            if stage == 3.42:
                continue
            cur = 0
            for lev in range(5 if stage != 3.43 else 1):
                nxt = 1 - cur
                Xc, Yc, Zc = nx[cur], ny[cur], nz[cur]
                Xn, Yn, IXn, Zn = nx[nxt], ny[nxt], nix[nxt], nz[nxt]
                ps = PS()
                for hh in range(4):
                    P.mm(ps[:, hh * 128:(hh + 1) * 128], Yc[:, hh, :], Xc[:, hh, :])
                px = view(ps, 512, 4)
                if lev < 4:
                    P.copy("act", Xn, px)
                P.tt("dve", IXn, px, bc(cst["ident"]), ALU.add)
                if lev < 4:
                    ps = PS()
                    for hh in range(4):
                        P.mm(ps[:, hh * 128:(hh + 1) * 128], Xc[:, hh, :], Yc[:, hh, :])
                    py = view(ps, 512, 4)
                    P.copy("act", Yn, py)
                ps = PS()
                for hh in range(4):
                    P.mm(ps[:, hh * 128:(hh + 1) * 128], IXn[:, hh, :], Zc[:, hh, :])
                pz = view(ps, 512, 4)
                if lev < 4:
                    P.copy("dve", Zn, pz)
                else:
                    P.copy("dve", sc["z"][:, b], pz)
                cur = nxt
        if 3.4 <= stage < 3.45:
            return ha[7]
        ST, STb = stA[l], stAb[l]
        OT = psacc[0]
        for ci in range(n // 64):
            b, c = ci // 2, ci % 2
            cr = slice(c * 64, c * 64 + 64)
            tk = slice(ci * 64, ci * 64 + 64)
            hr = lambda hh: slice((hh % 2) * 64, (hh % 2) * 64 + 64)
            ps1 = PS()
            for hh in range(4):
                hc = hh // 2
                P.mm(ps1[cr, hh * 64:(hh + 1) * 64], at[hr(hh), hc, tk], STb[hr(hh), hc, :], start=True, stop=False)
                P.mm(ps1[cr, hh * 64:(hh + 1) * 64], sc["ak"][cr, b, hh, cr], VT[cr, b, hc, hr(hh)], start=False, stop=True)
            P.copy("dve", rhsb[cr], Reg(ps1.ap[cr, 0:256].rearrange("p (a c) -> p a c", a=4), ps1.buf))
            if stage == 3.45:
                continue
            ps2 = PS()
            for hh in range(4):
                P.mm(ps2[cr, hh * 64:(hh + 1) * 64], sc["z"][cr, b, hh, cr], rhsb[cr, hh, :])
            P.copy("act", ub[cr], Reg(ps2.ap[cr, 0:256].rearrange("p (a c) -> p a c", a=4), ps2.buf))
            if stage == 3.46:
                continue
            for hh in range(4):
                hc = hh // 2
                oo = OT[hr(hh), hc * 256 + ci * 64:hc * 256 + ci * 64 + 64]
                P.mm(oo, STb[hr(hh), hc, :], rt[hr(hh), hc, tk], start=True, stop=False)
                P.mm(oo, ub[cr, hh, :], sc["rb"][cr, b, hh, cr], start=False, stop=False)
                P.mm(oo, VT[cr, b, hc, hr(hh)], sc["rk"][cr, b, hh, cr], start=False, stop=True)
            if stage == 3.47:
                continue
            ps3 = PS()
            for hh in range(4):
                hc = hh // 2
                so = ps3[hr(hh), hc * 64:(hc + 1) * 64]
                P.mm(so, BT[cr, b, hc, hr(hh)], ub[cr, hh, :], start=True, stop=False)
                P.mm(so, KT[cr, b, hc, hr(hh)], VT[cr, b, hc, hr(hh)], start=False, stop=True)
            for hc in range(2):
                P.tt("dve", ST[:, hc, :], ST[:, hc, :], ps3[:, hc * 64:(hc + 1) * 64], ALU.add)
                P.ts("dve", ST[:, hc, :], ST[:, hc, :], wc[:, hc, ci:ci + 1], None, ALU.mult)
                P.copy("act", STb[:, hc, :], ST[:, hc, :])
        if 3.45 <= stage <= 3.5:
            return ha[7]
        yA = ha[7]
        for hc in range(2):
            o = fa[0][:, hc, 0:n]
            P.copy("act", o, OT[:, hc * 256:hc * 256 + n])
            sq = fa[3][:, hc, 0:n]
            P.act(sq, OT[:, hc * 256:hc * 256 + n], AF.Square)
            pm = PS()
            P.mm(pm[:, 0:n], cst["bd64"], o)
            pq = PS()
            P.mm(pq[:, 0:n], cst["bd64"], sq)
            d = fa[4][:, hc, 0:n]
            P.tt("dve", d, o, pm[:, 0:n], ALU.subtract)
            m2 = fa[5][:, hc, 0:n]
            P.act(m2, pm[:, 0:n], AF.Square)
            P.tt("dve", m2, pq[:, 0:n], m2, ALU.subtract)
            P.ts("dve", m2, m2, 0.0, None, ALU.max)
            P.act(m2, m2, AF.Sqrt, bias=kcol[:, 3:4])
            P.recip(m2, m2)
            P.tt("dve", d, d, m2, ALU.mult)
            P.ts("dve", d, d, pv(l, "lnw", hc), pv(l, "lnb", hc), ALU.mult, ALU.add)
            P.tt("pool", d, d, BON[:, hc, 0:n], ALU.add)
            P.tt("dve", yA[:, hc, 0:n], d, G_[:, hc, 0:n], ALU.mult)
        return yA

    def branch_B(l, n, wBC, first_tile):
        u = ubp
        for c in range(2):
            P.copy("pool", u[:, c, 0:16], car_p[l][:, c, :])
            ps = proj(wBC, range(8), c * 128, lambda k: zb[:, k, 0:n], n)
            P.copy("act", u[:, c, 16:16 + n], ps[:, 0:n])
            P.copy("pool", car_p[l][:, c, :], u[:, c, n:n + 16])
        W = 16 + n
        s2, s4, s8 = pl
        yB = ha[0]
        for c in range(2):
            P.tt("dve", s2[:, c, 1:W], u[:, c, 1:W], u[:, c, 0:W - 1], ALU.add)
            P.tt("dve", s4[:, c, 3:W], s2[:, c, 3:W], s2[:, c, 1:W - 2], ALU.add)
            if c == 0:
                lo, hi = s2, s4
            else:
                P.tt("dve", s8[:, c, 7:W], s4[:, c, 7:W], s4[:, c, 3:W - 4], ALU.add)
                P.tt("dve", s2[:, c, 15:W], s8[:, c, 15:W], s8[:, c, 7:W - 8], ALU.add)
                lo, hi = s8, s2
            pool_ = s4 if c == 1 else s8
            for (src, pr) in ((lo, slice(0, 64)), (hi, slice(64, 128))):
                P.stt("dve", pool_[pr, c, 16:W], src[pr, c, 16:W], cst["poolt"][pr, c, 15:16], u[pr, c, 16:W],
                      ALU.mult, ALU.subtract)
                if first_tile:
                    fs = slice(16 + PADL, 32 + PADL)
                    P.tt("dve", pool_[pr, c, fs], src[pr, c, fs], cst["poolt"][pr, c, :], ALU.mult)
                    P.tt("dve", pool_[pr, c, fs], pool_[pr, c, fs], u[pr, c, fs], ALU.subtract)
            pb = ha[1][:, c, 0:n]
            P.copy("act", pb, pool_[:, c, 16:W])
            ps = PS()
            P.mm(ps[:, 0:n], bmix[l][:, c, :], pb)
            P.ts("dve", yB[:, c, 0:n], ps[:, 0:n], pv(l, "bscale", c), None, ALU.mult)
        return yB

    def branch_C(l, n, wBC, wKV, gb0):
        nb = n // 128
        for c in range(4):
            ps = proj(wBC, range(8), 256 + c * 128, lambda k: zb[:, k, 0:n], n)
            P.copy("act" if c % 2 else "dve", hq[:, c, 0:n], ps[:, 0:n])
        for g in range(2):
            P.copy("pool", kd[:, g, 0:128], car_k[l][:, g, :])
            ps = proj(wKV, range(8), g * 128, lambda k: zb[:, k, 0:n], n)
            P.copy("act", kd[:, g, 128:128 + n], ps[:, 0:n])
            P.copy("pool", car_k[l][:, g, :], kd[:, g, n:n + 128])
        P.copy("pool", vtk[:, 0, :], car_v[l])
        for b in range(nb):
            ps = PS()
            for k_ in range(8):
                P.mm(ps[:, 0:128], zb[:, k_, b * 128:(b + 1) * 128], wKV[:, k_, 256:384], start=(k_ == 0), stop=(k_ == 7))
            P.copy("act", vtk[:, b + 1, :], ps[:, 0:128])
        P.copy("pool", car_v[l], vtk[:, nb, :])
        for b in range(nb):
            qs = slice(b * 128, (b + 1) * 128)
            gb = gb0 + b
            for hp in range(4):
                ps = PS()
                for hh2 in range(2):
                    h = hp * 2 + hh2
                    g = h // 4
                    rows = slice((h % 2) * 64, (h % 2) * 64 + 64)
                    for part in range(2):
                        P.mm(ps[:, hh2 * 256 + part * 128:hh2 * 256 + part * 128 + 128],
                             kd[rows, g, (b + part) * 128:(b + part + 1) * 128], hq[rows, h // 2, qs])
                pv_ = view(ps, 512, 2)
                sT = sTs[hp % 2]
                P.stt("dve", sT, pv_, 0.125, cst["swab"][:, hp * 2:hp * 2 + 2, :], ALU.mult, ALU.add)
                if gb == 0:
                    P.memset("pool", sT[:, :, 0:128], NEG)
                    if PADL:
                        P.memset("pool", sT[0:PADL, :, 128:256], NEG)
                elif gb == 1 and PADL:
                    P.memset("pool", sT[0:PADL, :, 0:128], NEG)
                P.act(pT[:, hp * 2:hp * 2 + 2, :], sT, AF.Exp)
            for hc in range(4):
                po = PS()
                for hh2 in range(2):
                    h = hc * 2 + hh2
                    g = h // 4
                    rows = slice(hh2 * 64, hh2 * 64 + 64)
                    for part in range(2):
                        P.mm(po[rows, 0:128], vtk[:, b + part, g * 64:(g + 1) * 64], pT[:, h, part * 128:(part + 1) * 128],
                             start=(part == 0), stop=(part == 1))
                    for part in range(2):
                        P.mm(po[rows, 128:256], onesb[:, 0:64], pT[:, h, part * 128:(part + 1) * 128],
                             start=(part == 0), stop=(part == 1))
                den = fa[0][:, 0, 0:128]
                for hh2 in range(2):
                    h = hc * 2 + hh2
                    rows = slice(hh2 * 64, hh2 * 64 + 64)
                    P.ts("dve", den[rows], po[rows, 128:256], esink[rows, l, h:h + 1], None, ALU.add)
                P.recip(den, den)
                P.tt("dve", ycT[:, hc, qs], po[:, 0:128], den, ALU.mult)
        return ycT

    def branch_D(l, n, first_tile):
        nb = n // 128
        wD = load_w(win_s[l].ap[:, OFF_D:OFF_D + 1024], 8, 1024, win_s[l].buf)
        u = u8
        for c in range(8):
            ps = proj(wD, range(8), c * 128, lambda k: zb[:, k, 0:n], n)
            if c < 2:
                P.act(u[:, c, 0:n], ps[:, 0:n], AF.Silu)
            elif c < 4:
                P.copy("dve", u[:, c, 0:n], ps[:, 0:n])
            elif c < 6:
                P.copy("act", ha[0][:, c - 4, 0:n], ps[:, 0:n])
            else:
                P.act(u[:, c, 0:n], ps[:, 0:n], AF.Silu)
        vb = ha[0]
        lb = lambda hc: drv[:, DV_L * l + 8 + hc:DV_L * l + 9 + hc]
        omlb = lambda hc: drv[:, DV_L * l + 10 + hc:DV_L * l + 11 + hc]
        SG, LF, B_, E_, KK = fa[0], fa[1], fa[2], fa[3], fa[4]
        qt, kt = ha[1], ha[2]
        for hc in range(2):
            fpre = u[:, 2 + hc, 0:n]
            P.act(SG[:, hc, 0:n], fpre, AF.Sigmoid)
            P.ts("dve", LF[:, hc, 0:n], SG[:, hc, 0:n], omlb(hc), lb(hc), ALU.mult, ALU.add)
            P.ts("dve", LF[:, hc, 0:n], LF[:, hc, 0:n], 1e-30, None, ALU.max)
            P.act(LF[:, hc, 0:n], LF[:, hc, 0:n], AF.Ln)
            P.scan(B_[:, hc, 0:n], cst["rs32"][:, 0:n], LF[:, hc, 0:n])
            P.act(KK[:, hc, 0:n], fpre, AF.Sigmoid, scale=-1.0)
            P.ts("dve", KK[:, hc, 0:n], KK[:, hc, 0:n], omlb(hc), None, ALU.mult)
            P.act(E_[:, hc, 0:n], B_[:, hc, 0:n], AF.Exp)
            P.tt("dve", qt[:, hc, 0:n], u[:, hc, 0:n], E_[:, hc, 0:n], ALU.mult)
            for j in range(n // 32):
                P.copy("pool", wc[:, hc, j:j + 1], E_[:, hc, j * 32 + 31:j * 32 + 32])
            P.act(E_[:, hc, 0:n], B_[:, hc, 0:n], AF.Exp, scale=-1.0)
            P.tt("dve", kt[:, hc, 0:n], KK[:, hc, 0:n], E_[:, hc, 0:n], ALU.mult)
        for b in range(nb):
            ps = PS()
            for hc in range(2):
                P.mm(ps[:, hc * 128:(hc + 1) * 128], vb[:, hc, b * 128:(b + 1) * 128], identb)
            P.copy("act", tokm[0][:, b, :, :], view(ps, 256, 2))
            ps = PS()
            for hc in range(2):
                P.mm(ps[:, hc * 128:(hc + 1) * 128], kt[:, hc, b * 128:(b + 1) * 128], identb)
            P.ts("dve", tokm[1][:, b, :, :], view(ps, 256, 2), cst["hm32"][:, 0:1], None, ALU.mult)
            P.ts("dve", tokm[2][:, b, :, :], view(ps, 256, 2), cst["hm32"][:, 1:2], None, ALU.mult)
        VT = tokm[0]
        hr = lambda hh: slice((hh % 2) * 64, (hh % 2) * 64 + 64)
        for b in range(nb):
            bs = slice(b * 128, (b + 1) * 128)
            ps = PS()
            for hh in range(4):
                P.mm(ps[:, hh * 128:(hh + 1) * 128], kt[hr(hh), hh // 2, bs], qt[hr(hh), hh // 2, bs])
            P.tt("dve", sc["ak"][:, b], view(ps, 512, 4), bc(cst["mui32"]), ALU.mult)
        ST, STb = stD[l], stDb[l]
        OT = psacc[1]
        for ci in range(n // 32):
            b, c = ci // 4, ci % 4
            pr = slice((c // 2) * 64, (c // 2) * 64 + 64)
            cc = slice(c * 32, c * 32 + 32)
            tk = slice(ci * 32, ci * 32 + 32)
            KT = tokm[1 + (c % 2)]
            for hh in range(4):
                hc = hh // 2
                oo = OT[hr(hh), hc * 256 + ci * 32:hc * 256 + ci * 32 + 32]
                P.mm(oo, STb[hr(hh), hc, :], qt[hr(hh), hc, tk], start=True, stop=False)
                P.mm(oo, VT[pr, b, hc, hr(hh)], sc["ak"][pr, b, hh, cc], start=False, stop=True)
            ps3 = PS()
            for hh in range(4):
                hc = hh // 2
                P.mm(ps3[hr(hh), hc * 64:(hc + 1) * 64], KT[pr, b, hc, hr(hh)], VT[pr, b, hc, hr(hh)])
            for hc in range(2):
                P.tt("dve", ST[:, hc, :], ST[:, hc, :], ps3[:, hc * 64:(hc + 1) * 64], ALU.add)
                P.ts("dve", ST[:, hc, :], ST[:, hc, :], wc[:, hc, ci:ci + 1], None, ALU.mult)
                P.copy("act", STb[:, hc, :], ST[:, hc, :])
        yD = ha[7]
        for hc in range(2):
            o = fa[0][:, hc, 0:n]
            P.copy("act", o, OT[:, hc * 256:hc * 256 + n])
            sq = fa[1][:, hc, 0:n]
            P.act(sq, OT[:, hc * 256:hc * 256 + n], AF.Square)
            pq = PS()
            P.mm(pq[:, 0:n], cst["bd64"], sq)
            rs = fa[2][:, hc, 0:n]
            P.act(rs, pq[:, 0:n], AF.Sqrt, bias=kcol[:, 2:3])
            P.recip(rs, rs)
            P.tt("dve", o, o, rs, ALU.mult)
            P.stt("dve", yD[:, hc, 0:n], o, pv(l, "dnorm", hc), u[:, 6 + hc, 0:n], ALU.mult, ALU.mult)
        return yD

    onesb = P.tile("onesb", [128, 64], BF16)
    P.memset("dve", onesb, 1.0)

    def layer(l, n, first_tile, c0, gb0):
        layer_(l, n, first_tile, c0, gb0)
        dump("h2", hT, first_tile and l == 0)

    def layer_(l, n, first_tile, c0, gb0):
        dcond = first_tile and l == 0
        dump("h0", hT, dcond)
        rmsnorm(PVL * l + 0, n, zb)
        dump("z", zb, dcond)
        if stage == 2:
            return
        yA = branch_A(l, n, first_tile)
        dump("yA", yA, dcond)
        if 3 <= stage < 4:
            return
        gw = load_w(win_s[l].ap[:, 0:1024], 8, 1024, win_s[l].buf)
        gate_merge(l, 0, n, gw, lambda k: yA[:, k, 0:n], [0, 1], True)
        wBC = load_w(win_s[l].ap[:, OFF_B:OFF_B + 1024], 8, 1024, win_s[l].buf)
        wKV = wb[wbi[0] % NWB]
        wbi[0] += 1
        kv_src = win_s[l].ap
        for g in range(2):
            for dup in range(2):
                P.dma("sp", wKV[:, :, g * 128 + dup * 64:g * 128 + dup * 64 + 64],
                      Reg(kv_src[:, OFF_C + 512 + g * 64:OFF_C + 512 + g * 64 + 64].rearrange("(k p) c -> p k c", p=128), win_s[l].buf))
        P.dma("sp", wKV[:, :, 256:384], Reg(kv_src[:, OFF_C + 640:OFF_C + 768].rearrange("(k p) c -> p k c", p=128), win_s[l].buf))
        if stage == 4:
            return
        yB = branch_B(l, n, wBC, first_tile)
        dump("yB", yB, dcond)
        if stage == 5:
            return
        yC = branch_C(l, n, wBC, wKV, gb0)
        dump("yC", yC, dcond)
        if stage == 6:
            return
        gw = load_w(win_s[l].ap[:, 1024:2048], 8, 1024, win_s[l].buf)
        gate_merge(l, 1, n, gw, lambda k: yB[:, k, 0:n], [2, 3], False)
        gw = load_w(win_s[l].ap[:, 2048:3072], 8, 1024, win_s[l].buf)
        gate_merge(l, 2, n, gw, lambda k: yC[:, k, 0:n], [4, 5, 6, 7], False)
        yD = branch_D(l, n, first_tile)
        dump("yD", yD, dcond)
        if stage == 7:
            return
        gw = load_w(win_s[l].ap[:, 3072:4096], 8, 1024, win_s[l].buf)
        gate_merge(l, 3, n, gw, lambda k: yD[:, k, 0:n], [8, 9], False)
        dump("mg", mg, dcond)
        for c in range(8):
            P.copy("act" if c % 2 else "pool", mgb[:, c, 0:n], mg[:, c, 0:n])
        wo = load_w(wout_s[l].ap, 8, 1024, win_s[l].buf)
        for c in range(8):
            ps = proj(wo, range(8), c * 128, lambda k: mgb[:, k, 0:n], n)
            P.tt("dve", hT[:, c, c0:n], hT[:, c, c0:n], ps[:, c0:n], ALU.add)
        dump("h1", hT, dcond)
        if stage == 8:
            return
        rmsnorm(PVL * l + 8, n, zb)
        for j0 in range(0, 22, 4):
            nj = min(4, 22 - j0)
            w = wb[wbi[0] % NWB]
            wbi[0] += 1
            P.dma("sp", w[:, :, 0:nj * 128], Reg(wup_s[l].ap[:, j0 * 128:(j0 + nj) * 128].rearrange("(k p) c -> p k c", p=128), win_s[l].buf))
            P.dma("sp", w[:, :, 512:512 + nj * 128], Reg(wup_s[l].ap[:, DFF + j0 * 128:DFF + (j0 + nj) * 128].rearrange("(k p) c -> p k c", p=128), win_s[l].buf))
            for j in range(nj):
                pg = proj(w, range(8), j * 128, lambda k: zb[:, k, 0:n], n)
                sg = fa[6][:, j % 2, 0:n]
                P.act(sg, pg[:, 0:n], AF.Silu)
                pu = proj(w, range(8), 512 + j * 128, lambda k: zb[:, k, 0:n], n)
                P.tt("dve", actb[:, j0 + j, 0:n], pu[:, 0:n], sg, ALU.mult)
        for g0 in range(0, 22, 8):
            ng = min(8, 22 - g0)
            w = load_w(wdn_s[l].ap[g0 * 128:(g0 + ng) * 128, :], ng, 1024, win_s[l].buf)
            for c in range(8):
                ps = proj(w, range(ng), c * 128, lambda k: actb[:, g0 + k, 0:n], n)
                P.tt("dve", hT[:, c, 0:n], hT[:, c, 0:n], ps[:, 0:n], ALU.add)

    outbufs = []
    for s in range(nseq if stage >= 1 else 0):
        for l in range(depth):
            P.memset("pool", car_sh[l], 0.0)
            P.memset("pool", stA[l], 0.0)
            P.memset("pool", stAb[l], 0.0)
            P.memset("pool", stD[l], 0.0)
            P.memset("pool", stDb[l], 0.0)
            P.memset("pool", car_p[l], 0.0)
            P.memset("pool", car_k[l], 0.0)
            P.memset("pool", car_v[l], 0.0)
        for ti, (p0, n) in enumerate(tiles):
            first_tile = ti == 0
            nb = n // 128
            for b in range(nb):
                gb = p0 // 128 + b
                xt = xs[xsi[0] % 2]
                xsi[0] += 1
                if gb == 0:
                    P.memset("pool", xt, 0.0)
                    P.dma("sp", xt[PADL:128, :], DR(meta_d))
                else:
                    P.dma("sp", xt, DR(x_d[s, (gb - 1) * 128:gb * 128, :]))
                for c in range(8):
                    ps = PS()
                    P.mm(ps[:, 0:128], xt[:, c * 128:(c + 1) * 128], cst["ident"])
                    P.copy("act" if c % 2 else "dve", hT[:, c, b * 128:(b + 1) * 128], ps[:, 0:128])
            c0 = PADL if first_tile else 0
            for l in range(depth if stage >= 2 else 0):
                layer(l, n, first_tile, c0, p0 // 128)
            if dbg:
                pass
            rmsnorm(GO, n, mg)
            for b in range(nb):
                gb = p0 // 128 + b
                if gb == 0:
                    continue
                ot = os_[osi[0] % 2]
                osi[0] += 1
                for c in range(8):
                    ps = PS()
                    P.mm(ps[:, 0:128], mg[:, c, b * 128:(b + 1) * 128], cst["ident"])
                    P.copy("act" if c % 2 else "dve", ot[:, c * 128:(c + 1) * 128], ps[:, 0:128])
                P.dma("sp", DR(out_d[s, (gb - 1) * 128:gb * 128, :]), ot, owner=ot.buf)
                outbufs.append(ot.buf)
    P.wait_all("sp", list(set(outbufs)))
    P.emit()
    P.stack.close()
    nc._dbg_names = list(dbg_outs.keys())
    return nc, P


_CACHE = {}


def run(inp, nseq_total, seq, depth, ncores, NT=256, dbg=None):
    nseq = nseq_total // ncores
    key = (nseq, seq, depth, NT, dbg is not None)
    if key not in _CACHE:
        _CACHE[key] = build_nc(nseq, seq, depth, NT, dbg is not None)[0]
    nc = _CACHE[key]
    consts = make_consts()
    small = pack_small(inp, depth)
    f32 = lambda a: np.ascontiguousarray(np.asarray(a, np.float32))
    shared = {"meta": f32(inp["meta"]), "w_in": f32(inp["w_in"]), "w_branch": f32(inp["w_branch"]),
              "w_out": f32(inp["w_out"]), "w_ffn_up": f32(inp["w_ffn_up"]), "w_ffn_down": f32(inp["w_ffn_down"])}
    shared.update(small)
    for k in CONST_ORDER:
        shared["c_" + k] = consts[k]
    x = f32(inp["x"])
    in_maps = []
    for c in range(ncores):
        m = dict(shared)
        m["x"] = np.ascontiguousarray(x[c * nseq:(c + 1) * nseq])
        in_maps.append(m)
    res = run_bass_kernel_spmd(nc, in_maps, core_ids=list(range(ncores)))
    if dbg is not None:
        for k in nc._dbg_names:
            dbg[k] = np.asarray(res.results[0]["dbg_" + k])
    return np.concatenate([r["out"] for r in res.results], axis=0).astype(np.float32)


def kernel(**inputs):
    x = inputs["x"]
    depth = inputs["w_in"].shape[0]
    return run(inputs, x.shape[0], x.shape[1], depth, NCORES)
```

```python
import contextlib
import numpy as np
import concourse.bass as bass
import concourse.mybir as mybir
from concourse.bass_utils import run_bass_kernel_spmd

F32 = mybir.dt.float32
BF16 = mybir.dt.bfloat16
AF = mybir.ActivationFunctionType
ALU = mybir.AluOpType

D_MODEL = 1024
N_META = 16
DFF = 2816
NCORES = 8
C0 = 0.6065306597126334
NEG = -30000.0
B_WINDOWS = (2, 4, 8, 16)
OFF_A, OFF_B, OFF_C, OFF_D = 4096, 5120, 5376, 6144


class Buf:
    __slots__ = ("name", "w", "r", "dsem", "dval", "excl")

    def __init__(self, name):
        self.name = name
        self.w = None
        self.r = []
        self.dsem = None
        self.dval = 0
        self.excl = False


class Reg:
    __slots__ = ("ap", "buf")

    def __init__(self, ap, buf):
        self.ap = ap
        self.buf = buf

    def __getitem__(self, idx):
        return Reg(self.ap[idx], self.buf)


def R(x):
    return x.ap if isinstance(x, Reg) else x


class Prog:
    ENG = ("pe", "act", "dve", "pool", "sp")

    def __init__(self, nc):
        self.nc = nc
        self.stack = contextlib.ExitStack()
        self.ops = {e: [] for e in self.ENG}
        self.cnt = {e: 0 for e in self.ENG}
        self.seen = {e: {} for e in self.ENG}
        self.sems = {}
        self.nsem = 0
        for e in ("pe", "act", "dve", "pool"):
            self.sems[e] = self.stack.enter_context(nc.semaphore("s_" + e))
        self.nops = 0

    def tile(self, name, shape, dtype):
        t = self.stack.enter_context(self.nc.sbuf_tensor(name, list(shape), dtype))
        return Reg(t[:], Buf(name))

    def psum(self, name, shape, dtype):
        t = self.stack.enter_context(self.nc.psum_tensor(name, list(shape), dtype))
        b = Buf(name)
        b.excl = True
        return Reg(t[:], b)

    def dsem(self, buf):
        if buf.dsem is None:
            buf.dsem = self.stack.enter_context(self.nc.semaphore("d%d" % self.nsem))
            self.nsem += 1
        return buf.dsem

    def _need(self, eng, reads, writes):
        need = {}

        def add(tok, same_ok):
            if tok is None:
                return
            k, v = tok
            if k == eng and (eng == "pe"):
                return
            if need.get(k, 0) < v:
                need[k] = v

        for b in reads:
            add(b.w, False)
        for b in writes:
            add(b.w, True)
            for t in b.r:
                add(t, True)
        seen = self.seen[eng]
        waits = []
        for k, v in need.items():
            if seen.get(k, 0) >= v:
                continue
            seen[k] = v
            waits.append((k, v))
        return waits

    def _commit(self, tok, reads, writes):
        for b in writes:
            b.w = tok
            b.r = []
        for b in reads:
            b.r.append(tok)
            if len(b.r) > 16:
                d = {}
                for k, v in b.r:
                    if d.get(k, 0) < v:
                        d[k] = v
                b.r = list(d.items())

    rec = None

    def nop(self):
        if self.rec is not None:
            self.rec.append(None)

    def replay_zip(self, lists):
        n = max([len(x) for x in lists] + [0])
        for i in range(n):
            for x in lists:
                if i < len(x) and x[i] is not None:
                    it = x[i]
                    if it[0] == "op":
                        self.op(*it[1:])
                    else:
                        self.dma(*it[1:-1], **it[-1])

    def op(self, eng, emit, reads, writes, rg=None):
        if self.rec is not None:
            self.rec.append(("op", eng, emit, reads, writes, rg))
            return
        if rg is not None:
            last = getattr(self, "last_rg", None)
            if last is not None and last != rg and self.cnt["pe"] > 0:
                v = self.cnt["pe"]
                if self.seen["pe"].get("pe", 0) < v:
                    self.seen["pe"]["pe"] = v
                    self.ops["pe"].append(([("pe", v)], None, None))
            self.last_rg = rg
        reads = [r.buf for r in reads if isinstance(r, Reg) and r.buf is not None]
        writes = [r.buf for r in writes if isinstance(r, Reg) and r.buf is not None]
        ex = [b for b in reads if b.excl]
        if ex:
            reads = [b for b in reads if not b.excl]
            writes = writes + ex
        waits = self._need(eng, reads, writes)
        self.cnt[eng] += 1
        tok = (eng, self.cnt[eng])
        self.ops[eng].append((waits, emit, (eng, 1)))
        self._commit(tok, reads, writes)
        self.nops += 1

    def dma(self, q, out, in_, owner=None, **kw):
        if self.rec is not None:
            self.rec.append(("dma", q, out, in_, owner, kw))
            return
        if owner is None:
            owner = out.buf if out.buf is not None else in_.buf
        reads = [in_.buf] if in_.buf is not None else []
        writes = [out.buf] if out.buf is not None else []
        waits = self._need(q, reads, writes)
        self.dsem(owner)
        owner.dval += 16
        tok = (owner, owner.dval)
        oap, iap = out.ap, in_.ap

        def emit(e):
            return e.dma_start(out=oap, in_=iap, **kw)
        self.ops[q].append((waits, emit, (owner, 16)))
        self._commit(tok, reads, writes)
        self.nops += 1

    def wait_all(self, eng, bufs):
        need = {}
        for b in bufs:
            for tok in ([b.w] if b.w else []) + list(b.r):
                k, v = tok
                if need.get(k, 0) < v:
                    need[k] = v
        self.ops[eng].append((list(need.items()), None, None))

    def _semh(self, k):
        return self.sems[k] if isinstance(k, str) else k.dsem

    def emit(self):
        with self.nc.Block() as block:
            def run(name):
                def f(e):
                    for waits, emit, inc in self.ops[name]:
                        for k, v in waits:
                            e.wait_ge(self._semh(k), v)
                        if emit is None:
                            continue
                        emit(e).then_inc(self._semh(inc[0]), inc[1])
                return f
            block.tensor(run("pe"))
            block.scalar(run("act"))
            block.vector(run("dve"))
            block.gpsimd(run("pool"))
            block.sync(run("sp"))

    def mm(self, out, lhsT, rhs, start=True, stop=True):
        o, l, r = out.ap, lhsT.ap, rhs.ap
        sp = l.start_partition
        sp = sp() if callable(sp) else sp
        rg = (int(sp), int(l.shape[0]))
        self.op("pe", lambda e: e.matmul(o, l, r, start=start, stop=stop), [lhsT, rhs], [out], rg=rg)

    def tt(self, eng, out, in0, in1, op):
        o, a, b = out.ap, in0.ap, in1.ap
        self.op(eng, lambda e: e.tensor_tensor(out=o, in0=a, in1=b, op=op), [in0, in1], [out])

    def ts(self, eng, out, in0, s1, s2, op0, op1=None):
        o, a = out.ap, in0.ap
        rs = [in0] + [s for s in (s1, s2) if isinstance(s, Reg)]
        if op1 is None:
            self.op(eng, lambda e: e.tensor_scalar(out=o, in0=a, scalar1=R(s1), scalar2=None, op0=op0), rs, [out])
        else:
            self.op(eng, lambda e: e.tensor_scalar(out=o, in0=a, scalar1=R(s1), scalar2=R(s2), op0=op0, op1=op1), rs, [out])

    def stt(self, eng, out, in0, sc, in1, op0, op1):
        o, a, b = out.ap, in0.ap, in1.ap
        rs = [in0, in1] + ([sc] if isinstance(sc, Reg) else [])
        self.op(eng, lambda e: e.scalar_tensor_tensor(out=o, in0=a, scalar=R(sc), in1=b, op0=op0, op1=op1), rs, [out])

    def act(self, out, in_, func, bias=None, scale=1.0):
        o, a = out.ap, in_.ap
        rs = [in_] + [s for s in (bias, scale) if isinstance(s, Reg)]
        if bias is None:
            self.op("act", lambda e: e.activation(out=o, in_=a, func=func, scale=R(scale)), rs, [out])
        else:
            self.op("act", lambda e: e.activation(out=o, in_=a, func=func, bias=R(bias), scale=R(scale)), rs, [out])

    def copy(self, eng, out, in_):
        o, a = out.ap, in_.ap
        if eng == "act":
            self.op("act", lambda e: e.copy(out=o, in_=a), [in_], [out])
        else:
            self.op(eng, lambda e: e.tensor_copy(out=o, in_=a), [in_], [out])

    def memset(self, eng, out, val):
        o = out.ap
        self.op(eng, lambda e: e.memset(o, val), [], [out])

    def recip(self, out, in_):
        o, a = out.ap, in_.ap
        self.op("dve", lambda e: e.reciprocal(out=o, in_=a), [in_], [out])

    def scan(self, out, d0, d1):
        o, a, b = out.ap, d0.ap, d1.ap
        self.op("dve", lambda e: e.tensor_tensor_scan(out=o, data0=a, data1=b, initial=0.0,
                                                      op0=ALU.mult, op1=ALU.add), [d0, d1], [out])


def pcol(v):
    v = np.asarray(v, np.float32)
    return np.ascontiguousarray(v.reshape(-1, 128).T)


def make_consts():
    c = {}
    c["ident"] = np.eye(128, dtype=np.float32)
    bd = np.zeros((128, 128), np.float32)
    bd[:64, :64] = 1.0
    bd[64:, 64:] = 1.0
    c["bd1"] = bd
    c["bd64"] = bd / 64.0
    c["onesm"] = np.full((128, 128), 1.0 / D_MODEL, np.float32)
    i = np.arange(128)[:, None]
    j = np.arange(128)[None, :]
    same64 = (i // 64) == (j // 64)
    same32 = (i // 32) == (j // 32)
    c["mu64"] = (same64 & (i < j)).astype(np.float32)
    c["mui64"] = (same64 & (i <= j)).astype(np.float32)
    c["ml64"] = (same64 & (j < i)).astype(np.float32)
    c["mui32"] = (same32 & (i <= j)).astype(np.float32)
    m64 = np.ones((128, 256), np.float32)
    m64[:, ::64] = 0.0
    m32 = np.ones((128, 256), np.float32)
    m32[:, ::32] = 0.0
    hm = np.zeros((128, 2), np.float32)
    hm[:, 0] = ((np.arange(128) // 32) % 2 == 0)
    hm[:, 1] = ((np.arange(128) // 32) % 2 == 1)
    c["hm32"] = hm
    c["rs64"] = m64
    c["rs32"] = m32
    si = np.arange(128)[:, None]
    qi = np.arange(128)[None, :]
    bias = np.zeros((128, 8, 256), np.float32)
    for h in range(8):
        sl = 2.0 ** (-8.0 * (h + 1) / 8)
        dprev = 128 + qi - si
        dcur = qi - si
        bp = np.where((dprev >= 0) & (dprev < 128), -sl * dprev, NEG)
        bc = np.where((dcur >= 0) & (dcur < 128), -sl * dcur, NEG)
        bias[:, h, 0:128] = bp
        bias[:, h, 128:256] = bc
    c["swab"] = bias
    pt = np.zeros((128, 2, 16), np.float32)
    for ch in range(2):
        for half in range(2):
            w = B_WINDOWS[ch * 2 + half]
            pt[half * 64:(half + 1) * 64, ch, :] = 1.0 / np.minimum(np.arange(16) + 1, w)
    c["poolt"] = pt
    return c


CONST_ORDER = ["ident", "bd1", "bd64", "onesm", "mu64", "mui64", "ml64", "mui32", "rs64", "rs32",
               "swab", "poolt", "hm32"]

PV = {"nmix": 0, "nffn": 8, "mu": 16, "w0": 24, "a0": 26, "kk": 28, "ka": 30, "rk": 32, "lnw": 34,
      "lnb": 36, "vres0": 38, "bscale": 40, "dnorm": 42}
PVL = 44


def pack_small(inp, depth):
    cols = []
    for l in range(depth):
        blk = np.zeros((128, PVL), np.float32)
        blk[:, 0:8] = pcol(inp["norm_mix"][l])
        blk[:, 8:16] = pcol(inp["norm_ffn"][l])
        blk[:, 16:24] = pcol(inp["a_mu"][l])
        for nm, key in (("w0", "a_w0"), ("a0", "a_a0"), ("kk", "a_kk"), ("ka", "a_ka"), ("rk", "a_rk"),
                        ("lnw", "a_ln_w"), ("lnb", "a_ln_b"), ("bscale", "b_scale"), ("dnorm", "d_norm")):
            blk[:, PV[nm]:PV[nm] + 2] = pcol(inp[key][l])
        if l >= 1:
            blk[:, PV["vres0"]:PV["vres0"] + 2] = pcol(inp["a_vres0"][l - 1])
        cols.append(blk)
    g = np.zeros((128, 8 + 2 * depth + 8 * depth), np.float32)
    g[:, 0:8] = pcol(inp["norm_final"])
    for l in range(depth):
        g[:, 8 + 2 * l:10 + 2 * l] = pcol(inp["d_lower_bounds"][l])
        g[:, 8 + 2 * depth + 8 * l:8 + 2 * depth + 8 * l + 8] = np.broadcast_to(
            np.asarray(inp["c_sinks"][l], np.float32)[None, :], (128, 8))
    pvec = np.concatenate(cols + [g], axis=1)
    lora = np.zeros((depth, 128, 256), np.float32)
    gup = np.zeros((depth, 128, 256), np.float32)
    bmix = np.zeros((depth, 128, 2, 128), np.float32)
    vdn = np.zeros((depth, 128, 2, 32), np.float32)
    vup = np.zeros((depth, 32, 256), np.float32)
    for l in range(depth):
        lora[l, :64] = inp["a_w_up"][l]
        lora[l, 64:] = inp["a_a_up"][l]
        gup[l] = inp["a_g_up"][l]
        for gi in range(4):
            ch, half = gi // 2, gi % 2
            bmix[l, half * 64:(half + 1) * 64, ch, half * 64:(half + 1) * 64] = inp["b_mix"][l, gi]
        if l >= 1:
            vdn[l] = np.asarray(inp["a_vres_down"][l - 1], np.float32).reshape(2, 128, 32).transpose(1, 0, 2)
            vup[l] = inp["a_vres_up"][l - 1]
    return {"pvec": np.ascontiguousarray(pvec), "lora": lora, "gup": gup, "bmix": bmix,
            "vdn": vdn, "vup": vup}


def build_nc(nseq, seq, depth, NT=128, dbg=None, stage=9, castmode=0):
    L = N_META + seq
    PADL = (-L) % 128
    LP = L + PADL
    assert PADL == 112 or True
    tiles = []
    p = 0
    while p < LP:
        n = min(NT, LP - p)
        tiles.append((p, n))
        p += n
    NB = LP // 128

    nc = bass.Bass("TRN2", target_bir_lowering=False)
    P = Prog(nc)
    DR = lambda ap: Reg(ap, None)

    def din(name, shape, dt=F32):
        return nc.dram_tensor(name, list(shape), dt, kind="ExternalInput").ap()

    x_d = din("x", [nseq, seq, D_MODEL])
    meta_d = din("meta", [N_META, D_MODEL])
    win_d = din("w_in", [depth, D_MODEL, 7168])
    wbr_d = din("w_branch", [depth, 1280, D_MODEL])
    wout_d = din("w_out", [depth, D_MODEL, D_MODEL])
    wup_d = din("w_ffn_up", [depth, D_MODEL, 2 * DFF])
    wdn_d = din("w_ffn_down", [depth, DFF, D_MODEL])
    NPV = PVL * depth + 8 + 2 * depth + 8 * depth
    pvec_d = din("pvec", [128, NPV])
    lora_d = din("lora", [depth, 128, 256])
    gup_d = din("gup", [depth, 128, 256])
    bmix_d = din("bmix", [depth, 128, 2, 128])
    vdn_d = din("vdn", [depth, 128, 2, 32])
    vup_d = din("vup", [depth, 32, 256])
    cshape = {"ident": [128, 128], "bd1": [128, 128], "bd64": [128, 128], "onesm": [128, 128],
              "mu64": [128, 128], "mui64": [128, 128], "ml64": [128, 128], "mui32": [128, 128],
              "rs64": [128, 256], "rs32": [128, 256], "swab": [128, 8, 256], "hm32": [128, 2],
              "poolt": [128, 2, 16]}
    c_d = {k: din("c_" + k, cshape[k]) for k in CONST_ORDER}
    out_d = nc.dram_tensor("out", [nseq, seq, D_MODEL], F32, kind="ExternalOutput").ap()

    def dscr(name, shape):
        return nc.dram_tensor(name, list(shape), BF16, kind="Internal").ap()
    win_s = [Reg(dscr("s_win%d" % l, [D_MODEL, 7168]), Buf("s_win%d" % l)) for l in range(depth)]
    wbr_s = [Reg(dscr("s_wbr%d" % l, [1280, D_MODEL]), win_s[l].buf) for l in range(depth)]
    wout_s = [Reg(dscr("s_wout%d" % l, [D_MODEL, D_MODEL]), win_s[l].buf) for l in range(depth)]
    wup_s = [Reg(dscr("s_wup%d" % l, [D_MODEL, 2 * DFF]), win_s[l].buf) for l in range(depth)]
    wdn_s = [Reg(dscr("s_wdn%d" % l, [DFF, D_MODEL]), win_s[l].buf) for l in range(depth)]

    cst = {k: P.tile("k_" + k, cshape[k], F32) for k in CONST_ORDER}
    cbuf = Buf("consts")
    for k in cst:
        cst[k].buf = cbuf
    pvec = P.tile("pvec_sb", [128, NPV], F32)
    pvec.buf = cbuf
    lora = [P.tile("lora%d" % l, [128, 256], BF16) for l in range(depth)]
    gup = [P.tile("gup%d" % l, [128, 256], BF16) for l in range(depth)]
    bmix = [P.tile("bmix%d" % l, [128, 2, 128], BF16) for l in range(depth)]
    vdn = [P.tile("vdn%d" % l, [128, 2, 32], BF16) for l in range(depth)]
    vup = [P.tile("vup%d" % l, [32, 256], BF16) for l in range(depth)]
    identb = P.tile("identb", [128, 128], BF16)
    mu64b = P.tile("mu64b", [128, 128], BF16)
    drv = P.tile("drv", [128, 64], F32)
    DV_L = 12
    esink = P.tile("esink", [128, depth, 8], F32)
    kcol = P.tile("kcol", [128, 8], F32)
    P.memset("dve", kcol[:, 0:1], 0.0)
    P.memset("dve", kcol[:, 1:2], 1.0)
    P.memset("dve", kcol[:, 2:3], 1e-6)
    P.memset("dve", kcol[:, 3:4], 64e-5)
    P.memset("dve", kcol[:, 4:5], 1e-24)
    P.memset("dve", kcol[:, 5:6], 1e-30)

    NWB = 3
    wb = [P.tile("wb%d" % i, [128, 8, 1024], BF16) for i in range(NWB)]
    wbi = [0]
    wlog = []
    outbufs = []
    dbg_outs = {}
    onesb = P.tile("onesb", [128, 64], BF16)
    P.memset("dve", onesb, 1.0)

    def make_stream(sid):
        T = lambda name, shape, dt: P.tile("%s_s%d" % (name, sid), shape, dt)
        TP = lambda name, shape, dt: P.psum("%s_s%d" % (name, sid), shape, dt)
        wcnt = [0]
        hT = T("hT", [128, 8, NT], F32)
        zb = T("zb", [128, 8, NT], BF16)
        mg = T("mg", [128, 8, NT], F32)
        mgb = zb
        vfirst = T("vfirst", [128, 2, NT], F32)
        u8 = T("u8", [128, 8, NT], F32)
        assert NT == 128
        xs = [Reg(u8.ap.rearrange("p a c -> p (a c)"), u8.buf)]
        xsi = [0]
        os_ = xs
        osi = xsi
        fa = [T("fa%d" % i, [128, 2, NT + 16], F32) for i in range(8)]
        ha = [T("ha%d" % i, [128, 2, NT], BF16) for i in range(8)]
        hq = T("hq", [128, 4, NT], BF16)
        actb = T("actb", [128, 22, NT], BF16)
        tokm = [T("tokm%d" % i, [128, NT // 128, 2, 128], BF16) for i in range(3)]
        NBK = NT // 128
        sc = {nm: T("sc_" + nm, [128, NBK, 4, 128], BF16) for nm in ("z", "ak", "rb", "rk")}
        nx = [T("nx%d" % i, [128, 4, 128], BF16) for i in range(2)]
        ny = [T("ny%d" % i, [128, 4, 128], BF16) for i in range(2)]
        nix = [T("nix%d" % i, [128, 4, 128], BF16) for i in range(2)]
        nz = [T("nz%d" % i, [128, 4, 128], BF16) for i in range(2)]
        rhsb = T("rhsb", [128, 4, 64], BF16)
        ub = T("ub", [128, 4, 64], BF16)
        wc = T("wc", [128, 2, NT // 32], F32)
        kd = T("kd", [128, 2, (NBK + 1) * 128], BF16)
        vtk = T("vtk", [128, NBK + 1, 128], BF16)
        pT = T("pT", [128, 8, 256], BF16)
        sTs = [T("sT%d" % i, [128, 2, 256], F32) for i in range(1)]
        ycT = T("ycT", [128, 4, NT], BF16)
        ubp = fa[2]
        pl = [fa[3], fa[4], fa[5]]
        car_sh = [T("car_sh%d" % l, [128, 8], F32) for l in range(depth)]
        stA = [T("stA%d" % l, [128, 2, 64], F32) for l in range(depth)]
        stAb = [T("stAb%d" % l, [128, 2, 64], BF16) for l in range(depth)]
        stD = [T("stD%d" % l, [128, 2, 64], F32) for l in range(depth)]
        stDb = [T("stDb%d" % l, [128, 2, 64], BF16) for l in range(depth)]
        car_p = [T("car_p%d" % l, [128, 2, 16], F32) for l in range(depth)]
        car_k = [T("car_k%d" % l, [128, 2, 128], BF16) for l in range(depth)]
        car_v = [T("car_v%d" % l, [128, 128], BF16) for l in range(depth)]

        psb = [TP("ps%d" % i, [128, 512], F32) for i in range(3)]
        _pa = TP("pa", [128, 512], F32)
        psacc = [_pa, _pa]
        psi = [0]

        def PS():
            r = psb[psi[0] % 3]
            psi[0] += 1
            return r

        def view(ps, ncols, a):
            return Reg(ps.ap[:, 0:ncols].rearrange("p (a c) -> p a c", a=a), ps.buf)

        class BC:
            def __init__(self, m):
                self.m = m

        def bc(m):
            return BC(m)

        _tt = P.tt

        def tt_b(eng, out, in0, in1, op):
            if isinstance(in1, BC):
                for a in range(4):
                    _tt(eng, out[:, a, :], in0[:, a, :], in1.m, op)
            else:
                _tt(eng, out, in0, in1, op)
        P.tt = tt_b


        def dump(name, reg, cond=True):
            if not dbg or not cond or name in dbg_outs:
                return
            shp = list(reg.ap.shape)
            d = nc.dram_tensor("dbg_" + name, shp, F32, kind="ExternalOutput").ap()
            dbg_outs[name] = d
            P.dma("pool", DR(d), reg, owner=reg.buf)

        def next_w(loader, ndma):
            if sid == 0:
                w = wb[wbi[0] % NWB]
                wbi[0] += 1
                loader(w)
                wlog.append(w)
            else:
                w = wlog[wcnt[0]]
                for _ in range(ndma):
                    P.nop()
            wcnt[0] += 1
            return w

        def load_w(src_reg_ap, nkc, ncols, buf_owner, parts=None):
            return next_w(lambda w: P.dma("sp", w[:, 0:nkc, 0:ncols],
                                          Reg(src_reg_ap.rearrange("(k p) c -> p k c", p=128), buf_owner)), 1)

        def rmsnorm(l_gcol, n, zout):
            sq = u8
            for c in range(8):
                if c % 2 == 0:
                    P.act(sq[:, c, 0:n], hT[:, c, 0:n], AF.Square)
                else:
                    P.tt("pool", sq[:, c, 0:n], hT[:, c, 0:n], hT[:, c, 0:n], ALU.mult)
            ps = PS()
            for c in range(8):
                P.mm(ps[:, 0:n], cst["onesm"], sq[:, c, 0:n], start=(c == 0), stop=(c == 7))
            rs = fa[7][:, 0, 0:n]
            P.act(rs, ps[:, 0:n], AF.Sqrt, bias=kcol[:, 2:3])
            P.recip(rs, rs)
            for c in range(8):
                P.stt("dve", zout[:, c, 0:n], hT[:, c, 0:n], pvec[:, l_gcol + c:l_gcol + c + 1],
                      rs, ALU.mult, ALU.mult)

        def proj(w, kcs, col0, rhs_fn, n):
            ps = PS()
            for i, k in enumerate(kcs):
                P.mm(ps[:, 0:n], w[:, i, col0:col0 + 128], rhs_fn(k), start=(i == 0), stop=(i == len(kcs) - 1))
            return ps

        def gate_merge(l, bi, n, gw, ysrc, wrows, first):
            nk = len(wrows)
            wbw = load_w(wbr_s[l].ap[wrows[0] * 128:(wrows[-1] + 1) * 128, :], nk, 1024, win_s[l].buf)
            for c in range(8):
                pg = proj(gw, range(8), c * 128, lambda k: zb[:, k, 0:n], n)
                g = fa[6][:, c % 2, 0:n]
                P.act(g, pg[:, 0:n], AF.Sigmoid)
                pp = proj(wbw, range(nk), c * 128, ysrc, n)
                if first:
                    P.tt("dve", mg[:, c, 0:n], pp[:, 0:n], g, ALU.mult)
                else:
                    P.tt("dve", g, pp[:, 0:n], g, ALU.mult)
                    P.tt("pool", mg[:, c, 0:n], mg[:, c, 0:n], g, ALU.add)

        def branch_A(l, n, first_tile):
            nb = n // 128
            wA = load_w(win_s[l].ap[:, OFF_A:OFF_A + 1024], 8, 1024, win_s[l].buf)
            u = u8
            om = lambda c: drv[:, DV_L * l + c:DV_L * l + c + 1]
            mu = lambda c: pvec[:, PVL * l + 16 + c:PVL * l + 17 + c]
            for c in range(8):
                ps = proj(wA, range(8), c * 128, lambda k: zb[:, k, 0:n], n)
                P.act(u[:, c, 0:n], ps[:, 0:n], AF.Identity, scale=om(c))
                P.stt("dve", u[:, c, 1:n], ps[:, 0:n - 1], mu(c), u[:, c, 1:n], ALU.mult, ALU.add)
                P.stt("dve", u[:, c, 0:1], car_sh[l][:, c:c + 1], mu(c), u[:, c, 0:1], ALU.mult, ALU.add)
                P.copy("dve", car_sh[l][:, c:c + 1], ps[:, n - 1:n])
            if stage == 3.1:
                return ha[7]
            r = lambda hc: u[:, hc, 0:n]
            k = lambda hc: u[:, 2 + hc, 0:n]
            v = lambda hc: u[:, 4 + hc, 0:n]
            twd, adb, sg = ha[0][:, 0, 0:n], ha[0][:, 1, 0:n], ha[1][:, 0, 0:n]
            P.act(twd[0:64], u[0:64, 6, 0:n], AF.Tanh)
            P.copy("dve", adb[64:128], u[64:128, 6, 0:n])
            P.act(sg, u[:, 7, 0:n], AF.Sigmoid)
            vb = ha[1]
            if l == 0:
                for hc in range(2):
                    P.copy("pool", vfirst[:, hc, 0:n], v(hc))
                    P.copy("act", ha[2][:, hc, 0:n], v(hc))
            else:
                for hc in range(2):
                    P.copy("dve", ha[2][:, hc, 0:n], v(hc))
                ps = PS()
                for hc in range(2):
                    P.mm(ps[0:32, 0:n], vdn[l][:, hc, :], ha[2][:, hc, 0:n], start=(hc == 0), stop=(hc == 1))
                lo = ha[3][:, 0, 0:n]
                P.copy("dve", lo[0:32], ps[0:32, 0:n])
                for hc in range(2):
                    ps = PS()
                    P.mm(ps[:, 0:n], vup[l][:, hc * 128:(hc + 1) * 128], lo[0:32])
                    sgm = fa[0][:, hc, 0:n]
                    P.act(sgm, ps[:, 0:n], AF.Sigmoid, bias=pv(l, "vres0", hc))
                    d = fa[1][:, hc, 0:n]
                    P.tt("dve", d, vfirst[:, hc, 0:n], v(hc), ALU.subtract)
                    P.tt("dve", d, d, sgm, ALU.mult)
                    P.tt("dve", v(hc), v(hc), d, ALU.add)
                    P.copy("act", ha[2][:, hc, 0:n], v(hc))
            vbf = ha[2]
            S_, CUM, A_, KKN, KF, E_ = fa[0], fa[1], fa[2], fa[3], fa[4], fa[5]
            rt, kt, bt, at = ha[3], ha[4], ha[5], ha[6]
            for hc in range(2):
                cs = slice(hc * 128, (hc + 1) * 128)
                ps = PS()
                P.mm(ps[:, 0:n], lora[l][0:64, cs], twd[0:64])
                P.act(S_[:, hc, 0:n], ps[:, 0:n], AF.Sigmoid, bias=pv(l, "w0", hc))
                ps = PS()
                P.mm(ps[:, 0:n], lora[l][64:128, cs], adb[64:128])
                P.act(A_[:, hc, 0:n], ps[:, 0:n], AF.Sigmoid, bias=pv(l, "a0", hc))
                P.scan(CUM[:, hc, 0:n], cst["rs64"][:, 0:n], S_[:, hc, 0:n])
                P.ts("dve", KKN[:, hc, 0:n], k(hc), pv(l, "kk", hc), None, ALU.mult)
                P.act(E_[:, hc, 0:n], KKN[:, hc, 0:n], AF.Square)
                ps = PS()
                P.mm(ps[:, 0:n], cst["bd1"], E_[:, hc, 0:n])
                P.ts("dve", E_[:, hc, 0:n], ps[:, 0:n], 1e-24, None, ALU.max)
                P.act(E_[:, hc, 0:n], E_[:, hc, 0:n], AF.Sqrt)
                P.recip(E_[:, hc, 0:n], E_[:, hc, 0:n])
                P.tt("dve", KKN[:, hc, 0:n], KKN[:, hc, 0:n], E_[:, hc, 0:n], ALU.mult)
                P.ts("dve", KF[:, hc, 0:n], A_[:, hc, 0:n], -1.0, pv(l, "ka", hc), ALU.add, ALU.mult)
                P.stt("dve", KF[:, hc, 0:n], KF[:, hc, 0:n], 1.0, k(hc), ALU.add, ALU.mult)
                P.act(E_[:, hc, 0:n], CUM[:, hc, 0:n], AF.Exp, scale=-C0)
                P.tt("dve", rt[:, hc, 0:n], r(hc), E_[:, hc, 0:n], ALU.mult)
                for j in range(n // 64):
                    P.copy("pool", wc[:, hc, j:j + 1], E_[:, hc, j * 64 + 63:j * 64 + 64])
                P.act(E_[:, hc, 0:n], CUM[:, hc, 0:n], AF.Exp, scale=C0)
                P.tt("dve", kt[:, hc, 0:n], KF[:, hc, 0:n], E_[:, hc, 0:n], ALU.mult)
                P.tt("pool", A_[:, hc, 0:n], A_[:, hc, 0:n], KKN[:, hc, 0:n], ALU.mult)
                P.tt("dve", bt[:, hc, 0:n], A_[:, hc, 0:n], E_[:, hc, 0:n], ALU.mult)
                P.tt("dve", S_[:, hc, 0:n], CUM[:, hc, 0:n], S_[:, hc, 0:n], ALU.subtract)
                P.act(E_[:, hc, 0:n], S_[:, hc, 0:n], AF.Exp, scale=-C0)
                P.stt("dve", at[:, hc, 0:n], KKN[:, hc, 0:n], -1.0, E_[:, hc, 0:n], ALU.mult, ALU.mult)
                P.stt("dve", S_[:, hc, 0:n], r(hc), pv(l, "rk", hc), KF[:, hc, 0:n], ALU.mult, ALU.mult)
                ps = PS()
                P.mm(ps[:, 0:n], cst["bd1"], S_[:, hc, 0:n])
                P.tt("dve", CUM[:, hc, 0:n], ps[:, 0:n], v(hc), ALU.mult)
                ps = PS()
                P.mm(ps[:, 0:n], gup[l][:, cs], sg)
                P.copy("act", A_[:, hc, 0:n], ps[:, 0:n])
            BON, G_ = CUM, A_
            if stage == 3.2:
                return ha[7]
            for (src, dst) in ((vbf, tokm[0]), (bt, tokm[1]), (kt, tokm[2])):
                for b in range(nb):
                    ps = PS()
                    for hc in range(2):
                        P.mm(ps[:, hc * 128:(hc + 1) * 128], src[:, hc, b * 128:(b + 1) * 128], identb)
                    P.copy("act", dst[:, b, :, :], view(ps, 256, 2))
            VT, BT, KT = tokm
            if stage == 3.3:
                return ha[7]
            for b in range(nb):
                bs = slice(b * 128, (b + 1) * 128)
                hr = lambda hh: slice((hh % 2) * 64, (hh % 2) * 64 + 64)

                def scores(lhs, rhs):
                    ps = PS()
                    for hh in range(4):
                        P.mm(ps[:, hh * 128:(hh + 1) * 128], lhs[hr(hh), hh // 2, bs], rhs[hr(hh), hh // 2, bs])
                    return view(ps, 512, 4)
                X, Y, IX, Z = nx[0], ny[0], nix[0], nz[0]
                p1 = scores(bt, at)
                P.tt("dve", Y, p1, bc(cst["mu64"]), ALU.mult)
                if stage == 3.41:
                    continue
                p2 = scores(at, bt)
                P.tt("dve", X, p2, bc(cst["ml64"]), ALU.mult)
                P.tt("pool", Z, Y, bc(identb), ALU.add)
                p3 = scores(kt, at)
                P.tt("dve", sc["ak"][:, b], p3, bc(cst["mu64"]), ALU.mult)
                p4 = scores(bt, rt)
                P.tt("dve", sc["rb"][:, b], p4, bc(cst["mui64"]), ALU.mult)
                p5 = scores(kt, rt)
                P.tt("dve", sc["rk"][:, b], p5, bc(cst["mui64"]), ALU.mult)
                if stage == 3.42:
                    continue
                cur = 0
                for lev in range(5 if stage != 3.43 else 1):
                    nxt = 1 - cur
                    Xc, Yc, Zc = nx[cur], ny[cur], nz[cur]
                    Xn, Yn, IXn, Zn = nx[nxt], ny[nxt], nix[nxt], nz[nxt]
                    ps = PS()
                    for hh in range(4):
                        P.mm(ps[:, hh * 128:(hh + 1) * 128], Yc[:, hh, :], Xc[:, hh, :])
                    px = view(ps, 512, 4)
                    if lev < 4:
                        P.copy("act", Xn, px)
                    P.tt("dve", IXn, px, bc(cst["ident"]), ALU.add)
                    if lev < 4:
                        ps = PS()
                        for hh in range(4):
                            P.mm(ps[:, hh * 128:(hh + 1) * 128], Xc[:, hh, :], Yc[:, hh, :])
                        py = view(ps, 512, 4)
                        P.copy("act", Yn, py)
                    ps = PS()
                    for hh in range(4):
                        P.mm(ps[:, hh * 128:(hh + 1) * 128], IXn[:, hh, :], Zc[:, hh, :])
                    pz = view(ps, 512, 4)
                    if lev < 4:
                        P.copy("dve", Zn, pz)
                    else:
                        P.copy("dve", sc["z"][:, b], pz)
                    cur = nxt
            if 3.4 <= stage < 3.45:
                return ha[7]
            ST, STb = stA[l], stAb[l]
            OT = psacc[0]
            for ci in range(n // 64):
                b, c = ci // 2, ci % 2
                cr = slice(c * 64, c * 64 + 64)
                tk = slice(ci * 64, ci * 64 + 64)
                hr = lambda hh: slice((hh % 2) * 64, (hh % 2) * 64 + 64)
                ps1 = PS()
                for hh in range(4):
                    hc = hh // 2
                    P.mm(ps1[cr, hh * 64:(hh + 1) * 64], at[hr(hh), hc, tk], STb[hr(hh), hc, :], start=True, stop=False)
                    P.mm(ps1[cr, hh * 64:(hh + 1) * 64], sc["ak"][cr, b, hh, cr], VT[cr, b, hc, hr(hh)], start=False, stop=True)
                P.copy("dve", rhsb[cr], Reg(ps1.ap[cr, 0:256].rearrange("p (a c) -> p a c", a=4), ps1.buf))
                if stage == 3.45:
                    continue
                ps2 = PS()
                for hh in range(4):
                    P.mm(ps2[cr, hh * 64:(hh + 1) * 64], sc["z"][cr, b, hh, cr], rhsb[cr, hh, :])
                P.copy("act", ub[cr], Reg(ps2.ap[cr, 0:256].rearrange("p (a c) -> p a c", a=4), ps2.buf))
                if stage == 3.46:
                    continue
                for hh in range(4):
                    hc = hh // 2
                    oo = OT[hr(hh), hc * 256 + ci * 64:hc * 256 + ci * 64 + 64]
                    P.mm(oo, STb[hr(hh), hc, :], rt[hr(hh), hc, tk], start=True, stop=False)
                    P.mm(oo, ub[cr, hh, :], sc["rb"][cr, b, hh, cr], start=False, stop=False)
                    P.mm(oo, VT[cr, b, hc, hr(hh)], sc["rk"][cr, b, hh, cr], start=False, stop=True)
                if stage == 3.47:
                    continue
                ps3 = PS()
                for hh in range(4):
                    hc = hh // 2
                    so = ps3[hr(hh), hc * 64:(hc + 1) * 64]
                    P.mm(so, BT[cr, b, hc, hr(hh)], ub[cr, hh, :], start=True, stop=False)
                    P.mm(so, KT[cr, b, hc, hr(hh)], VT[cr, b, hc, hr(hh)], start=False, stop=True)
                for hc in range(2):
                    P.tt("dve", ST[:, hc, :], ST[:, hc, :], ps3[:, hc * 64:(hc + 1) * 64], ALU.add)
                    P.ts("dve", ST[:, hc, :], ST[:, hc, :], wc[:, hc, ci:ci + 1], None, ALU.mult)
                    P.copy("act", STb[:, hc, :], ST[:, hc, :])
            if 3.45 <= stage <= 3.5:
                return ha[7]
            yA = ha[7]
            for hc in range(2):
                o = fa[0][:, hc, 0:n]
                P.copy("act", o, OT[:, hc * 256:hc * 256 + n])
                sq = fa[3][:, hc, 0:n]
                P.act(sq, OT[:, hc * 256:hc * 256 + n], AF.Square)
                pm = PS()
                P.mm(pm[:, 0:n], cst["bd64"], o)
                pq = PS()
                P.mm(pq[:, 0:n], cst["bd64"], sq)
                d = fa[4][:, hc, 0:n]
                P.tt("dve", d, o, pm[:, 0:n], ALU.subtract)
                m2 = fa[5][:, hc, 0:n]
                P.act(m2, pm[:, 0:n], AF.Square)
                P.tt("dve", m2, pq[:, 0:n], m2, ALU.subtract)
                P.ts("dve", m2, m2, 0.0, None, ALU.max)
                P.act(m2, m2, AF.Sqrt, bias=kcol[:, 3:4])
                P.recip(m2, m2)
                P.tt("dve", d, d, m2, ALU.mult)
                P.ts("dve", d, d, pv(l, "lnw", hc), pv(l, "lnb", hc), ALU.mult, ALU.add)
                P.tt("pool", d, d, BON[:, hc, 0:n], ALU.add)
                P.tt("dve", yA[:, hc, 0:n], d, G_[:, hc, 0:n], ALU.mult)
            return yA

        def branch_B(l, n, wBC, first_tile):
            u = ubp
            for c in range(2):
                P.copy("pool", u[:, c, 0:16], car_p[l][:, c, :])
                ps = proj(wBC, range(8), c * 128, lambda k: zb[:, k, 0:n], n)
                P.copy("act", u[:, c, 16:16 + n], ps[:, 0:n])
                P.copy("pool", car_p[l][:, c, :], u[:, c, n:n + 16])
            W = 16 + n
            s2, s4, s8 = pl
            yB = ha[0]
            for c in range(2):
                P.tt("dve", s2[:, c, 1:W], u[:, c, 1:W], u[:, c, 0:W - 1], ALU.add)
                P.tt("dve", s4[:, c, 3:W], s2[:, c, 3:W], s2[:, c, 1:W - 2], ALU.add)
                if c == 0:
                    lo, hi = s2, s4
                else:
                    P.tt("dve", s8[:, c, 7:W], s4[:, c, 7:W], s4[:, c, 3:W - 4], ALU.add)
                    P.tt("dve", s2[:, c, 15:W], s8[:, c, 15:W], s8[:, c, 7:W - 8], ALU.add)
                    lo, hi = s8, s2
                pool_ = s4 if c == 1 else s8
                for (src, pr) in ((lo, slice(0, 64)), (hi, slice(64, 128))):
                    P.stt("dve", pool_[pr, c, 16:W], src[pr, c, 16:W], cst["poolt"][pr, c, 15:16], u[pr, c, 16:W],
                          ALU.mult, ALU.subtract)
                    if first_tile:
                        fs = slice(16 + PADL, 32 + PADL)
                        P.tt("dve", pool_[pr, c, fs], src[pr, c, fs], cst["poolt"][pr, c, :], ALU.mult)
                        P.tt("dve", pool_[pr, c, fs], pool_[pr, c, fs], u[pr, c, fs], ALU.subtract)
                pb = ha[1][:, c, 0:n]
                P.copy("act", pb, pool_[:, c, 16:W])
                ps = PS()
                P.mm(ps[:, 0:n], bmix[l][:, c, :], pb)
                P.ts("dve", yB[:, c, 0:n], ps[:, 0:n], pv(l, "bscale", c), None, ALU.mult)
            return yB

        def branch_C(l, n, wBC, wKV, gb0):
            nb = n // 128
            for c in range(4):
                ps = proj(wBC, range(8), 256 + c * 128, lambda k: zb[:, k, 0:n], n)
                P.copy("act" if c % 2 else "dve", hq[:, c, 0:n], ps[:, 0:n])
            for g in range(2):
                P.copy("pool", kd[:, g, 0:128], car_k[l][:, g, :])
                ps = proj(wKV, range(8), g * 128, lambda k: zb[:, k, 0:n], n)
                P.copy("act", kd[:, g, 128:128 + n], ps[:, 0:n])
                P.copy("pool", car_k[l][:, g, :], kd[:, g, n:n + 128])
            P.copy("pool", vtk[:, 0, :], car_v[l])
            for b in range(nb):
                ps = PS()
                for k_ in range(8):
                    P.mm(ps[:, 0:128], zb[:, k_, b * 128:(b + 1) * 128], wKV[:, k_, 256:384], start=(k_ == 0), stop=(k_ == 7))
                P.copy("act", vtk[:, b + 1, :], ps[:, 0:128])
            P.copy("pool", car_v[l], vtk[:, nb, :])
            for b in range(nb):
                qs = slice(b * 128, (b + 1) * 128)
                gb = gb0 + b
                for hp in range(4):
                    ps = PS()
                    for hh2 in range(2):
                        h = hp * 2 + hh2
                        g = h // 4
                        rows = slice((h % 2) * 64, (h % 2) * 64 + 64)
                        for part in range(2):
                            P.mm(ps[:, hh2 * 256 + part * 128:hh2 * 256 + part * 128 + 128],
                                 kd[rows, g, (b + part) * 128:(b + part + 1) * 128], hq[rows, h // 2, qs])
                    pv_ = view(ps, 512, 2)
                    sT = sTs[0]
                    P.stt("dve", sT, pv_, 0.125, cst["swab"][:, hp * 2:hp * 2 + 2, :], ALU.mult, ALU.add)
                    if gb == 0:
                        P.memset("pool", sT[:, :, 0:128], NEG)
                        if PADL:
                            P.memset("pool", sT[0:PADL, :, 128:256], NEG)
                    elif gb == 1 and PADL:
                        P.memset("pool", sT[0:PADL, :, 0:128], NEG)
                    P.act(pT[:, hp * 2:hp * 2 + 2, :], sT, AF.Exp)
                for hc in range(4):
                    po = PS()
                    for hh2 in range(2):
                        h = hc * 2 + hh2
                        g = h // 4
                        rows = slice(hh2 * 64, hh2 * 64 + 64)
                        for part in range(2):
                            P.mm(po[rows, 0:128], vtk[:, b + part, g * 64:(g + 1) * 64], pT[:, h, part * 128:(part + 1) * 128],
                                 start=(part == 0), stop=(part == 1))
                        for part in range(2):
                            P.mm(po[rows, 128:256], onesb[:, 0:64], pT[:, h, part * 128:(part + 1) * 128],
                                 start=(part == 0), stop=(part == 1))
                    den = fa[0][:, 0, 0:128]
                    for hh2 in range(2):
                        h = hc * 2 + hh2
                        rows = slice(hh2 * 64, hh2 * 64 + 64)
                        P.ts("dve", den[rows], po[rows, 128:256], esink[rows, l, h:h + 1], None, ALU.add)
                    P.recip(den, den)
                    P.tt("dve", ycT[:, hc, qs], po[:, 0:128], den, ALU.mult)
            return ycT

        def branch_D(l, n, first_tile):
            nb = n // 128
            wD = load_w(win_s[l].ap[:, OFF_D:OFF_D + 1024], 8, 1024, win_s[l].buf)
            u = u8
            for c in range(8):
                ps = proj(wD, range(8), c * 128, lambda k: zb[:, k, 0:n], n)
                if c < 2:
                    P.act(u[:, c, 0:n], ps[:, 0:n], AF.Silu)
                elif c < 4:
                    P.copy("dve", u[:, c, 0:n], ps[:, 0:n])
                elif c < 6:
                    P.copy("act", ha[0][:, c - 4, 0:n], ps[:, 0:n])
                else:
                    P.act(u[:, c, 0:n], ps[:, 0:n], AF.Silu)
            vb = ha[0]
            lb = lambda hc: drv[:, DV_L * l + 8 + hc:DV_L * l + 9 + hc]
            omlb = lambda hc: drv[:, DV_L * l + 10 + hc:DV_L * l + 11 + hc]
            SG, LF, B_, E_, KK = fa[0], fa[1], fa[2], fa[3], fa[4]
            qt, kt = ha[1], ha[2]
            for hc in range(2):
                fpre = u[:, 2 + hc, 0:n]
                P.act(SG[:, hc, 0:n], fpre, AF.Sigmoid)
                P.ts("dve", LF[:, hc, 0:n], SG[:, hc, 0:n], omlb(hc), lb(hc), ALU.mult, ALU.add)
                P.ts("dve", LF[:, hc, 0:n], LF[:, hc, 0:n], 1e-30, None, ALU.max)
                P.act(LF[:, hc, 0:n], LF[:, hc, 0:n], AF.Ln)
                P.scan(B_[:, hc, 0:n], cst["rs32"][:, 0:n], LF[:, hc, 0:n])
                P.act(KK[:, hc, 0:n], fpre, AF.Sigmoid, scale=-1.0)
                P.ts("dve", KK[:, hc, 0:n], KK[:, hc, 0:n], omlb(hc), None, ALU.mult)
                P.act(E_[:, hc, 0:n], B_[:, hc, 0:n], AF.Exp)
                P.tt("dve", qt[:, hc, 0:n], u[:, hc, 0:n], E_[:, hc, 0:n], ALU.mult)
                for j in range(n // 32):
                    P.copy("pool", wc[:, hc, j:j + 1], E_[:, hc, j * 32 + 31:j * 32 + 32])
                P.act(E_[:, hc, 0:n], B_[:, hc, 0:n], AF.Exp, scale=-1.0)
                P.tt("dve", kt[:, hc, 0:n], KK[:, hc, 0:n], E_[:, hc, 0:n], ALU.mult)
            for b in range(nb):
                ps = PS()
                for hc in range(2):
                    P.mm(ps[:, hc * 128:(hc + 1) * 128], vb[:, hc, b * 128:(b + 1) * 128], identb)
                P.copy("act", tokm[0][:, b, :, :], view(ps, 256, 2))
                ps = PS()
                for hc in range(2):
                    P.mm(ps[:, hc * 128:(hc + 1) * 128], kt[:, hc, b * 128:(b + 1) * 128], identb)
                P.ts("dve", tokm[1][:, b, :, :], view(ps, 256, 2), cst["hm32"][:, 0:1], None, ALU.mult)
                P.ts("dve", tokm[2][:, b, :, :], view(ps, 256, 2), cst["hm32"][:, 1:2], None, ALU.mult)
            VT = tokm[0]
            hr = lambda hh: slice((hh % 2) * 64, (hh % 2) * 64 + 64)
            for b in range(nb):
                bs = slice(b * 128, (b + 1) * 128)
                ps = PS()
                for hh in range(4):
                    P.mm(ps[:, hh * 128:(hh + 1) * 128], kt[hr(hh), hh // 2, bs], qt[hr(hh), hh // 2, bs])
                P.tt("dve", sc["ak"][:, b], view(ps, 512, 4), bc(cst["mui32"]), ALU.mult)
            ST, STb = stD[l], stDb[l]
            OT = psacc[1]
            for ci in range(n // 32):
                b, c = ci // 4, ci % 4
                pr = slice((c // 2) * 64, (c // 2) * 64 + 64)
                cc = slice(c * 32, c * 32 + 32)
                tk = slice(ci * 32, ci * 32 + 32)
                KT = tokm[1 + (c % 2)]
                for hh in range(4):
                    hc = hh // 2
                    oo = OT[hr(hh), hc * 256 + ci * 32:hc * 256 + ci * 32 + 32]
                    P.mm(oo, STb[hr(hh), hc, :], qt[hr(hh), hc, tk], start=True, stop=False)
                    P.mm(oo, VT[pr, b, hc, hr(hh)], sc["ak"][pr, b, hh, cc], start=False, stop=True)
                ps3 = PS()
                for hh in range(4):
                    hc = hh // 2
                    P.mm(ps3[hr(hh), hc * 64:(hc + 1) * 64], KT[pr, b, hc, hr(hh)], VT[pr, b, hc, hr(hh)])
                for hc in range(2):
                    P.tt("dve", ST[:, hc, :], ST[:, hc, :], ps3[:, hc * 64:(hc + 1) * 64], ALU.add)
                    P.ts("dve", ST[:, hc, :], ST[:, hc, :], wc[:, hc, ci:ci + 1], None, ALU.mult)
                    P.copy("act", STb[:, hc, :], ST[:, hc, :])
            yD = ha[7]
            for hc in range(2):
                o = fa[0][:, hc, 0:n]
                P.copy("act", o, OT[:, hc * 256:hc * 256 + n])
                sq = fa[1][:, hc, 0:n]
                P.act(sq, OT[:, hc * 256:hc * 256 + n], AF.Square)
                pq = PS()
                P.mm(pq[:, 0:n], cst["bd64"], sq)
                rs = fa[2][:, hc, 0:n]
                P.act(rs, pq[:, 0:n], AF.Sqrt, bias=kcol[:, 2:3])
                P.recip(rs, rs)
                P.tt("dve", o, o, rs, ALU.mult)
                P.stt("dve", yD[:, hc, 0:n], o, pv(l, "dnorm", hc), u[:, 6 + hc, 0:n], ALU.mult, ALU.mult)
            return yD


        def layer(l, n, first_tile, c0, gb0):
            layer_(l, n, first_tile, c0, gb0)
            dump("h2", hT, first_tile and l == 0)

        def layer_(l, n, first_tile, c0, gb0):
            dcond = first_tile and l == 0
            dump("h0", hT, dcond)
            rmsnorm(PVL * l + 0, n, zb)
            dump("z", zb, dcond)
            if stage == 2:
                return
            yA = branch_A(l, n, first_tile)
            dump("yA", yA, dcond)
            if 3 <= stage < 4:
                return
            gw = load_w(win_s[l].ap[:, 0:1024], 8, 1024, win_s[l].buf)
            gate_merge(l, 0, n, gw, lambda k: yA[:, k, 0:n], [0, 1], True)
            wBC = load_w(win_s[l].ap[:, OFF_B:OFF_B + 1024], 8, 1024, win_s[l].buf)
            kv_src = win_s[l].ap

            def _ld_kv(w):
                for g in range(2):
                    for dup in range(2):
                        P.dma("sp", w[:, :, g * 128 + dup * 64:g * 128 + dup * 64 + 64],
                              Reg(kv_src[:, OFF_C + 512 + g * 64:OFF_C + 512 + g * 64 + 64].rearrange("(k p) c -> p k c", p=128), win_s[l].buf))
                P.dma("sp", w[:, :, 256:384], Reg(kv_src[:, OFF_C + 640:OFF_C + 768].rearrange("(k p) c -> p k c", p=128), win_s[l].buf))
            wKV = next_w(_ld_kv, 5)
            if stage == 4:
                return
            yB = branch_B(l, n, wBC, first_tile)
            dump("yB", yB, dcond)
            if stage == 5:
                return
            yC = branch_C(l, n, wBC, wKV, gb0)
            dump("yC", yC, dcond)
            if stage == 6:
                return
            gw = load_w(win_s[l].ap[:, 1024:2048], 8, 1024, win_s[l].buf)
            gate_merge(l, 1, n, gw, lambda k: yB[:, k, 0:n], [2, 3], False)
            gw = load_w(win_s[l].ap[:, 2048:3072], 8, 1024, win_s[l].buf)
            gate_merge(l, 2, n, gw, lambda k: yC[:, k, 0:n], [4, 5, 6, 7], False)
            yD = branch_D(l, n, first_tile)
            dump("yD", yD, dcond)
            if stage == 7:
                return
            gw = load_w(win_s[l].ap[:, 3072:4096], 8, 1024, win_s[l].buf)
            gate_merge(l, 3, n, gw, lambda k: yD[:, k, 0:n], [8, 9], False)
            dump("mg", mg, dcond)
            for c in range(8):
                P.copy("act" if c % 2 else "pool", mgb[:, c, 0:n], mg[:, c, 0:n])
            wo = load_w(wout_s[l].ap, 8, 1024, win_s[l].buf)
            for c in range(8):
                ps = proj(wo, range(8), c * 128, lambda k: mgb[:, k, 0:n], n)
                P.tt("dve", hT[:, c, c0:n], hT[:, c, c0:n], ps[:, c0:n], ALU.add)
            dump("h1", hT, dcond)
            if stage == 8:
                return
            rmsnorm(PVL * l + 8, n, zb)
            for j0 in range(0, 22, 4):
                nj = min(4, 22 - j0)
                def _ld_up(w, j0=j0, nj=nj):
                    P.dma("sp", w[:, :, 0:nj * 128], Reg(wup_s[l].ap[:, j0 * 128:(j0 + nj) * 128].rearrange("(k p) c -> p k c", p=128), win_s[l].buf))
                    P.dma("sp", w[:, :, 512:512 + nj * 128], Reg(wup_s[l].ap[:, DFF + j0 * 128:DFF + (j0 + nj) * 128].rearrange("(k p) c -> p k c", p=128), win_s[l].buf))
                w = next_w(_ld_up, 2)
                for j in range(nj):
                    pg = proj(w, range(8), j * 128, lambda k: zb[:, k, 0:n], n)
                    sg = fa[6][:, j % 2, 0:n]
                    P.act(sg, pg[:, 0:n], AF.Silu)
                    pu = proj(w, range(8), 512 + j * 128, lambda k: zb[:, k, 0:n], n)
                    P.tt("dve", actb[:, j0 + j, 0:n], pu[:, 0:n], sg, ALU.mult)
            for g0 in range(0, 22, 8):
                ng = min(8, 22 - g0)
                w = load_w(wdn_s[l].ap[g0 * 128:(g0 + ng) * 128, :], ng, 1024, win_s[l].buf)
                for c in range(8):
                    ps = proj(w, range(ng), c * 128, lambda k: actb[:, g0 + k, 0:n], n)
                    P.tt("dve", hT[:, c, 0:n], hT[:, c, 0:n], ps[:, 0:n], ALU.add)

        def run_seq(s):
            for l in range(depth):
                P.memset("pool", car_sh[l], 0.0)
                P.memset("pool", stA[l], 0.0)
                P.memset("pool", stAb[l], 0.0)
                P.memset("pool", stD[l], 0.0)
                P.memset("pool", stDb[l], 0.0)
                P.memset("pool", car_p[l], 0.0)
                P.memset("pool", car_k[l], 0.0)
                P.memset("pool", car_v[l], 0.0)
            for ti, (p0, n) in enumerate(tiles):
                first_tile = ti == 0
                nb = n // 128
                for b in range(nb):
                    gb = p0 // 128 + b
                    xt = xs[0]
                    xsi[0] += 1
                    if gb == 0:
                        P.memset("pool", xt, 0.0)
                        P.dma("sp", xt[PADL:128, :], DR(meta_d))
                    else:
                        P.dma("sp", xt, DR(x_d[s, (gb - 1) * 128:gb * 128, :]))
                    for c in range(8):
                        ps = PS()
                        P.mm(ps[:, 0:128], xt[:, c * 128:(c + 1) * 128], cst["ident"])
                        P.copy("act" if c % 2 else "dve", hT[:, c, b * 128:(b + 1) * 128], ps[:, 0:128])
                c0 = PADL if first_tile else 0
                for l in range(depth if stage >= 2 else 0):
                    layer(l, n, first_tile, c0, p0 // 128)
                if dbg:
                    pass
                rmsnorm(GO, n, mg)
                for b in range(nb):
                    gb = p0 // 128 + b
                    if gb == 0:
                        continue
                    ot = os_[0]
                    osi[0] += 1
                    for c in range(8):
                        ps = PS()
                        P.mm(ps[:, 0:128], mg[:, c, b * 128:(b + 1) * 128], cst["ident"])
                        P.copy("act" if c % 2 else "dve", ot[:, c * 128:(c + 1) * 128], ps[:, 0:128])
                    P.dma("sp", DR(out_d[s, (gb - 1) * 128:gb * 128, :]), ot, owner=ot.buf)
                    outbufs.append(ot.buf)
        return locals()

    streams = [make_stream(i) for i in range(nseq)]
    S0 = streams[0]
    fa = S0["fa"]
    _mg0 = S0["mg"]
    st32 = Reg(_mg0.ap[:, 0:4, :].rearrange("p (a b) c -> p a (b c)", a=2), _mg0.buf)
    st32b = Reg(_mg0.ap[:, 4:8, :].rearrange("p (a b) c -> p a (b c)", a=2), _mg0.buf)
    for k in CONST_ORDER:
        P.dma("sp", cst[k], DR(c_d[k]), owner=cbuf)
    P.dma("sp", pvec, DR(pvec_d), owner=cbuf)
    P.copy("dve", identb, cst["ident"])
    P.copy("dve", mu64b, cst["mu64"])
    for l in range(depth):
        for (dst, src, shp) in ((lora[l], lora_d[l], None), (gup[l], gup_d[l], None)):
            P.dma("sp", st32[:, 0, :], DR(src))
            P.copy("dve", dst, st32[:, 0, :])
        P.dma("sp", st32b[:, :, 0:128], DR(bmix_d[l]))
        P.copy("dve", bmix[l], st32b[:, :, 0:128])
        P.dma("sp", st32[:, :, 0:32], DR(vdn_d[l]))
        P.copy("dve", vdn[l], st32[:, :, 0:32])
        P.dma("sp", st32b[0:32, 0, :], DR(vup_d[l]))
        P.copy("dve", vup[l], st32b[0:32, 0, :])
    for l in range(depth):
        for (dst, src, rows) in ((win_s[l], win_d[l], D_MODEL), (wbr_s[l], wbr_d[l], 1280),
                                 (wout_s[l], wout_d[l], D_MODEL), (wup_s[l], wup_d[l], D_MODEL),
                                 (wdn_s[l], wdn_d[l], DFF)):
            step = 256
            for r0 in range(0, rows, step):
                r1 = min(rows, r0 + step)
                P.dma("pool", Reg(dst.ap[r0:r1, :], None), DR(src[r0:r1, :]), owner=win_s[l].buf)
        win_s[l].buf.w = (win_s[l].buf, win_s[l].buf.dval)

    def pv(l, nm, c=None, n=None):
        o = PVL * l + PV[nm]
        if c is None:
            return pvec[:, o:o + (n or 2)]
        return pvec[:, o + c:o + c + 1]
    GO = PVL * depth
    for l in range(depth):
        P.ts("dve", drv[:, DV_L * l:DV_L * l + 8], pvec[:, PVL * l + 16:PVL * l + 24], -1.0, 1.0, ALU.mult, ALU.add)
        so = GO + 8 + 2 * depth + 8 * l
        P.act(esink[:, l, :], pvec[:, so:so + 8], AF.Exp)
    lbx = fa[0]
    for c in range(2):
        mx = fa[1][:, 0, 0:1]
        P.copy("dve", mx, pvec[:, GO + 8 + c:GO + 9 + c])
        for l in range(1, depth):
            P.tt("dve", mx, mx, pvec[:, GO + 8 + 2 * l + c:GO + 9 + 2 * l + c], ALU.max)
        P.ts("dve", fa[1][:, 0, 1:2], mx, -1.0, None, ALU.mult)
        sm = fa[1][:, 0, 2:3]
        for l in range(depth):
            P.act(lbx[:, 0, l:l + 1], pvec[:, GO + 8 + 2 * l + c:GO + 9 + 2 * l + c], AF.Exp, bias=fa[1][:, 0, 1:2])
            if l == 0:
                P.copy("dve", sm, lbx[:, 0, 0:1])
            else:
                P.tt("dve", sm, sm, lbx[:, 0, l:l + 1], ALU.add)
        P.recip(fa[1][:, 0, 3:4], sm)
        for l in range(depth):
            P.ts("dve", lbx[:, 0, l:l + 1], lbx[:, 0, l:l + 1], fa[1][:, 0, 3:4], None, ALU.mult)
        for l in range(depth):
            dst = drv[:, DV_L * l + 8 + c:DV_L * l + 9 + c]
            if l == 0:
                P.memset("dve", dst, 0.0)
            elif l == 1:
                P.copy("dve", dst, lbx[:, 0, 1:2])
            else:
                P.tt("dve", dst, drv[:, DV_L * (l - 1) + 8 + c:DV_L * (l - 1) + 9 + c], lbx[:, 0, l:l + 1], ALU.add)
            P.ts("dve", drv[:, DV_L * l + 10 + c:DV_L * l + 11 + c], dst, -1.0, 1.0, ALU.mult, ALU.add)

    lists = []
    for i in range(nseq if stage >= 1 else 0):
        P.rec = []
        streams[i]["run_seq"](i)
        lists.append(P.rec)
        P.rec = None
    P.replay_zip(lists)
    P.wait_all("sp", list(set(outbufs)))
    P.emit()
    P.stack.close()
    nc._dbg_names = list(dbg_outs.keys())
    return nc, P


_CACHE = {}


def run(inp, nseq_total, seq, depth, ncores, NT=128, dbg=None):
    nseq = nseq_total // ncores
    key = (nseq, seq, depth, NT, dbg is not None)
    if key not in _CACHE:
        _CACHE[key] = build_nc(nseq, seq, depth, NT, dbg is not None)[0]
    nc = _CACHE[key]
    consts = make_consts()
    small = pack_small(inp, depth)
    f32 = lambda a: np.ascontiguousarray(np.asarray(a, np.float32))
    shared = {"meta": f32(inp["meta"]), "w_in": f32(inp["w_in"]), "w_branch": f32(inp["w_branch"]),
              "w_out": f32(inp["w_out"]), "w_ffn_up": f32(inp["w_ffn_up"]), "w_ffn_down": f32(inp["w_ffn_down"])}
    shared.update(small)
    for k in CONST_ORDER:
        shared["c_" + k] = consts[k]
    x = f32(inp["x"])
    in_maps = []
    for c in range(ncores):
        m = dict(shared)
        m["x"] = np.ascontiguousarray(x[c * nseq:(c + 1) * nseq])
        in_maps.append(m)
    res = run_bass_kernel_spmd(nc, in_maps, core_ids=list(range(ncores)))
    if dbg is not None:
        for k in nc._dbg_names:
            dbg[k] = np.asarray(res.results[0]["dbg_" + k])
    return np.concatenate([r["out"] for r in res.results], axis=0).astype(np.float32)


def kernel(**inputs):
    x = inputs["x"]
    depth = inputs["w_in"].shape[0]
    return run(inputs, x.shape[0], x.shape[1], depth, NCORES)
```

```python
import contextlib
import numpy as np
import concourse.bass as bass
import concourse.mybir as mybir
from concourse.bass_utils import run_bass_kernel_spmd

F32 = mybir.dt.float32
BF16 = mybir.dt.bfloat16
AF = mybir.ActivationFunctionType
ALU = mybir.AluOpType

D_MODEL = 1024
N_META = 16
DFF = 2816
NCORES = 8
C0 = 0.6065306597126334
NEG = -30000.0
B_WINDOWS = (2, 4, 8, 16)
OFF_A, OFF_B, OFF_C, OFF_D = 4096, 5120, 5376, 6144


class Buf:
    __slots__ = ("name", "w", "r", "dsem", "dval", "excl")

    def __init__(self, name):
        self.name = name
        self.w = None
        self.r = []
        self.dsem = None
        self.dval = 0
        self.excl = False


class Reg:
    __slots__ = ("ap", "buf")

    def __init__(self, ap, buf):
        self.ap = ap
        self.buf = buf

    def __getitem__(self, idx):
        return Reg(self.ap[idx], self.buf)


def R(x):
    return x.ap if isinstance(x, Reg) else x


class Prog:
    ENG = ("pe", "act", "dve", "pool", "sp")

    def __init__(self, nc):
        self.nc = nc
        self.stack = contextlib.ExitStack()
        self.ops = {e: [] for e in self.ENG}
        self.cnt = {e: 0 for e in self.ENG}
        self.seen = {e: {} for e in self.ENG}
        self.sems = {}
        self.nsem = 0
        for e in ("pe", "act", "dve", "pool"):
            self.sems[e] = self.stack.enter_context(nc.semaphore("s_" + e))
        self.nops = 0

    def tile(self, name, shape, dtype):
        t = self.stack.enter_context(self.nc.sbuf_tensor(name, list(shape), dtype))
        return Reg(t[:], Buf(name))

    def psum(self, name, shape, dtype):
        t = self.stack.enter_context(self.nc.psum_tensor(name, list(shape), dtype))
        b = Buf(name)
        b.excl = True
        return Reg(t[:], b)

    def dsem(self, buf):
        if buf.dsem is None:
            buf.dsem = self.stack.enter_context(self.nc.semaphore("d%d" % self.nsem))
            self.nsem += 1
        return buf.dsem

    def _need(self, eng, reads, writes):
        need = {}

        def add(tok, same_ok):
            if tok is None:
                return
            k, v = tok
            if k == eng and (eng == "pe"):
                return
            if need.get(k, 0) < v:
                need[k] = v

        for b in reads:
            add(b.w, False)
        for b in writes:
            add(b.w, True)
            for t in b.r:
                add(t, True)
        seen = self.seen[eng]
        waits = []
        for k, v in need.items():
            if seen.get(k, 0) >= v:
                continue
            seen[k] = v
            waits.append((k, v))
        return waits

    def _commit(self, tok, reads, writes):
        for b in writes:
            b.w = tok
            b.r = []
        for b in reads:
            b.r.append(tok)
            if len(b.r) > 16:
                d = {}
                for k, v in b.r:
                    if d.get(k, 0) < v:
                        d[k] = v
                b.r = list(d.items())

    rec = None

    def nop(self):
        if self.rec is not None:
            self.rec.append(None)

    def replay_zip(self, lists):
        n = max([len(x) for x in lists] + [0])
        for i in range(n):
            for x in lists:
                if i < len(x) and x[i] is not None:
                    it = x[i]
                    if it[0] == "op":
                        self.op(*it[1:])
                    else:
                        self.dma(*it[1:-1], **it[-1])

    def op(self, eng, emit, reads, writes, rg=None):
        if self.rec is not None:
            self.rec.append(("op", eng, emit, reads, writes, rg))
            return
        if rg is not None:
            last = getattr(self, "last_rg", None)
            if last is not None and last != rg and self.cnt["pe"] > 0:
                v = self.cnt["pe"]
                if self.seen["pe"].get("pe", 0) < v:
                    self.seen["pe"]["pe"] = v
                    self.ops["pe"].append(([("pe", v)], None, None))
            self.last_rg = rg
        reads = [r.buf for r in reads if isinstance(r, Reg) and r.buf is not None]
        writes = [r.buf for r in writes if isinstance(r, Reg) and r.buf is not None]
        ex = [b for b in reads if b.excl]
        if ex:
            reads = [b for b in reads if not b.excl]
            writes = writes + ex
        waits = self._need(eng, reads, writes)
        self.cnt[eng] += 1
        tok = (eng, self.cnt[eng])
        self.ops[eng].append((waits, emit, (eng, 1)))
        self._commit(tok, reads, writes)
        self.nops += 1

    def dma(self, q, out, in_, owner=None, **kw):
        if self.rec is not None:
            self.rec.append(("dma", q, out, in_, owner, kw))
            return
        if owner is None:
            owner = out.buf if out.buf is not None else in_.buf
        reads = [in_.buf] if in_.buf is not None else []
        writes = [out.buf] if out.buf is not None else []
        waits = self._need(q, reads, writes)
        self.dsem(owner)
        owner.dval += 16
        tok = (owner, owner.dval)
        oap, iap = out.ap, in_.ap

        def emit(e):
            return e.dma_start(out=oap, in_=iap, **kw)
        self.ops[q].append((waits, emit, (owner, 16)))
        self._commit(tok, reads, writes)
        self.nops += 1

    def wait_all(self, eng, bufs):
        need = {}
        for b in bufs:
            for tok in ([b.w] if b.w else []) + list(b.r):
                k, v = tok
                if need.get(k, 0) < v:
                    need[k] = v
        self.ops[eng].append((list(need.items()), None, None))

    def _semh(self, k):
        return self.sems[k] if isinstance(k, str) else k.dsem

    def emit(self):
        with self.nc.Block() as block:
            def run(name):
                def f(e):
                    for waits, emit, inc in self.ops[name]:
                        if emit is None:
                            for k, v in waits:
                                e.wait_ge(self._semh(k), v)
                            continue
                        for k, v in waits[1:]:
                            e.wait_ge(self._semh(k), v)
                        ins = emit(e)
                        if waits:
                            ins._wait_ge(self._semh(waits[0][0]), waits[0][1])
                        ins.then_inc(self._semh(inc[0]), inc[1])
                return f
            block.tensor(run("pe"))
            block.scalar(run("act"))
            block.vector(run("dve"))
            block.gpsimd(run("pool"))
            block.sync(run("sp"))

    def mm(self, out, lhsT, rhs, start=True, stop=True):
        o, l, r = out.ap, lhsT.ap, rhs.ap
        sp = l.start_partition
        sp = sp() if callable(sp) else sp
        rg = (int(sp), int(l.shape[0]))
        self.op("pe", lambda e: e.matmul(o, l, r, start=start, stop=stop), [lhsT, rhs], [out], rg=rg)

    def tt(self, eng, out, in0, in1, op):
        o, a, b = out.ap, in0.ap, in1.ap
        self.op(eng, lambda e: e.tensor_tensor(out=o, in0=a, in1=b, op=op), [in0, in1], [out])

    def ts(self, eng, out, in0, s1, s2, op0, op1=None):
        o, a = out.ap, in0.ap
        rs = [in0] + [s for s in (s1, s2) if isinstance(s, Reg)]
        if op1 is None:
            self.op(eng, lambda e: e.tensor_scalar(out=o, in0=a, scalar1=R(s1), scalar2=None, op0=op0), rs, [out])
        else:
            self.op(eng, lambda e: e.tensor_scalar(out=o, in0=a, scalar1=R(s1), scalar2=R(s2), op0=op0, op1=op1), rs, [out])

    def stt(self, eng, out, in0, sc, in1, op0, op1):
        o, a, b = out.ap, in0.ap, in1.ap
        rs = [in0, in1] + ([sc] if isinstance(sc, Reg) else [])
        self.op(eng, lambda e: e.scalar_tensor_tensor(out=o, in0=a, scalar=R(sc), in1=b, op0=op0, op1=op1), rs, [out])

    def act(self, out, in_, func, bias=None, scale=1.0):
        o, a = out.ap, in_.ap
        rs = [in_] + [s for s in (bias, scale) if isinstance(s, Reg)]
        if bias is None:
            self.op("act", lambda e: e.activation(out=o, in_=a, func=func, scale=R(scale)), rs, [out])
        else:
            self.op("act", lambda e: e.activation(out=o, in_=a, func=func, bias=R(bias), scale=R(scale)), rs, [out])

    def copy(self, eng, out, in_):
        o, a = out.ap, in_.ap
        if eng == "act":
            self.op("act", lambda e: e.copy(out=o, in_=a), [in_], [out])
        else:
            self.op(eng, lambda e: e.tensor_copy(out=o, in_=a), [in_], [out])

    def memset(self, eng, out, val):
        o = out.ap
        self.op(eng, lambda e: e.memset(o, val), [], [out])

    def recip(self, out, in_):
        o, a = out.ap, in_.ap
        self.op("dve", lambda e: e.reciprocal(out=o, in_=a), [in_], [out])

    def scan(self, out, d0, d1):
        o, a, b = out.ap, d0.ap, d1.ap
        self.op("dve", lambda e: e.tensor_tensor_scan(out=o, data0=a, data1=b, initial=0.0,
                                                      op0=ALU.mult, op1=ALU.add), [d0, d1], [out])


def pcol(v):
    v = np.asarray(v, np.float32)
    return np.ascontiguousarray(v.reshape(-1, 128).T)


def make_consts():
    c = {}
    c["ident"] = np.eye(128, dtype=np.float32)
    bd = np.zeros((128, 128), np.float32)
    bd[:64, :64] = 1.0
    bd[64:, 64:] = 1.0
    c["bd1"] = bd
    c["bd64"] = bd / 64.0
    c["onesm"] = np.full((128, 128), 1.0 / D_MODEL, np.float32)
    i = np.arange(128)[:, None]
    j = np.arange(128)[None, :]
    same64 = (i // 64) == (j // 64)
    same32 = (i // 32) == (j // 32)
    c["mu64"] = (same64 & (i < j)).astype(np.float32)
    c["mui64"] = (same64 & (i <= j)).astype(np.float32)
    c["ml64"] = (same64 & (j < i)).astype(np.float32)
    c["mui32"] = (same32 & (i <= j)).astype(np.float32)
    m64 = np.ones((128, 256), np.float32)
    m64[:, ::64] = 0.0
    m32 = np.ones((128, 256), np.float32)
    m32[:, ::32] = 0.0
    hm = np.zeros((128, 2), np.float32)
    hm[:, 0] = ((np.arange(128) // 32) % 2 == 0)
    hm[:, 1] = ((np.arange(128) // 32) % 2 == 1)
    c["hm32"] = hm
    c["rs64"] = m64
    c["rs32"] = m32
    si = np.arange(128)[:, None]
    qi = np.arange(128)[None, :]
    bias = np.zeros((128, 8, 256), np.float32)
    for h in range(8):
        sl = 2.0 ** (-8.0 * (h + 1) / 8)
        dprev = 128 + qi - si
        dcur = qi - si
        bp = np.where((dprev >= 0) & (dprev < 128), -sl * dprev, NEG)
        bc = np.where((dcur >= 0) & (dcur < 128), -sl * dcur, NEG)
        bias[:, h, 0:128] = bp
        bias[:, h, 128:256] = bc
    c["swab"] = bias
    pt = np.zeros((128, 2, 16), np.float32)
    for ch in range(2):
        for half in range(2):
            w = B_WINDOWS[ch * 2 + half]
            pt[half * 64:(half + 1) * 64, ch, :] = 1.0 / np.minimum(np.arange(16) + 1, w)
    c["poolt"] = pt
    return c


CONST_ORDER = ["ident", "bd1", "bd64", "onesm", "mu64", "mui64", "ml64", "mui32", "rs64", "rs32",
               "swab", "poolt", "hm32"]

PV = {"nmix": 0, "nffn": 8, "mu": 16, "w0": 24, "a0": 26, "kk": 28, "ka": 30, "rk": 32, "lnw": 34,
      "lnb": 36, "vres0": 38, "bscale": 40, "dnorm": 42}
PVL = 44


def pack_small(inp, depth):
    cols = []
    for l in range(depth):
        blk = np.zeros((128, PVL), np.float32)
        blk[:, 0:8] = pcol(inp["norm_mix"][l])
        blk[:, 8:16] = pcol(inp["norm_ffn"][l])
        blk[:, 16:24] = pcol(inp["a_mu"][l])
        for nm, key in (("w0", "a_w0"), ("a0", "a_a0"), ("kk", "a_kk"), ("ka", "a_ka"), ("rk", "a_rk"),
                        ("lnw", "a_ln_w"), ("lnb", "a_ln_b"), ("bscale", "b_scale"), ("dnorm", "d_norm")):
            blk[:, PV[nm]:PV[nm] + 2] = pcol(inp[key][l])
        if l >= 1:
            blk[:, PV["vres0"]:PV["vres0"] + 2] = pcol(inp["a_vres0"][l - 1])
        cols.append(blk)
    g = np.zeros((128, 8 + 2 * depth + 8 * depth), np.float32)
    g[:, 0:8] = pcol(inp["norm_final"])
    for l in range(depth):
        g[:, 8 + 2 * l:10 + 2 * l] = pcol(inp["d_lower_bounds"][l])
        g[:, 8 + 2 * depth + 8 * l:8 + 2 * depth + 8 * l + 8] = np.broadcast_to(
            np.asarray(inp["c_sinks"][l], np.float32)[None, :], (128, 8))
    pvec = np.concatenate(cols + [g], axis=1)
    lora = np.zeros((depth, 128, 256), np.float32)
    gup = np.zeros((depth, 128, 256), np.float32)
    bmix = np.zeros((depth, 128, 2, 128), np.float32)
    vdn = np.zeros((depth, 128, 2, 32), np.float32)
    vup = np.zeros((depth, 32, 256), np.float32)
    for l in range(depth):
        lora[l, :64] = inp["a_w_up"][l]
        lora[l, 64:] = inp["a_a_up"][l]
        gup[l] = inp["a_g_up"][l]
        for gi in range(4):
            ch, half = gi // 2, gi % 2
            bmix[l, half * 64:(half + 1) * 64, ch, half * 64:(half + 1) * 64] = inp["b_mix"][l, gi]
        if l >= 1:
            vdn[l] = np.asarray(inp["a_vres_down"][l - 1], np.float32).reshape(2, 128, 32).transpose(1, 0, 2)
            vup[l] = inp["a_vres_up"][l - 1]
    return {"pvec": np.ascontiguousarray(pvec), "lora": lora, "gup": gup, "bmix": bmix,
            "vdn": vdn, "vup": vup}


def build_nc(nseq, seq, depth, NT=128, dbg=None, stage=9, castmode=0):
    L = N_META + seq
    PADL = (-L) % 128
    LP = L + PADL
    assert PADL == 112 or True
    tiles = []
    p = 0
    while p < LP:
        n = min(NT, LP - p)
        tiles.append((p, n))
        p += n
    NB = LP // 128

    nc = bass.Bass("TRN2", target_bir_lowering=False)
    P = Prog(nc)
    DR = lambda ap: Reg(ap, None)

    def din(name, shape, dt=F32):
        return nc.dram_tensor(name, list(shape), dt, kind="ExternalInput").ap()

    x_d = din("x", [nseq, seq, D_MODEL])
    meta_d = din("meta", [N_META, D_MODEL])
    win_d = din("w_in", [depth, D_MODEL, 7168])
    wbr_d = din("w_branch", [depth, 1280, D_MODEL])
    wout_d = din("w_out", [depth, D_MODEL, D_MODEL])
    wup_d = din("w_ffn_up", [depth, D_MODEL, 2 * DFF])
    wdn_d = din("w_ffn_down", [depth, DFF, D_MODEL])
    NPV = PVL * depth + 8 + 2 * depth + 8 * depth
    pvec_d = din("pvec", [128, NPV])
    lora_d = din("lora", [depth, 128, 256])
    gup_d = din("gup", [depth, 128, 256])
    bmix_d = din("bmix", [depth, 128, 2, 128])
    vdn_d = din("vdn", [depth, 128, 2, 32])
    vup_d = din("vup", [depth, 32, 256])
    cshape = {"ident": [128, 128], "bd1": [128, 128], "bd64": [128, 128], "onesm": [128, 128],
              "mu64": [128, 128], "mui64": [128, 128], "ml64": [128, 128], "mui32": [128, 128],
              "rs64": [128, 256], "rs32": [128, 256], "swab": [128, 8, 256], "hm32": [128, 2],
              "poolt": [128, 2, 16]}
    c_d = {k: din("c_" + k, cshape[k]) for k in CONST_ORDER}
    out_d = nc.dram_tensor("out", [nseq, seq, D_MODEL], F32, kind="ExternalOutput").ap()

    def dscr(name, shape):
        return nc.dram_tensor(name, list(shape), BF16, kind="Internal").ap()
    win_s = [Reg(dscr("s_win%d" % l, [D_MODEL, 7168]), Buf("s_win%d" % l)) for l in range(depth)]
    wbr_s = [Reg(dscr("s_wbr%d" % l, [1280, D_MODEL]), win_s[l].buf) for l in range(depth)]
    wout_s = [Reg(dscr("s_wout%d" % l, [D_MODEL, D_MODEL]), win_s[l].buf) for l in range(depth)]
    wup_s = [Reg(dscr("s_wup%d" % l, [D_MODEL, 2 * DFF]), win_s[l].buf) for l in range(depth)]
    wdn_s = [Reg(dscr("s_wdn%d" % l, [DFF, D_MODEL]), win_s[l].buf) for l in range(depth)]

    cst = {k: P.tile("k_" + k, cshape[k], F32) for k in CONST_ORDER}
    cbuf = Buf("consts")
    for k in cst:
        cst[k].buf = cbuf
    pvec = P.tile("pvec_sb", [128, NPV], F32)
    pvec.buf = cbuf
    lora = [P.tile("lora%d" % l, [128, 256], BF16) for l in range(depth)]
    gup = [P.tile("gup%d" % l, [128, 256], BF16) for l in range(depth)]
    bmix = [P.tile("bmix%d" % l, [128, 2, 128], BF16) for l in range(depth)]
    vdn = [P.tile("vdn%d" % l, [128, 2, 32], BF16) for l in range(depth)]
    vup = [P.tile("vup%d" % l, [32, 256], BF16) for l in range(depth)]
    identb = P.tile("identb", [128, 128], BF16)
    mu64b = P.tile("mu64b", [128, 128], BF16)
    drv = P.tile("drv", [128, 64], F32)
    DV_L = 12
    esink = P.tile("esink", [128, depth, 8], F32)
    kcol = P.tile("kcol", [128, 8], F32)
    P.memset("dve", kcol[:, 0:1], 0.0)
    P.memset("dve", kcol[:, 1:2], 1.0)
    P.memset("dve", kcol[:, 2:3], 1e-6)
    P.memset("dve", kcol[:, 3:4], 64e-5)
    P.memset("dve", kcol[:, 4:5], 1e-24)
    P.memset("dve", kcol[:, 5:6], 1e-30)

    NWB = 3
    wb = [P.tile("wb%d" % i, [128, 8, 1024], BF16) for i in range(NWB)]
    wbi = [0]
    wlog = []
    outbufs = []
    dbg_outs = {}
    onesb = P.tile("onesb", [128, 64], BF16)
    P.memset("dve", onesb, 1.0)

    def make_stream(sid):
        T = lambda name, shape, dt: P.tile("%s_s%d" % (name, sid), shape, dt)
        TP = lambda name, shape, dt: P.psum("%s_s%d" % (name, sid), shape, dt)
        wcnt = [0]
        hT = T("hT", [128, 8, NT], F32)
        zb = T("zb", [128, 8, NT], BF16)
        mg = T("mg", [128, 8, NT], F32)
        mgb = zb
        vfirst = T("vfirst", [128, 2, NT], F32)
        u8 = T("u8", [128, 8, NT], F32)
        assert NT == 128
        xs = [Reg(u8.ap.rearrange("p a c -> p (a c)"), u8.buf)]
        xsi = [0]
        os_ = xs
        osi = xsi
        fa = [T("fa%d" % i, [128, 2, NT + 16], F32) for i in range(8)]
        ha = [T("ha%d" % i, [128, 2, NT], BF16) for i in range(8)]
        hq = T("hq", [128, 4, NT], BF16)
        actb = T("actb", [128, 22, NT], BF16)
        tokm = [T("tokm%d" % i, [128, NT // 128, 2, 128], BF16) for i in range(3)]
        NBK = NT // 128
        sc = {nm: T("sc_" + nm, [128, NBK, 4, 128], BF16) for nm in ("z", "ak", "rb", "rk")}
        nx = [T("nx%d" % i, [128, 4, 128], BF16) for i in range(2)]
        ny = [T("ny%d" % i, [128, 4, 128], BF16) for i in range(2)]
        nix = [T("nix%d" % i, [128, 4, 128], BF16) for i in range(2)]
        nz = [T("nz%d" % i, [128, 4, 128], BF16) for i in range(2)]
        rhsb = T("rhsb", [128, 4, 64], BF16)
        ub = T("ub", [128, 4, 64], BF16)
        wc = T("wc", [128, 2, NT // 32], F32)
        kd = T("kd", [128, 2, (NBK + 1) * 128], BF16)
        vtk = T("vtk", [128, NBK + 1, 128], BF16)
        pT = T("pT", [128, 8, 256], BF16)
        sTs = [T("sT%d" % i, [128, 2, 256], F32) for i in range(1)]
        ycT = T("ycT", [128, 4, NT], BF16)
        ubp = fa[2]
        pl = [fa[3], fa[4], fa[5]]
        car_sh = [T("car_sh%d" % l, [128, 8], F32) for l in range(depth)]
        stA = [T("stA%d" % l, [128, 2, 64], F32) for l in range(depth)]
        stAb = [T("stAb%d" % l, [128, 2, 64], BF16) for l in range(depth)]
        stD = [T("stD%d" % l, [128, 2, 64], F32) for l in range(depth)]
        stDb = [T("stDb%d" % l, [128, 2, 64], BF16) for l in range(depth)]
        car_p = [T("car_p%d" % l, [128, 2, 16], F32) for l in range(depth)]
        car_k = [T("car_k%d" % l, [128, 2, 128], BF16) for l in range(depth)]
        car_v = [T("car_v%d" % l, [128, 128], BF16) for l in range(depth)]

        psb = [TP("ps%d" % i, [128, 512], F32) for i in range(3)]
        _pa = TP("pa", [128, 512], F32)
        psacc = [_pa, _pa]
        psi = [0]

        def PS():
            r = psb[psi[0] % 3]
            psi[0] += 1
            return r

        def view(ps, ncols, a):
            return Reg(ps.ap[:, 0:ncols].rearrange("p (a c) -> p a c", a=a), ps.buf)

        class BC:
            def __init__(self, m):
                self.m = m

        def bc(m):
            return BC(m)

        _tt = P.tt

        def tt_b(eng, out, in0, in1, op):
            if isinstance(in1, BC):
                for a in range(4):
                    _tt(eng, out[:, a, :], in0[:, a, :], in1.m, op)
            else:
                _tt(eng, out, in0, in1, op)
        P.tt = tt_b


        def dump(name, reg, cond=True):
            if not dbg or not cond or name in dbg_outs:
                return
            shp = list(reg.ap.shape)
            d = nc.dram_tensor("dbg_" + name, shp, F32, kind="ExternalOutput").ap()
            dbg_outs[name] = d
            P.dma("pool", DR(d), reg, owner=reg.buf)

        def next_w(loader, ndma):
            if sid == 0:
                w = wb[wbi[0] % NWB]
                wbi[0] += 1
                loader(w)
                wlog.append(w)
            else:
                w = wlog[wcnt[0]]
                for _ in range(ndma):
                    P.nop()
            wcnt[0] += 1
            return w

        def load_w(src_reg_ap, nkc, ncols, buf_owner, parts=None):
            return next_w(lambda w: P.dma("sp", w[:, 0:nkc, 0:ncols],
                                          Reg(src_reg_ap.rearrange("(k p) c -> p k c", p=128), buf_owner)), 1)

        def rmsnorm(l_gcol, n, zout):
            sq = u8
            for c in range(8):
                if c % 2 == 0:
                    P.act(sq[:, c, 0:n], hT[:, c, 0:n], AF.Square)
                else:
                    P.tt("pool", sq[:, c, 0:n], hT[:, c, 0:n], hT[:, c, 0:n], ALU.mult)
            ps = PS()
            for c in range(8):
                P.mm(ps[:, 0:n], cst["onesm"], sq[:, c, 0:n], start=(c == 0), stop=(c == 7))
            rs = fa[7][:, 0, 0:n]
            P.act(rs, ps[:, 0:n], AF.Sqrt, bias=kcol[:, 2:3])
            P.recip(rs, rs)
            for c in range(8):
                P.stt("dve", zout[:, c, 0:n], hT[:, c, 0:n], pvec[:, l_gcol + c:l_gcol + c + 1],
                      rs, ALU.mult, ALU.mult)

        def proj(w, kcs, col0, rhs_fn, n):
            ps = PS()
            for i, k in enumerate(kcs):
                P.mm(ps[:, 0:n], w[:, i, col0:col0 + 128], rhs_fn(k), start=(i == 0), stop=(i == len(kcs) - 1))
            return ps

        def gate_merge(l, bi, n, gw, ysrc, wrows, first):
            nk = len(wrows)
            wbw = load_w(wbr_s[l].ap[wrows[0] * 128:(wrows[-1] + 1) * 128, :], nk, 1024, win_s[l].buf)
            for c in range(8):
                pg = proj(gw, range(8), c * 128, lambda k: zb[:, k, 0:n], n)
                g = fa[6][:, c % 2, 0:n]
                P.act(g, pg[:, 0:n], AF.Sigmoid)
                pp = proj(wbw, range(nk), c * 128, ysrc, n)
                if first:
                    P.tt("dve", mg[:, c, 0:n], pp[:, 0:n], g, ALU.mult)
                else:
                    P.tt("dve", g, pp[:, 0:n], g, ALU.mult)
                    P.tt("pool", mg[:, c, 0:n], mg[:, c, 0:n], g, ALU.add)

        def branch_A(l, n, first_tile):
            nb = n // 128
            wA = load_w(win_s[l].ap[:, OFF_A:OFF_A + 1024], 8, 1024, win_s[l].buf)
            u = u8
            om = lambda c: drv[:, DV_L * l + c:DV_L * l + c + 1]
            mu = lambda c: pvec[:, PVL * l + 16 + c:PVL * l + 17 + c]
            for c in range(8):
                ps = proj(wA, range(8), c * 128, lambda k: zb[:, k, 0:n], n)
                P.act(u[:, c, 0:n], ps[:, 0:n], AF.Identity, scale=om(c))
                P.stt("dve", u[:, c, 1:n], ps[:, 0:n - 1], mu(c), u[:, c, 1:n], ALU.mult, ALU.add)
                P.stt("dve", u[:, c, 0:1], car_sh[l][:, c:c + 1], mu(c), u[:, c, 0:1], ALU.mult, ALU.add)
                P.copy("dve", car_sh[l][:, c:c + 1], ps[:, n - 1:n])
            if stage == 3.1:
                return ha[7]
            r = lambda hc: u[:, hc, 0:n]
            k = lambda hc: u[:, 2 + hc, 0:n]
            v = lambda hc: u[:, 4 + hc, 0:n]
            twd, adb, sg = ha[0][:, 0, 0:n], ha[0][:, 1, 0:n], ha[1][:, 0, 0:n]
            P.act(twd[0:64], u[0:64, 6, 0:n], AF.Tanh)
            P.copy("dve", adb[64:128], u[64:128, 6, 0:n])
            P.act(sg, u[:, 7, 0:n], AF.Sigmoid)
            vb = ha[1]
            if l == 0:
                for hc in range(2):
                    P.copy("pool", vfirst[:, hc, 0:n], v(hc))
                    P.copy("act", ha[2][:, hc, 0:n], v(hc))
            else:
                for hc in range(2):
                    P.copy("dve", ha[2][:, hc, 0:n], v(hc))
                ps = PS()
                for hc in range(2):
                    P.mm(ps[0:32, 0:n], vdn[l][:, hc, :], ha[2][:, hc, 0:n], start=(hc == 0), stop=(hc == 1))
                lo = ha[3][:, 0, 0:n]
                P.copy("dve", lo[0:32], ps[0:32, 0:n])
                for hc in range(2):
                    ps = PS()
                    P.mm(ps[:, 0:n], vup[l][:, hc * 128:(hc + 1) * 128], lo[0:32])
                    sgm = fa[0][:, hc, 0:n]
                    P.act(sgm, ps[:, 0:n], AF.Sigmoid, bias=pv(l, "vres0", hc))
                    d = fa[1][:, hc, 0:n]
                    P.tt("dve", d, vfirst[:, hc, 0:n], v(hc), ALU.subtract)
                    P.tt("dve", d, d, sgm, ALU.mult)
                    P.tt("dve", v(hc), v(hc), d, ALU.add)
                    P.copy("act", ha[2][:, hc, 0:n], v(hc))
            vbf = ha[2]
            S_, CUM, A_, KKN, KF, E_ = fa[0], fa[1], fa[2], fa[3], fa[4], fa[5]
            rt, kt, bt, at = ha[3], ha[4], ha[5], ha[6]
            for hc in range(2):
                cs = slice(hc * 128, (hc + 1) * 128)
                ps = PS()
                P.mm(ps[:, 0:n], lora[l][0:64, cs], twd[0:64])
                P.act(S_[:, hc, 0:n], ps[:, 0:n], AF.Sigmoid, bias=pv(l, "w0", hc))
                ps = PS()
                P.mm(ps[:, 0:n], lora[l][64:128, cs], adb[64:128])
                P.act(A_[:, hc, 0:n], ps[:, 0:n], AF.Sigmoid, bias=pv(l, "a0", hc))
                P.scan(CUM[:, hc, 0:n], cst["rs64"][:, 0:n], S_[:, hc, 0:n])
                P.ts("dve", KKN[:, hc, 0:n], k(hc), pv(l, "kk", hc), None, ALU.mult)
                P.act(E_[:, hc, 0:n], KKN[:, hc, 0:n], AF.Square)
                ps = PS()
                P.mm(ps[:, 0:n], cst["bd1"], E_[:, hc, 0:n])
                P.ts("dve", E_[:, hc, 0:n], ps[:, 0:n], 1e-24, None, ALU.max)
                P.act(E_[:, hc, 0:n], E_[:, hc, 0:n], AF.Sqrt)
                P.recip(E_[:, hc, 0:n], E_[:, hc, 0:n])
                P.tt("dve", KKN[:, hc, 0:n], KKN[:, hc, 0:n], E_[:, hc, 0:n], ALU.mult)
                P.ts("dve", KF[:, hc, 0:n], A_[:, hc, 0:n], -1.0, pv(l, "ka", hc), ALU.add, ALU.mult)
                P.stt("dve", KF[:, hc, 0:n], KF[:, hc, 0:n], 1.0, k(hc), ALU.add, ALU.mult)
                P.act(E_[:, hc, 0:n], CUM[:, hc, 0:n], AF.Exp, scale=-C0)
                P.tt("dve", rt[:, hc, 0:n], r(hc), E_[:, hc, 0:n], ALU.mult)
                for j in range(n // 64):
                    P.copy("pool", wc[:, hc, j:j + 1], E_[:, hc, j * 64 + 63:j * 64 + 64])
                P.act(E_[:, hc, 0:n], CUM[:, hc, 0:n], AF.Exp, scale=C0)
                P.tt("dve", kt[:, hc, 0:n], KF[:, hc, 0:n], E_[:, hc, 0:n], ALU.mult)
                P.tt("pool", A_[:, hc, 0:n], A_[:, hc, 0:n], KKN[:, hc, 0:n], ALU.mult)
                P.tt("dve", bt[:, hc, 0:n], A_[:, hc, 0:n], E_[:, hc, 0:n], ALU.mult)
                P.tt("dve", S_[:, hc, 0:n], CUM[:, hc, 0:n], S_[:, hc, 0:n], ALU.subtract)
                P.act(E_[:, hc, 0:n], S_[:, hc, 0:n], AF.Exp, scale=-C0)
                P.stt("dve", at[:, hc, 0:n], KKN[:, hc, 0:n], -1.0, E_[:, hc, 0:n], ALU.mult, ALU.mult)
                P.stt("dve", S_[:, hc, 0:n], r(hc), pv(l, "rk", hc), KF[:, hc, 0:n], ALU.mult, ALU.mult)
                ps = PS()
                P.mm(ps[:, 0:n], cst["bd1"], S_[:, hc, 0:n])
                P.tt("dve", CUM[:, hc, 0:n], ps[:, 0:n], v(hc), ALU.mult)
                ps = PS()
                P.mm(ps[:, 0:n], gup[l][:, cs], sg)
                P.copy("act", A_[:, hc, 0:n], ps[:, 0:n])
            BON, G_ = CUM, A_
            if stage == 3.2:
                return ha[7]
            for (src, dst) in ((vbf, tokm[0]), (bt, tokm[1]), (kt, tokm[2])):
                for b in range(nb):
                    ps = PS()
                    for hc in range(2):
                        P.mm(ps[:, hc * 128:(hc + 1) * 128], src[:, hc, b * 128:(b + 1) * 128], identb)
                    P.copy("act", dst[:, b, :, :], view(ps, 256, 2))
            VT, BT, KT = tokm
            if stage == 3.3:
                return ha[7]
            for b in range(nb):
                bs = slice(b * 128, (b + 1) * 128)
                hr = lambda hh: slice((hh % 2) * 64, (hh % 2) * 64 + 64)

                def scores(lhs, rhs):
                    ps = PS()
                    for hh in range(4):
                        P.mm(ps[:, hh * 128:(hh + 1) * 128], lhs[hr(hh), hh // 2, bs], rhs[hr(hh), hh // 2, bs])
                    return view(ps, 512, 4)
                X, Y, IX, Z = nx[0], ny[0], nix[0], nz[0]
                p1 = scores(bt, at)
                P.tt("dve", Y, p1, bc(cst["mu64"]), ALU.mult)
                if stage == 3.41:
                    continue
                p2 = scores(at, bt)
                P.tt("dve", X, p2, bc(cst["ml64"]), ALU.mult)
                P.tt("pool", Z, Y, bc(identb), ALU.add)
                p3 = scores(kt, at)
                P.tt("dve", sc["ak"][:, b], p3, bc(cst["mu64"]), ALU.mult)
                p4 = scores(bt, rt)
                P.tt("dve", sc["rb"][:, b], p4, bc(cst["mui64"]), ALU.mult)
                p5 = scores(kt, rt)
                P.tt("dve", sc["rk"][:, b], p5, bc(cst["mui64"]), ALU.mult)
                if stage == 3.42:
                    continue
                cur = 0
                for lev in range(5 if stage != 3.43 else 1):
                    nxt = 1 - cur
                    Xc, Yc, Zc = nx[cur], ny[cur], nz[cur]
                    Xn, Yn, IXn, Zn = nx[nxt], ny[nxt], nix[nxt], nz[nxt]
                    ps = PS()
                    for hh in range(4):
                        P.mm(ps[:, hh * 128:(hh + 1) * 128], Yc[:, hh, :], Xc[:, hh, :])
                    px = view(ps, 512, 4)
                    if lev < 4:
                        P.copy("act", Xn, px)
                    P.tt("dve", IXn, px, bc(cst["ident"]), ALU.add)
                    if lev < 4:
                        ps = PS()
                        for hh in range(4):
                            P.mm(ps[:, hh * 128:(hh + 1) * 128], Xc[:, hh, :], Yc[:, hh, :])
                        py = view(ps, 512, 4)
                        P.copy("act", Yn, py)
                    ps = PS()
                    for hh in range(4):
                        P.mm(ps[:, hh * 128:(hh + 1) * 128], IXn[:, hh, :], Zc[:, hh, :])
                    pz = view(ps, 512, 4)
                    if lev < 4:
                        P.copy("dve", Zn, pz)
                    else:
                        P.copy("dve", sc["z"][:, b], pz)
                    cur = nxt
            if 3.4 <= stage < 3.45:
                return ha[7]
            ST, STb = stA[l], stAb[l]
            OT = psacc[0]
            for ci in range(n // 64):
                b, c = ci // 2, ci % 2
                cr = slice(c * 64, c * 64 + 64)
                tk = slice(ci * 64, ci * 64 + 64)
                hr = lambda hh: slice((hh % 2) * 64, (hh % 2) * 64 + 64)
                ps1 = PS()
                for hh in range(4):
                    hc = hh // 2
                    P.mm(ps1[cr, hh * 64:(hh + 1) * 64], at[hr(hh), hc, tk], STb[hr(hh), hc, :], start=True, stop=False)
                    P.mm(ps1[cr, hh * 64:(hh + 1) * 64], sc["ak"][cr, b, hh, cr], VT[cr, b, hc, hr(hh)], start=False, stop=True)
                P.copy("dve", rhsb[cr], Reg(ps1.ap[cr, 0:256].rearrange("p (a c) -> p a c", a=4), ps1.buf))
                if stage == 3.45:
                    continue
                ps2 = PS()
                for hh in range(4):
                    P.mm(ps2[cr, hh * 64:(hh + 1) * 64], sc["z"][cr, b, hh, cr], rhsb[cr, hh, :])
                P.copy("act", ub[cr], Reg(ps2.ap[cr, 0:256].rearrange("p (a c) -> p a c", a=4), ps2.buf))
                if stage == 3.46:
                    continue
                for hh in range(4):
                    hc = hh // 2
                    oo = OT[hr(hh), hc * 256 + ci * 64:hc * 256 + ci * 64 + 64]
                    P.mm(oo, STb[hr(hh), hc, :], rt[hr(hh), hc, tk], start=True, stop=False)
                    P.mm(oo, ub[cr, hh, :], sc["rb"][cr, b, hh, cr], start=False, stop=False)
                    P.mm(oo, VT[cr, b, hc, hr(hh)], sc["rk"][cr, b, hh, cr], start=False, stop=True)
                if stage == 3.47:
                    continue
                ps3 = PS()
                for hh in range(4):
                    hc = hh // 2
                    so = ps3[hr(hh), hc * 64:(hc + 1) * 64]
                    P.mm(so, BT[cr, b, hc, hr(hh)], ub[cr, hh, :], start=True, stop=False)
                    P.mm(so, KT[cr, b, hc, hr(hh)], VT[cr, b, hc, hr(hh)], start=False, stop=True)
                for hc in range(2):
                    P.tt("dve", ST[:, hc, :], ST[:, hc, :], ps3[:, hc * 64:(hc + 1) * 64], ALU.add)
                    P.ts("dve", ST[:, hc, :], ST[:, hc, :], wc[:, hc, ci:ci + 1], None, ALU.mult)
                    P.copy("act", STb[:, hc, :], ST[:, hc, :])
            if 3.45 <= stage <= 3.5:
                return ha[7]
            yA = ha[7]
            for hc in range(2):
                o = fa[0][:, hc, 0:n]
                P.copy("act", o, OT[:, hc * 256:hc * 256 + n])
                sq = fa[3][:, hc, 0:n]
                P.act(sq, OT[:, hc * 256:hc * 256 + n], AF.Square)
                pm = PS()
                P.mm(pm[:, 0:n], cst["bd64"], o)
                pq = PS()
                P.mm(pq[:, 0:n], cst["bd64"], sq)
                d = fa[4][:, hc, 0:n]
                P.tt("dve", d, o, pm[:, 0:n], ALU.subtract)
                m2 = fa[5][:, hc, 0:n]
                P.act(m2, pm[:, 0:n], AF.Square)
                P.tt("dve", m2, pq[:, 0:n], m2, ALU.subtract)
                P.ts("dve", m2, m2, 0.0, None, ALU.max)
                P.act(m2, m2, AF.Sqrt, bias=kcol[:, 3:4])
                P.recip(m2, m2)
                P.tt("dve", d, d, m2, ALU.mult)
                P.ts("dve", d, d, pv(l, "lnw", hc), pv(l, "lnb", hc), ALU.mult, ALU.add)
                P.tt("pool", d, d, BON[:, hc, 0:n], ALU.add)
                P.tt("dve", yA[:, hc, 0:n], d, G_[:, hc, 0:n], ALU.mult)
            return yA

        def branch_B(l, n, wBC, first_tile):
            u = ubp
            for c in range(2):
                P.copy("pool", u[:, c, 0:16], car_p[l][:, c, :])
                ps = proj(wBC, range(8), c * 128, lambda k: zb[:, k, 0:n], n)
                P.copy("act", u[:, c, 16:16 + n], ps[:, 0:n])
                P.copy("pool", car_p[l][:, c, :], u[:, c, n:n + 16])
            W = 16 + n
            s2, s4, s8 = pl
            yB = ha[0]
            for c in range(2):
                P.tt("dve", s2[:, c, 1:W], u[:, c, 1:W], u[:, c, 0:W - 1], ALU.add)
                P.tt("dve", s4[:, c, 3:W], s2[:, c, 3:W], s2[:, c, 1:W - 2], ALU.add)
                if c == 0:
                    lo, hi = s2, s4
                else:
                    P.tt("dve", s8[:, c, 7:W], s4[:, c, 7:W], s4[:, c, 3:W - 4], ALU.add)
                    P.tt("dve", s2[:, c, 15:W], s8[:, c, 15:W], s8[:, c, 7:W - 8], ALU.add)
                    lo, hi = s8, s2
                pool_ = s4 if c == 1 else s8
                for (src, pr) in ((lo, slice(0, 64)), (hi, slice(64, 128))):
                    P.stt("dve", pool_[pr, c, 16:W], src[pr, c, 16:W], cst["poolt"][pr, c, 15:16], u[pr, c, 16:W],
                          ALU.mult, ALU.subtract)
                    if first_tile:
                        fs = slice(16 + PADL, 32 + PADL)
                        P.tt("dve", pool_[pr, c, fs], src[pr, c, fs], cst["poolt"][pr, c, :], ALU.mult)
                        P.tt("dve", pool_[pr, c, fs], pool_[pr, c, fs], u[pr, c, fs], ALU.subtract)
                pb = ha[1][:, c, 0:n]
                P.copy("act", pb, pool_[:, c, 16:W])
                ps = PS()
                P.mm(ps[:, 0:n], bmix[l][:, c, :], pb)
                P.ts("dve", yB[:, c, 0:n], ps[:, 0:n], pv(l, "bscale", c), None, ALU.mult)
            return yB

        def branch_C(l, n, wBC, wKV, gb0):
            nb = n // 128
            for c in range(4):
                ps = proj(wBC, range(8), 256 + c * 128, lambda k: zb[:, k, 0:n], n)
                P.copy("act" if c % 2 else "dve", hq[:, c, 0:n], ps[:, 0:n])
            for g in range(2):
                P.copy("pool", kd[:, g, 0:128], car_k[l][:, g, :])
                ps = proj(wKV, range(8), g * 128, lambda k: zb[:, k, 0:n], n)
                P.copy("act", kd[:, g, 128:128 + n], ps[:, 0:n])
                P.copy("pool", car_k[l][:, g, :], kd[:, g, n:n + 128])
            P.copy("pool", vtk[:, 0, :], car_v[l])
            for b in range(nb):
                ps = PS()
                for k_ in range(8):
                    P.mm(ps[:, 0:128], zb[:, k_, b * 128:(b + 1) * 128], wKV[:, k_, 256:384], start=(k_ == 0), stop=(k_ == 7))
                P.copy("act", vtk[:, b + 1, :], ps[:, 0:128])
            P.copy("pool", car_v[l], vtk[:, nb, :])
            for b in range(nb):
                qs = slice(b * 128, (b + 1) * 128)
                gb = gb0 + b
                for hp in range(4):
                    ps = PS()
                    for hh2 in range(2):
                        h = hp * 2 + hh2
                        g = h // 4
                        rows = slice((h % 2) * 64, (h % 2) * 64 + 64)
                        for part in range(2):
                            P.mm(ps[:, hh2 * 256 + part * 128:hh2 * 256 + part * 128 + 128],
                                 kd[rows, g, (b + part) * 128:(b + part + 1) * 128], hq[rows, h // 2, qs])
                    pv_ = view(ps, 512, 2)
                    sT = sTs[0]
                    P.stt("dve", sT, pv_, 0.125, cst["swab"][:, hp * 2:hp * 2 + 2, :], ALU.mult, ALU.add)
                    if gb == 0:
                        P.memset("pool", sT[:, :, 0:128], NEG)
                        if PADL:
                            P.memset("pool", sT[0:PADL, :, 128:256], NEG)
                    elif gb == 1 and PADL:
                        P.memset("pool", sT[0:PADL, :, 0:128], NEG)
                    P.act(pT[:, hp * 2:hp * 2 + 2, :], sT, AF.Exp)
                for hc in range(4):
                    po = PS()
                    for hh2 in range(2):
                        h = hc * 2 + hh2
                        g = h // 4
                        rows = slice(hh2 * 64, hh2 * 64 + 64)
                        for part in range(2):
                            P.mm(po[rows, 0:128], vtk[:, b + part, g * 64:(g + 1) * 64], pT[:, h, part * 128:(part + 1) * 128],
                                 start=(part == 0), stop=(part == 1))
                        for part in range(2):
                            P.mm(po[rows, 128:256], onesb[:, 0:64], pT[:, h, part * 128:(part + 1) * 128],
                                 start=(part == 0), stop=(part == 1))
                    den = fa[0][:, 0, 0:128]
                    for hh2 in range(2):
                        h = hc * 2 + hh2
                        rows = slice(hh2 * 64, hh2 * 64 + 64)
                        P.ts("dve", den[rows], po[rows, 128:256], esink[rows, l, h:h + 1], None, ALU.add)
                    P.recip(den, den)
                    P.tt("dve", ycT[:, hc, qs], po[:, 0:128], den, ALU.mult)
            return ycT

        def branch_D(l, n, first_tile):
            nb = n // 128
            wD = load_w(win_s[l].ap[:, OFF_D:OFF_D + 1024], 8, 1024, win_s[l].buf)
            u = u8
            for c in range(8):
                ps = proj(wD, range(8), c * 128, lambda k: zb[:, k, 0:n], n)
                if c < 2:
                    P.act(u[:, c, 0:n], ps[:, 0:n], AF.Silu)
                elif c < 4:
                    P.copy("dve", u[:, c, 0:n], ps[:, 0:n])
                elif c < 6:
                    P.copy("act", ha[0][:, c - 4, 0:n], ps[:, 0:n])
                else:
                    P.act(u[:, c, 0:n], ps[:, 0:n], AF.Silu)
            vb = ha[0]
            lb = lambda hc: drv[:, DV_L * l + 8 + hc:DV_L * l + 9 + hc]
            omlb = lambda hc: drv[:, DV_L * l + 10 + hc:DV_L * l + 11 + hc]
            SG, LF, B_, E_, KK = fa[0], fa[1], fa[2], fa[3], fa[4]
            qt, kt = ha[1], ha[2]
            for hc in range(2):
                fpre = u[:, 2 + hc, 0:n]
                P.act(SG[:, hc, 0:n], fpre, AF.Sigmoid)
                P.ts("dve", LF[:, hc, 0:n], SG[:, hc, 0:n], omlb(hc), lb(hc), ALU.mult, ALU.add)
                P.ts("dve", LF[:, hc, 0:n], LF[:, hc, 0:n], 1e-30, None, ALU.max)
                P.act(LF[:, hc, 0:n], LF[:, hc, 0:n], AF.Ln)
                P.scan(B_[:, hc, 0:n], cst["rs32"][:, 0:n], LF[:, hc, 0:n])
                P.act(KK[:, hc, 0:n], fpre, AF.Sigmoid, scale=-1.0)
                P.ts("dve", KK[:, hc, 0:n], KK[:, hc, 0:n], omlb(hc), None, ALU.mult)
                P.act(E_[:, hc, 0:n], B_[:, hc, 0:n], AF.Exp)
                P.tt("dve", qt[:, hc, 0:n], u[:, hc, 0:n], E_[:, hc, 0:n], ALU.mult)
                for j in range(n // 32):
                    P.copy("pool", wc[:, hc, j:j + 1], E_[:, hc, j * 32 + 31:j * 32 + 32])
                P.act(E_[:, hc, 0:n], B_[:, hc, 0:n], AF.Exp, scale=-1.0)
                P.tt("dve", kt[:, hc, 0:n], KK[:, hc, 0:n], E_[:, hc, 0:n], ALU.mult)
            for b in range(nb):
                ps = PS()
                for hc in range(2):
                    P.mm(ps[:, hc * 128:(hc + 1) * 128], vb[:, hc, b * 128:(b + 1) * 128], identb)
                P.copy("act", tokm[0][:, b, :, :], view(ps, 256, 2))
                ps = PS()
                for hc in range(2):
                    P.mm(ps[:, hc * 128:(hc + 1) * 128], kt[:, hc, b * 128:(b + 1) * 128], identb)
                P.ts("dve", tokm[1][:, b, :, :], view(ps, 256, 2), cst["hm32"][:, 0:1], None, ALU.mult)
                P.ts("dve", tokm[2][:, b, :, :], view(ps, 256, 2), cst["hm32"][:, 1:2], None, ALU.mult)
            VT = tokm[0]
            hr = lambda hh: slice((hh % 2) * 64, (hh % 2) * 64 + 64)
            for b in range(nb):
                bs = slice(b * 128, (b + 1) * 128)
                ps = PS()
                for hh in range(4):
                    P.mm(ps[:, hh * 128:(hh + 1) * 128], kt[hr(hh), hh // 2, bs], qt[hr(hh), hh // 2, bs])
                P.tt("dve", sc["ak"][:, b], view(ps, 512, 4), bc(cst["mui32"]), ALU.mult)
            ST, STb = stD[l], stDb[l]
            OT = psacc[1]
            for ci in range(n // 32):
                b, c = ci // 4, ci % 4
                pr = slice((c // 2) * 64, (c // 2) * 64 + 64)
                cc = slice(c * 32, c * 32 + 32)
                tk = slice(ci * 32, ci * 32 + 32)
                KT = tokm[1 + (c % 2)]
                for hh in range(4):
                    hc = hh // 2
                    oo = OT[hr(hh), hc * 256 + ci * 32:hc * 256 + ci * 32 + 32]
                    P.mm(oo, STb[hr(hh), hc, :], qt[hr(hh), hc, tk], start=True, stop=False)
                    P.mm(oo, VT[pr, b, hc, hr(hh)], sc["ak"][pr, b, hh, cc], start=False, stop=True)
                ps3 = PS()
                for hh in range(4):
                    hc = hh // 2
                    P.mm(ps3[hr(hh), hc * 64:(hc + 1) * 64], KT[pr, b, hc, hr(hh)], VT[pr, b, hc, hr(hh)])
                for hc in range(2):
                    P.tt("dve", ST[:, hc, :], ST[:, hc, :], ps3[:, hc * 64:(hc + 1) * 64], ALU.add)
                    P.ts("dve", ST[:, hc, :], ST[:, hc, :], wc[:, hc, ci:ci + 1], None, ALU.mult)
                    P.copy("act", STb[:, hc, :], ST[:, hc, :])
            yD = ha[7]
            for hc in range(2):
                o = fa[0][:, hc, 0:n]
                P.copy("act", o, OT[:, hc * 256:hc * 256 + n])
                sq = fa[1][:, hc, 0:n]
                P.act(sq, OT[:, hc * 256:hc * 256 + n], AF.Square)
                pq = PS()
                P.mm(pq[:, 0:n], cst["bd64"], sq)
                rs = fa[2][:, hc, 0:n]
                P.act(rs, pq[:, 0:n], AF.Sqrt, bias=kcol[:, 2:3])
                P.recip(rs, rs)
                P.tt("dve", o, o, rs, ALU.mult)
                P.stt("dve", yD[:, hc, 0:n], o, pv(l, "dnorm", hc), u[:, 6 + hc, 0:n], ALU.mult, ALU.mult)
            return yD


        def layer(l, n, first_tile, c0, gb0):
            layer_(l, n, first_tile, c0, gb0)
            dump("h2", hT, first_tile and l == 0)

        def layer_(l, n, first_tile, c0, gb0):
            dcond = first_tile and l == 0
            dump("h0", hT, dcond)
            rmsnorm(PVL * l + 0, n, zb)
            dump("z", zb, dcond)
            if stage == 2:
                return
            yA = branch_A(l, n, first_tile)
            dump("yA", yA, dcond)
            if 3 <= stage < 4:
                return
            gw = load_w(win_s[l].ap[:, 0:1024], 8, 1024, win_s[l].buf)
            gate_merge(l, 0, n, gw, lambda k: yA[:, k, 0:n], [0, 1], True)
            wBC = load_w(win_s[l].ap[:, OFF_B:OFF_B + 1024], 8, 1024, win_s[l].buf)
            kv_src = win_s[l].ap

            def _ld_kv(w):
                for g in range(2):
                    for dup in range(2):
                        P.dma("sp", w[:, :, g * 128 + dup * 64:g * 128 + dup * 64 + 64],
                              Reg(kv_src[:, OFF_C + 512 + g * 64:OFF_C + 512 + g * 64 + 64].rearrange("(k p) c -> p k c", p=128), win_s[l].buf))
                P.dma("sp", w[:, :, 256:384], Reg(kv_src[:, OFF_C + 640:OFF_C + 768].rearrange("(k p) c -> p k c", p=128), win_s[l].buf))
            wKV = next_w(_ld_kv, 5)
            if stage == 4:
                return
            yB = branch_B(l, n, wBC, first_tile)
            dump("yB", yB, dcond)
            if stage == 5:
                return
            yC = branch_C(l, n, wBC, wKV, gb0)
            dump("yC", yC, dcond)
            if stage == 6:
                return
            gw = load_w(win_s[l].ap[:, 1024:2048], 8, 1024, win_s[l].buf)
            gate_merge(l, 1, n, gw, lambda k: yB[:, k, 0:n], [2, 3], False)
            gw = load_w(win_s[l].ap[:, 2048:3072], 8, 1024, win_s[l].buf)
            gate_merge(l, 2, n, gw, lambda k: yC[:, k, 0:n], [4, 5, 6, 7], False)
            yD = branch_D(l, n, first_tile)
            dump("yD", yD, dcond)
            if stage == 7:
                return
            gw = load_w(win_s[l].ap[:, 3072:4096], 8, 1024, win_s[l].buf)
            gate_merge(l, 3, n, gw, lambda k: yD[:, k, 0:n], [8, 9], False)
            dump("mg", mg, dcond)
            for c in range(8):
                P.copy("act" if c % 2 else "pool", mgb[:, c, 0:n], mg[:, c, 0:n])
            wo = load_w(wout_s[l].ap, 8, 1024, win_s[l].buf)
            for c in range(8):
                ps = proj(wo, range(8), c * 128, lambda k: mgb[:, k, 0:n], n)
                P.tt("dve", hT[:, c, c0:n], hT[:, c, c0:n], ps[:, c0:n], ALU.add)
            dump("h1", hT, dcond)
            if stage == 8:
                return
            rmsnorm(PVL * l + 8, n, zb)
            for j0 in range(0, 22, 4):
                nj = min(4, 22 - j0)
                def _ld_up(w, j0=j0, nj=nj):
                    P.dma("sp", w[:, :, 0:nj * 128], Reg(wup_s[l].ap[:, j0 * 128:(j0 + nj) * 128].rearrange("(k p) c -> p k c", p=128), win_s[l].buf))
                    P.dma("sp", w[:, :, 512:512 + nj * 128], Reg(wup_s[l].ap[:, DFF + j0 * 128:DFF + (j0 + nj) * 128].rearrange("(k p) c -> p k c", p=128), win_s[l].buf))
                w = next_w(_ld_up, 2)
                for j in range(nj):
                    pg = proj(w, range(8), j * 128, lambda k: zb[:, k, 0:n], n)
                    sg = fa[6][:, j % 2, 0:n]
                    P.act(sg, pg[:, 0:n], AF.Silu)
                    pu = proj(w, range(8), 512 + j * 128, lambda k: zb[:, k, 0:n], n)
                    P.tt("dve", actb[:, j0 + j, 0:n], pu[:, 0:n], sg, ALU.mult)
            for g0 in range(0, 22, 8):
                ng = min(8, 22 - g0)
                w = load_w(wdn_s[l].ap[g0 * 128:(g0 + ng) * 128, :], ng, 1024, win_s[l].buf)
                for c in range(8):
                    ps = proj(w, range(ng), c * 128, lambda k: actb[:, g0 + k, 0:n], n)
                    P.tt("dve", hT[:, c, 0:n], hT[:, c, 0:n], ps[:, 0:n], ALU.add)

        def run_seq(s):
            for l in range(depth):
                P.memset("pool", car_sh[l], 0.0)
                P.memset("pool", stA[l], 0.0)
                P.memset("pool", stAb[l], 0.0)
                P.memset("pool", stD[l], 0.0)
                P.memset("pool", stDb[l], 0.0)
                P.memset("pool", car_p[l], 0.0)
                P.memset("pool", car_k[l], 0.0)
                P.memset("pool", car_v[l], 0.0)
            for ti, (p0, n) in enumerate(tiles):
                first_tile = ti == 0
                nb = n // 128
                for b in range(nb):
                    gb = p0 // 128 + b
                    xt = xs[0]
                    xsi[0] += 1
                    if gb == 0:
                        P.memset("pool", xt, 0.0)
                        P.dma("sp", xt[PADL:128, :], DR(meta_d))
                    else:
                        P.dma("sp", xt, DR(x_d[s, (gb - 1) * 128:gb * 128, :]))
                    for c in range(8):
                        ps = PS()
                        P.mm(ps[:, 0:128], xt[:, c * 128:(c + 1) * 128], cst["ident"])
                        P.copy("act" if c % 2 else "dve", hT[:, c, b * 128:(b + 1) * 128], ps[:, 0:128])
                c0 = PADL if first_tile else 0
                for l in range(depth if stage >= 2 else 0):
                    layer(l, n, first_tile, c0, p0 // 128)
                if dbg:
                    pass
                rmsnorm(GO, n, mg)
                for b in range(nb):
                    gb = p0 // 128 + b
                    if gb == 0:
                        continue
                    ot = os_[0]
                    osi[0] += 1
                    for c in range(8):
                        ps = PS()
                        P.mm(ps[:, 0:128], mg[:, c, b * 128:(b + 1) * 128], cst["ident"])
                        P.copy("act" if c % 2 else "dve", ot[:, c * 128:(c + 1) * 128], ps[:, 0:128])
                    P.dma("sp", DR(out_d[s, (gb - 1) * 128:gb * 128, :]), ot, owner=ot.buf)
                    outbufs.append(ot.buf)
        return locals()

    streams = [make_stream(i) for i in range(nseq)]
    S0 = streams[0]
    fa = S0["fa"]
    _mg0 = S0["mg"]
    st32 = Reg(_mg0.ap[:, 0:4, :].rearrange("p (a b) c -> p a (b c)", a=2), _mg0.buf)
    st32b = Reg(_mg0.ap[:, 4:8, :].rearrange("p (a b) c -> p a (b c)", a=2), _mg0.buf)
    for k in CONST_ORDER:
        P.dma("sp", cst[k], DR(c_d[k]), owner=cbuf)
    P.dma("sp", pvec, DR(pvec_d), owner=cbuf)
    P.copy("dve", identb, cst["ident"])
    P.copy("dve", mu64b, cst["mu64"])
    for l in range(depth):
        for (dst, src, shp) in ((lora[l], lora_d[l], None), (gup[l], gup_d[l], None)):
            P.dma("sp", st32[:, 0, :], DR(src))
            P.copy("dve", dst, st32[:, 0, :])
        P.dma("sp", st32b[:, :, 0:128], DR(bmix_d[l]))
        P.copy("dve", bmix[l], st32b[:, :, 0:128])
        P.dma("sp", st32[:, :, 0:32], DR(vdn_d[l]))
        P.copy("dve", vdn[l], st32[:, :, 0:32])
        P.dma("sp", st32b[0:32, 0, :], DR(vup_d[l]))
        P.copy("dve", vup[l], st32b[0:32, 0, :])
    for l in range(depth):
        for (dst, src, rows) in ((win_s[l], win_d[l], D_MODEL), (wbr_s[l], wbr_d[l], 1280),
                                 (wout_s[l], wout_d[l], D_MODEL), (wup_s[l], wup_d[l], D_MODEL),
                                 (wdn_s[l], wdn_d[l], DFF)):
            step = 256
            for r0 in range(0, rows, step):
                r1 = min(rows, r0 + step)
                P.dma("pool", Reg(dst.ap[r0:r1, :], None), DR(src[r0:r1, :]), owner=win_s[l].buf)
        win_s[l].buf.w = (win_s[l].buf, win_s[l].buf.dval)

    def pv(l, nm, c=None, n=None):
        o = PVL * l + PV[nm]
        if c is None:
            return pvec[:, o:o + (n or 2)]
        return pvec[:, o + c:o + c + 1]
    GO = PVL * depth
    for l in range(depth):
        P.ts("dve", drv[:, DV_L * l:DV_L * l + 8], pvec[:, PVL * l + 16:PVL * l + 24], -1.0, 1.0, ALU.mult, ALU.add)
        so = GO + 8 + 2 * depth + 8 * l
        P.act(esink[:, l, :], pvec[:, so:so + 8], AF.Exp)
    lbx = fa[0]
    for c in range(2):
        mx = fa[1][:, 0, 0:1]
        P.copy("dve", mx, pvec[:, GO + 8 + c:GO + 9 + c])
        for l in range(1, depth):
            P.tt("dve", mx, mx, pvec[:, GO + 8 + 2 * l + c:GO + 9 + 2 * l + c], ALU.max)
        P.ts("dve", fa[1][:, 0, 1:2], mx, -1.0, None, ALU.mult)
        sm = fa[1][:, 0, 2:3]
        for l in range(depth):
            P.act(lbx[:, 0, l:l + 1], pvec[:, GO + 8 + 2 * l + c:GO + 9 + 2 * l + c], AF.Exp, bias=fa[1][:, 0, 1:2])
            if l == 0:
                P.copy("dve", sm, lbx[:, 0, 0:1])
            else:
                P.tt("dve", sm, sm, lbx[:, 0, l:l + 1], ALU.add)
        P.recip(fa[1][:, 0, 3:4], sm)
        for l in range(depth):
            P.ts("dve", lbx[:, 0, l:l + 1], lbx[:, 0, l:l + 1], fa[1][:, 0, 3:4], None, ALU.mult)
        for l in range(depth):
            dst = drv[:, DV_L * l + 8 + c:DV_L * l + 9 + c]
            if l == 0:
                P.memset("dve", dst, 0.0)
            elif l == 1:
                P.copy("dve", dst, lbx[:, 0, 1:2])
            else:
                P.tt("dve", dst, drv[:, DV_L * (l - 1) + 8 + c:DV_L * (l - 1) + 9 + c], lbx[:, 0, l:l + 1], ALU.add)
            P.ts("dve", drv[:, DV_L * l + 10 + c:DV_L * l + 11 + c], dst, -1.0, 1.0, ALU.mult, ALU.add)

    lists = []
    for i in range(nseq if stage >= 1 else 0):
        P.rec = []
        streams[i]["run_seq"](i)
        lists.append(P.rec)
        P.rec = None
    P.replay_zip(lists)
    P.wait_all("sp", list(set(outbufs)))
    P.emit()
    P.stack.close()
    nc._dbg_names = list(dbg_outs.keys())
    return nc, P


_CACHE = {}


def run(inp, nseq_total, seq, depth, ncores, NT=128, dbg=None):
    nseq = nseq_total // ncores
    key = (nseq, seq, depth, NT, dbg is not None)
    if key not in _CACHE:
        _CACHE[key] = build_nc(nseq, seq, depth, NT, dbg is not None)[0]
    nc = _CACHE[key]
    consts = make_consts()
    small = pack_small(inp, depth)
    f32 = lambda a: np.ascontiguousarray(np.asarray(a, np.float32))
    shared = {"meta": f32(inp["meta"]), "w_in": f32(inp["w_in"]), "w_branch": f32(inp["w_branch"]),
              "w_out": f32(inp["w_out"]), "w_ffn_up": f32(inp["w_ffn_up"]), "w_ffn_down": f32(inp["w_ffn_down"])}
    shared.update(small)
    for k in CONST_ORDER:
        shared["c_" + k] = consts[k]
    x = f32(inp["x"])
    in_maps = []
    for c in range(ncores):
        m = dict(shared)
        m["x"] = np.ascontiguousarray(x[c * nseq:(c + 1) * nseq])
        in_maps.append(m)
    res = run_bass_kernel_spmd(nc, in_maps, core_ids=list(range(ncores)))
    if dbg is not None:
        for k in nc._dbg_names:
            dbg[k] = np.asarray(res.results[0]["dbg_" + k])
    return np.concatenate([r["out"] for r in res.results], axis=0).astype(np.float32)


def kernel(**inputs):
    x = inputs["x"]
    depth = inputs["w_in"].shape[0]
    return run(inputs, x.shape[0], x.shape[1], depth, NCORES)
```
